# Optimizing a Trainium2 kernel written in Bass

```python
import math
import jax, jax.numpy as jnp
from jax import lax
import numpy as np

D_MODEL = 1024
BATCH = 8
SEQ = 2048
DEPTH = 2
DEC_BATCH = 128
DEC_SEQ = 1
PAST_LEN = 16384
PAGE_SIZE = 128


N_META = 16
EPS = 1e-6
GLA_HEADS = 4
GLA_DK = 64
GLA_DV = 128
GLA_RANK = 16
GLA_TAU = 16.0
GLA_CHUNK = 64
GLA_K = GLA_HEADS * GLA_DK
GLA_V = GLA_HEADS * GLA_DV
S5_GROUPS = 32
S5_H = 16
S5_P = 64
S5_W = S5_GROUPS * S5_H
IN0 = 2 * GLA_K + 2 * GLA_V + GLA_RANK + S5_W
SPLIT0 = [GLA_K, 2 * GLA_K, 2 * GLA_K + GLA_V, 2 * GLA_K + 2 * GLA_V, 2 * GLA_K + 2 * GLA_V + GLA_RANK]
RNN_W = 1536
RNN_BLOCKS = 16
RNN_BW = RNN_W // RNN_BLOCKS
RNN_C = 8.0
RNN_CONV = 4
D_FF = 2816
FFN_CONV = 3

kernel_name = 'hybrid_gla_s5_rglru_convffn_step'


def rmsnorm(x, g):
    xf = x.astype(jnp.float32)
    y = xf * lax.rsqrt(jnp.mean(xf * xf, axis=-1, keepdims=True) + EPS)
    return (y * g.astype(jnp.float32)).astype(x.dtype)


def causal_dwconv(x, buf, w, b):
    W = w.shape[0]
    T = x.shape[1]
    xx = jnp.concatenate([buf.astype(x.dtype), x], axis=1)
    y = b + xx[:, 0:T] * w[0]
    for j in range(1, W):
        y = y + xx[:, j:j + T] * w[j]
    return y, xx[:, -(W - 1):]


def _lin_combine(e1, e2):
    a1, b1 = e1
    a2, b2 = e2
    return a1 * a2, a2 * b1 + b2


def _cplx_combine(e1, e2):
    a1r, a1i, b1r, b1i = e1
    a2r, a2i, b2r, b2i = e2
    return (a2r * a1r - a2i * a1i, a2r * a1i + a2i * a1r,
            a2r * b1r - a2i * b1i + b2r, a2r * b1i + a2i * b1r + b2i)


def gla_chunked(q, k, v, log_a, s0):
    B_, T = q.shape[:2]
    c = min(GLA_CHUNK, T)
    pad = (-T) % c
    n = (T + pad) // c

    def prep(z):
        z = jnp.pad(z.astype(jnp.float32), ((0, 0), (pad, 0), (0, 0), (0, 0)))
        return z.reshape(B_, n, c, GLA_HEADS, z.shape[-1]).transpose(1, 0, 3, 2, 4)

    qc, kc, vc, ac = prep(q), prep(k), prep(v), prep(log_a)
    mask = jnp.tril(jnp.ones((c, c), bool))[:, :, None]

    def step(S, xs):
        qi, ki, vi, ai = xs
        b = jnp.cumsum(ai, axis=2)
        o = jnp.einsum('bhcd,bhde->bhce', qi * jnp.exp(b), S)
        rel = b[:, :, :, None, :] - b[:, :, None, :, :]
        decay = jnp.exp(jnp.where(mask, rel, -jnp.inf))
        att = jnp.einsum('bhid,bhjd,bhijd->bhij', qi, ki, decay)
        o = o + jnp.einsum('bhij,bhje->bhie', att, vi)
        b_last = b[:, :, -1:, :]
        S = jnp.exp(b_last[:, :, 0, :, None]) * S + jnp.einsum('bhcd,bhce->bhde', ki * jnp.exp(b_last - b), vi)
        return S, o

    S, o = lax.scan(step, s0.astype(jnp.float32), (qc, kc, vc, ac))
    o = o.transpose(1, 0, 3, 2, 4).reshape(B_, T + pad, GLA_HEADS, GLA_DV)[:, pad:]
    return o, S


def s5_mixer(u, x0_re, x0_im, lam_re, lam_im, log_dt, b_re, b_im, c_re, c_im, d, w_glu, b_glu):
    f32 = jnp.float32
    B_, T = u.shape[:2]
    uf = u.astype(f32).reshape(B_, T, S5_GROUPS, S5_H)
    lr, li = lam_re.astype(f32), lam_im.astype(f32)
    dt = jnp.exp(log_dt.astype(f32))[:, None]
    mag = jnp.exp(lr * dt)
    ab_re, ab_im = mag * jnp.cos(li * dt), mag * jnp.sin(li * dt)
    den = lr * lr + li * li
    nr, ni = ab_re - 1.0, ab_im
    f_re = (nr * lr + ni * li) / den
    f_im = (ni * lr - nr * li) / den
    br, bi = b_re.astype(f32), b_im.astype(f32)
    bb_re = f_re[..., None] * br - f_im[..., None] * bi
    bb_im = f_re[..., None] * bi + f_im[..., None] * br
    bu_re = jnp.einsum('gph,btgh->btgp', bb_re, uf)
    bu_im = jnp.einsum('gph,btgh->btgp', bb_im, uf)
    full = (B_, T, S5_GROUPS, S5_P)
    a_re = jnp.broadcast_to(ab_re, full)
    a_im = jnp.broadcast_to(ab_im, full)
    Ar, Ai, xr, xi = lax.associative_scan(_cplx_combine, (a_re, a_im, bu_re, bu_im), axis=1)
    x0r = x0_re.astype(f32)[:, None]
    x0i = x0_im.astype(f32)[:, None]
    xr, xi = xr + Ar * x0r - Ai * x0i, xi + Ar * x0i + Ai * x0r
    y = (jnp.einsum('ghp,btgp->btgh', c_re.astype(f32), xr)
         - jnp.einsum('ghp,btgp->btgh', c_im.astype(f32), xi)
         + d.astype(f32) * uf).reshape(B_, T, S5_W)
    y = jax.nn.gelu(y)
    y = y * jax.nn.sigmoid(y @ w_glu.astype(f32) + b_glu.astype(f32))
    return y, xr[:, -1], xi[:, -1]


def even_mixer(x, s_gla, s_re, s_im, norm_mix, w_in, w_alpha, b_alpha, gla_norm,
               lam_re, lam_im, log_dt, b_re, b_im, c_re, c_im, d, w_glu, b_glu, w_out):
    B_, T, _ = x.shape
    z = rmsnorm(x, norm_mix) @ w_in
    q, k, v, g, lr, u = jnp.split(z, SPLIT0, axis=-1)
    log_a = jax.nn.log_sigmoid((lr @ w_alpha + b_alpha).astype(jnp.float32)) / GLA_TAU
    hd = lambda t, dh: t.reshape(B_, T, GLA_HEADS, dh)
    o, s_gla = gla_chunked(hd(q, GLA_DK) * (GLA_DK ** -0.5), hd(k, GLA_DK), hd(v, GLA_DV),
                           hd(log_a, GLA_DK), s_gla)
    o = rmsnorm(o, gla_norm).reshape(B_, T, GLA_V) * jax.nn.silu(g.astype(jnp.float32))
    y5, s_re, s_im = s5_mixer(u, s_re, s_im, lam_re, lam_im, log_dt, b_re, b_im, c_re, c_im, d, w_glu, b_glu)
    mix = jnp.concatenate([o, y5], axis=-1).astype(x.dtype) @ w_out
    return x + mix, s_gla, s_re, s_im


def odd_mixer(x, h0, conv_buf, norm_mix, w_in, conv_w, conv_b, w_a, b_a, w_x, b_x, lam, w_out):
    f32 = jnp.float32
    B_, T, _ = x.shape
    z = rmsnorm(x, norm_mix) @ w_in
    gate, xr = jnp.split(z, 2, axis=-1)
    xc, new_buf = causal_dwconv(xr, conv_buf, conv_w, conv_b)
    xc = xc.astype(f32)
    xb = xc.reshape(B_, T, RNN_BLOCKS, RNN_BW)
    r = jax.nn.sigmoid(jnp.einsum('btnc,ncd->btnd', xb, w_a.astype(f32)) + b_a.astype(f32)).reshape(B_, T, RNN_W)
    i = jax.nn.sigmoid(jnp.einsum('btnc,ncd->btnd', xb, w_x.astype(f32)) + b_x.astype(f32)).reshape(B_, T, RNN_W)
    log_a = -RNN_C * r * jax.nn.softplus(-lam.astype(f32))
    a = jnp.exp(log_a)
    bx = jnp.sqrt(-jnp.expm1(2.0 * log_a)) * (i * xc)
    a_cum, h = lax.associative_scan(_lin_combine, (a, bx), axis=1)
    h = h + a_cum * h0.astype(f32)[:, None]
    y = (h * jax.nn.gelu(gate.astype(f32))).astype(x.dtype) @ w_out
    return x + y, h[:, -1], new_buf


def conv_ffn(x, buf, norm, w_up, conv_w, conv_b, w_down):
    hup = rmsnorm(x, norm) @ w_up
    gate, val = jnp.split(hup, 2, axis=-1)
    gate, new_buf = causal_dwconv(gate, buf, conv_w, conv_b)
    return x + (jax.nn.gelu(gate) * val) @ w_down, new_buf


def trunk(x, s_gla, s_re, s_im, h_rnn, buf_rnn, buf_ffn, mix0, mix1, ffn, norm_final):
    norm_ffn, w_up, f_conv_w, f_conv_b, w_down = ffn
    new_ffn = []
    for layer in range(DEPTH):
        if layer % 2 == 0:
            x, s_gla, s_re, s_im = even_mixer(x, s_gla, s_re, s_im, *mix0)
        else:
            x, h_rnn, buf_rnn = odd_mixer(x, h_rnn, buf_rnn, *mix1)
        x, nb = conv_ffn(x, buf_ffn[layer], norm_ffn[layer], w_up[layer], f_conv_w[layer],
                         f_conv_b[layer], w_down[layer])
        new_ffn.append(nb)
    return rmsnorm(x, norm_final), s_gla, s_re, s_im, h_rnn, buf_rnn, jnp.stack(new_ffn)


def setup_inputs(seed: int = 0) -> dict:
    key = jax.random.key(seed)
    ks = iter(jax.random.split(key, 64))
    f32 = jnp.float32
    nrm = lambda shape, scale: jax.random.normal(next(ks), shape, f32) * scale
    gain = lambda shape: 1.0 + nrm(shape, 0.02)
    lam_re = -0.5 + nrm((S5_GROUPS, S5_P), 0.01)
    lam_im = math.pi * jnp.arange(S5_P, dtype=f32)[None, :] + nrm((S5_GROUPS, S5_P), 0.01)
    log_dt = jax.random.uniform(next(ks), (S5_GROUPS,), f32, math.log(1e-3), math.log(1e-1))
    d_a = jax.random.uniform(next(ks), (RNN_W,), f32, 0.9, 0.999) ** (1.0 / RNN_C)
    rnn_lam = jnp.log(d_a) - jnp.log1p(-d_a)
    return {
        'x_prompt': nrm((BATCH, SEQ, D_MODEL), 1.0),
        'x_sample': nrm((DEC_BATCH, DEC_SEQ, D_MODEL), 1.0),
        'state_gla': nrm((DEC_BATCH, GLA_HEADS, GLA_DK, GLA_DV), 0.5),
        'state_s5_re': nrm((DEC_BATCH, S5_GROUPS, S5_P), 0.3),
        'state_s5_im': nrm((DEC_BATCH, S5_GROUPS, S5_P), 0.3),
        'state_rglru': nrm((DEC_BATCH, RNN_W), 0.5),
        'cache_rglru_conv': nrm((DEC_BATCH, RNN_CONV - 1, RNN_W), 1.0),
        'cache_ffn_conv': nrm((DEPTH, DEC_BATCH, FFN_CONV - 1, D_FF), 1.0),
        'meta_tokens': nrm((N_META, D_MODEL), 1.0),
        'norm_mix_0': gain((D_MODEL,)),
        'w_in_0': nrm((D_MODEL, IN0), D_MODEL ** -0.5),
        'w_alpha_0': nrm((GLA_RANK, GLA_K), GLA_RANK ** -0.5),
        'b_alpha_0': nrm((GLA_K,), 0.1),
        'gla_norm_0': gain((GLA_HEADS, GLA_DV)),
        's5_lam_re': lam_re,
        's5_lam_im': lam_im,
        's5_log_dt': log_dt,
        's5_b_re': nrm((S5_GROUPS, S5_P, S5_H), (0.5 / S5_H) ** 0.5),
        's5_b_im': nrm((S5_GROUPS, S5_P, S5_H), (0.5 / S5_H) ** 0.5),
        's5_c_re': nrm((S5_GROUPS, S5_H, S5_P), S5_P ** -0.5),
        's5_c_im': nrm((S5_GROUPS, S5_H, S5_P), S5_P ** -0.5),
        's5_d': nrm((S5_GROUPS, S5_H), 1.0),
        's5_w_glu': nrm((S5_W, S5_W), S5_W ** -0.5),
        's5_b_glu': nrm((S5_W,), 0.02),
        'w_out_0': nrm((GLA_V + S5_W, D_MODEL), (GLA_V + S5_W) ** -0.5),
        'norm_mix_1': gain((D_MODEL,)),
        'w_in_1': nrm((D_MODEL, 2 * RNN_W), D_MODEL ** -0.5),
        'rnn_conv_w': nrm((RNN_CONV, RNN_W), RNN_CONV ** -0.5),
        'rnn_conv_b': nrm((RNN_W,), 0.02),
        'rnn_w_a': nrm((RNN_BLOCKS, RNN_BW, RNN_BW), RNN_BW ** -0.5),
        'rnn_b_a': nrm((RNN_BLOCKS, RNN_BW), 0.02),
        'rnn_w_x': nrm((RNN_BLOCKS, RNN_BW, RNN_BW), RNN_BW ** -0.5),
        'rnn_b_x': nrm((RNN_BLOCKS, RNN_BW), 0.02),
        'rnn_lam': rnn_lam,
        'w_out_1': nrm((RNN_W, D_MODEL), RNN_W ** -0.5),
        'norm_ffn': gain((DEPTH, D_MODEL)),
        'ffn_w_up': nrm((DEPTH, D_MODEL, 2 * D_FF), D_MODEL ** -0.5),
        'ffn_conv_w': nrm((DEPTH, FFN_CONV, D_FF), FFN_CONV ** -0.5),
        'ffn_conv_b': nrm((DEPTH, D_FF), 0.02),
        'ffn_w_down': nrm((DEPTH, D_FF, D_MODEL), D_FF ** -0.5),
        'norm_final': gain((D_MODEL,)),
    }


def reference(x_prompt, x_sample, state_gla, state_s5_re, state_s5_im, state_rglru, cache_rglru_conv,
              cache_ffn_conv, meta_tokens, norm_mix_0, w_in_0, w_alpha_0, b_alpha_0, gla_norm_0,
              s5_lam_re, s5_lam_im, s5_log_dt, s5_b_re, s5_b_im, s5_c_re, s5_c_im, s5_d, s5_w_glu,
              s5_b_glu, w_out_0, norm_mix_1, w_in_1, rnn_conv_w, rnn_conv_b, rnn_w_a, rnn_b_a, rnn_w_x,
              rnn_b_x, rnn_lam, w_out_1, norm_ffn, ffn_w_up, ffn_conv_w, ffn_conv_b, ffn_w_down, norm_final):
    f32 = jnp.float32
    mix0 = (norm_mix_0, w_in_0, w_alpha_0, b_alpha_0, gla_norm_0, s5_lam_re, s5_lam_im, s5_log_dt,
            s5_b_re, s5_b_im, s5_c_re, s5_c_im, s5_d, s5_w_glu, s5_b_glu, w_out_0)
    mix1 = (norm_mix_1, w_in_1, rnn_conv_w, rnn_conv_b, rnn_w_a, rnn_b_a, rnn_w_x, rnn_b_x, rnn_lam, w_out_1)
    ffn = (norm_ffn, ffn_w_up, ffn_conv_w, ffn_conv_b, ffn_w_down)

    bp = x_prompt.shape[0]
    meta = jnp.broadcast_to(meta_tokens.astype(x_prompt.dtype)[None], (bp, N_META, D_MODEL))
    xp = jnp.concatenate([meta, x_prompt], axis=1)
    yp, gla_p, re_p, im_p, h_p, rc_p, fc_p = trunk(
        xp,
        jnp.zeros((bp, GLA_HEADS, GLA_DK, GLA_DV), f32),
        jnp.zeros((bp, S5_GROUPS, S5_P), f32),
        jnp.zeros((bp, S5_GROUPS, S5_P), f32),
        jnp.zeros((bp, RNN_W), f32),
        jnp.zeros((bp, RNN_CONV - 1, RNN_W), x_prompt.dtype),
        jnp.zeros((DEPTH, bp, FFN_CONV - 1, D_FF), x_prompt.dtype),
        mix0, mix1, ffn, norm_final)
    yp = yp[:, N_META:]

    ys, gla_s, re_s, im_s, h_s, rc_s, fc_s = trunk(
        x_sample, state_gla, state_s5_re, state_s5_im, state_rglru, cache_rglru_conv, cache_ffn_conv,
        mix0, mix1, ffn, norm_final)

    return (yp, ys, gla_p, gla_s, re_p, re_s, im_p, im_s, h_p, h_s, rc_p, rc_s, fc_p, fc_s)
```

```python
import heapq
import numpy as np
import concourse.bass as bass
import concourse.mybir as mybir
from contextlib import ExitStack

F32 = mybir.dt.float32
BF16 = mybir.dt.bfloat16
I32 = mybir.dt.int32
AF = mybir.ActivationFunctionType
ALU = mybir.AluOpType

N_DMA_SEMS = 8
N_SW_SEMS = 88

_ACT_SETS = {
    "Exp": ("lnexp", "exp"), "Tanh": ("gelu_t", "exp", "sig"), "Ln": ("lnexp",), "Sigmoid": ("sig",),
    "Sqrt": ("sqrt",), "Sin": ("trig", "silu"), "Silu": ("silu",), "Gelu_apprx_tanh": ("gelu_t",),
    "Gelu": ("gelu",),
}


class T:
    __slots__ = ("ap", "writer", "readers", "name")

    ALL = []

    def __init__(self, ap, name=""):
        T.ALL.append(self)
        self.ap = ap
        self.writer = None
        self.readers = []
        self.name = name

    def __getitem__(self, idx):
        return self.ap[idx]


class _Rec:
    def __init__(self):
        self.call = None

    def __getattr__(self, name):
        def f(*a, **kw):
            self.call = (name, a, kw)
            return self
        return f


def _nfree(ap):
    n = 1
    for d in tuple(ap.shape)[1:]:
        n *= int(d)
    return n


class _Op:
    __slots__ = ("i", "eng", "fn", "kind", "cost", "lat", "sync", "order", "succ", "npred", "ready",
                 "start", "fin", "tok", "tbl", "pos")


class Ctx:
    ENGS = ("pe", "act", "dve", "pool", "sp")

    def __init__(self, nc, stack):
        self.nc = nc
        self.stack = stack
        self.sems = {}
        for e in self.ENGS:
            self.sems[e] = stack.enter_context(nc.semaphore("s_" + e))
        for i in range(N_DMA_SEMS):
            self.sems["d%d" % i] = stack.enter_context(nc.semaphore("d%d" % i))
        for i in range(N_SW_SEMS):
            self.sems["w%d" % i] = stack.enter_context(nc.semaphore("w%d" % i))
        self.sw_next = 0
        T.ALL.clear()
        self.cnt = {e: 0 for e in self.ENGS}
        self.dval = [0] * N_DMA_SEMS
        self.drr = 0
        self.seen = {e: {} for e in self.ENGS}
        self.ops = []
        self.uid = 0
        self.reorder = True

    def init(self):
        nc = self.nc
        sems = list(self.sems.values())
        with nc.Block() as block:
            @block.sync
            def _(e):
                for s in sems:
                    e.sem_clear(s)

    def sb(self, shape, dtype, name=None, stack=None):
        self.uid += 1
        name = (name or "t") + "_%d" % self.uid
        t = (stack or self.stack).enter_context(self.nc.sbuf_tensor(name, list(shape), dtype))
        return T(t, name)

    def ps(self, shape, dtype=F32, name=None, stack=None):
        self.uid += 1
        name = (name or "p") + "_%d" % self.uid
        t = (stack or self.stack).enter_context(self.nc.psum_tensor(name, list(shape), dtype))
        return T(t, name)

    def _new(self, eng, fn, kind, cost, lat, reads, writes, accum, tbl=None):
        def flat(lst):
            out = []
            for t in lst:
                if isinstance(t, (list, tuple)):
                    out.extend(flat(t))
                else:
                    t = getattr(t, "t", t)
                    if isinstance(t, (list, tuple)):
                        out.extend(flat(t))
                    elif isinstance(t, T):
                        out.append(t)
            return out
        reads, writes = flat(reads), flat(writes)
        o = _Op()
        o.i = len(self.ops)
        o.eng, o.fn, o.kind, o.cost, o.lat, o.tbl = eng, fn, kind, cost, lat, tbl
        sync, order = set(), set()
        for t in reads:
            if t.writer is not None:
                sync.add(t.writer)
        for t in writes:
            if t.writer is not None:
                if accum and self.ops[t.writer].eng == eng:
                    order.add(t.writer)
                else:
                    sync.add(t.writer)
            sync.update(t.readers)
        sync.discard(o.i)
        o.sync, o.order = sync, order - sync
        o.succ = []
        self.ops.append(o)
        for t in reads:
            t.readers.append(o.i)
        for t in writes:
            if accum and t.writer is not None and self.ops[t.writer].eng == eng:
                t.writer = o.i
            else:
                t.writer = o.i
                t.readers = []
        return o

    def op(self, eng, fn, reads=(), writes=(), accum=False):
        rec = _Rec()
        fn(rec)
        name, a, kw = rec.call

        def fn2(e, name=name, a=a, kw=kw):
            return getattr(e, name)(*a, **kw)
        out = kw.get("out", a[0] if a else None)
        tbl = None
        try:
            n = _nfree(out)
        except Exception:
            n = 128
        if eng == "pe":
            rhs = kw.get("rhs", a[2] if len(a) > 2 else None)
            if name == "matmul" and rhs is not None:
                n = _nfree(rhs)
                if rhs.dtype == F32:
                    n *= 4
            cost = 0.03 + max(n, 64) / 2400.0
        elif eng == "dve":
            mult = 2.0 if name in ("tensor_tensor_scan", "scalar_tensor_tensor") else 1.0
            if name == "reciprocal":
                mult = 8.0
            cost = 0.08 + mult * max(n, 64) / 960.0
        elif eng == "act":
            cost = 0.12 + max(n, 64) / 1200.0
            f = kw.get("func")
            if f is not None:
                tbl = _ACT_SETS.get(str(f).split(".")[-1])
        else:
            cost = 0.25 + max(n, 64) / 600.0
        self._new(eng, fn2, "c", cost, cost, reads, writes, accum, tbl)

    def dma(self, q, out, in_, reads=(), writes=(), **kw):
        def fn(e, out=out, in_=in_, kw=kw):
            return e.dma_start(out=out, in_=in_, **kw)
        try:
            nbytes = _nfree(out) * int(tuple(out.shape)[0]) * 4
        except Exception:
            nbytes = 65536
        lat = 2.0 + nbytes / 150000.0
        self._new(q, fn, "sw" if q == "pool" else "hw", 0.1 if q != "pool" else 1.0, lat, reads, writes, False)

    def _schedule(self, ops):
        for o in ops:
            o.npred = 0
            o.ready = 0.0
        for o in ops:
            for p in (o.sync | o.order):
                self.ops[p].succ.append(o.i)
                o.npred += 1
        bl = [0.0] * len(ops)
        for o in reversed(ops):
            m = 0.0
            for sidx in o.succ:
                if bl[sidx] > m:
                    m = bl[sidx]
            bl[o.i] = m + (o.cost if o.kind == "c" else o.lat)
        self._bl = bl
        future = {e: [] for e in self.ENGS}
        now = {e: [] for e in self.ENGS}
        avail = {e: 0.0 for e in self.ENGS}
        cur_tbl = [None]
        for o in ops:
            if o.npred == 0:
                heapq.heappush(future[o.eng], (0.0, o.i))
        order = {e: [] for e in self.ENGS}
        glob = []
        left = len(ops)
        while left:
            best, be = None, None
            for e in self.ENGS:
                f, nw = future[e], now[e]
                while f and f[0][0] <= avail[e]:
                    ii_ = heapq.heappop(f)[1]
                    heapq.heappush(nw, (-bl[ii_], ii_))
                if nw:
                    st = avail[e]
                elif f:
                    st = f[0][0]
                else:
                    continue
                if best is None or st < best:
                    best, be = st, e
            e = be
            if now[e]:
                i = heapq.heappop(now[e])[1]
                if e == "act":
                    o0 = self.ops[i]
                    if o0.tbl is not None and cur_tbl[0] is not None and cur_tbl[0] not in o0.tbl:
                        held = [i]
                        pick = None
                        for _ in range(12):
                            if not now[e]:
                                break
                            j = heapq.heappop(now[e])[1]
                            oj = self.ops[j]
                            if oj.tbl is None or cur_tbl[0] in oj.tbl:
                                pick = j
                                break
                            held.append(j)
                        for h in held:
                            heapq.heappush(now[e], (-bl[h], h))
                        lazy = None
                        if pick is not None:
                            i = pick
                        else:
                            best_r = None
                            for (r_, j_) in future[e]:
                                oj = self.ops[j_]
                                if r_ <= avail[e] + 3.5 and oj.tbl is not None and cur_tbl[0] in oj.tbl:
                                    if best_r is None or r_ < best_r:
                                        best_r, lazy = r_, j_
                            if lazy is not None:
                                future[e].remove((best_r, lazy))
                                heapq.heapify(future[e])
                                i = lazy
                            else:
                                i = heapq.heappop(now[e])[1]
                        if lazy is not None:
                            start = max(avail[e], best_r)
                        else:
                            start = avail[e]
                    else:
                        start = avail[e]
                else:
                    start = avail[e]
            else:
                start, i = heapq.heappop(future[e])
            o = self.ops[i]
            cost = o.cost
            if e == "act" and o.tbl is not None:
                if cur_tbl[0] is None or cur_tbl[0] not in o.tbl:
                    cost += 1.3
                    cur_tbl[0] = o.tbl[0]
            o.start = start
            avail[e] = start + cost
            o.fin = start + (o.lat if o.kind != "c" else cost)
            order[e].append(o)
            glob.append(o)
            left -= 1
            for s in o.succ:
                so = self.ops[s]
                so.npred -= 1
                if o.fin > so.ready:
                    so.ready = o.fin
                if so.npred == 0:
                    heapq.heappush(future[so.eng], (so.ready, so.i))
        return order, glob

    def _inorder(self, ops):
        order = {e: [] for e in self.ENGS}
        for o in ops:
            order[o.eng].append(o)
        return order, list(ops)

    def flush(self, final=False):
        nc = self.nc
        ops = self.ops
        if self.reorder:
            order, glob = self._schedule(ops)
        else:
            order, glob = self._inorder(ops)
        for e in self.ENGS:
            for o in order[e]:
                if o.kind == "c":
                    self.cnt[e] += 1
                    o.tok = (e, self.cnt[e])
        extra = {}
        for o in glob:
            if o.kind == "hw":
                j = self.drr
                self.drr = (self.drr + 1) % N_DMA_SEMS
                if self.dval[j] > 0:
                    extra[o.i] = ("d%d" % j, self.dval[j])
                self.dval[j] += 16
                o.tok = ("d%d" % j, self.dval[j])
            elif o.kind == "sw":
                assert self.sw_next < N_SW_SEMS, "out of SW DMA semaphores"
                o.tok = ("w%d" % self.sw_next, 16)
                self.sw_next += 1
        sems = self.sems
        prog = {}
        for e in self.ENGS:
            seen = self.seen[e]
            lst = []
            for o in order[e]:
                need = {}
                toks = [self.ops[p].tok for p in o.sync]
                if o.i in extra:
                    toks.append(extra[o.i])
                for (k, v) in toks:
                    if seen.get(k, 0) >= v:
                        continue
                    if need.get(k, 0) < v:
                        need[k] = v
                for k, v in need.items():
                    seen[k] = v
                lst.append((list(need.items()), o.fn, o.tok))
            prog[e] = lst
        fin = {}
        for e in self.ENGS:
            if self.cnt[e] > self.seen["sp"].get(e, 0):
                fin[e] = self.cnt[e]
        for j in range(N_DMA_SEMS):
            k = "d%d" % j
            if self.dval[j] > self.seen["sp"].get(k, 0):
                fin[k] = self.dval[j]
        for o in glob:
            if o.kind == "sw" and self.seen["sp"].get(o.tok[0], 0) < 16:
                fin[o.tok[0]] = 16
        for k, v in fin.items():
            self.seen["sp"][k] = v
        prog["sp"].append((list(fin.items()), None, None))
        for t in T.ALL:
            t.writer = None
            t.readers = []
        self.ops = []

        def run(e, lst):
            for waits, fn, tok in lst:
                for k, v in waits:
                    e.wait_ge(sems[k], v)
                if fn is not None:
                    ins = fn(e)
                    ins.then_inc(sems[tok[0]], 16 if tok[0][1:].isdigit() else 1)

        with nc.Block() as block:
            @block.sync
            def _(e):
                run(e, prog["sp"])
            if prog["pe"]:
                @block.tensor
                def _(e):
                    run(e, prog["pe"])
            if prog["act"]:
                @block.scalar
                def _(e):
                    run(e, prog["act"])
            if prog["dve"]:
                @block.vector
                def _(e):
                    run(e, prog["dve"])
            if prog["pool"]:
                @block.gpsimd
                def _(e):
                    run(e, prog["pool"])
from concourse.bass_utils import run_bass_kernel_spmd
EPS = 1e-6
NT = 2080
NP_TOK = 2064
BLOCKS = [(0, 512), (512, 1024), (1024, 1536), (1536, 2048), (2048, 2080)]
IN0 = 2064
D_FF = 2816


class Pool_:
    def __init__(self, c, shape, dtype, n, name, stack, psum=False):
        self.ts = [(c.ps if psum else c.sb)(shape, dtype, name, stack) for _ in range(n)]
        self.i = 0

    def next(self):
        t = self.ts[self.i]
        self.i = (self.i + 1) % len(self.ts)
        return t


class K:
    pass


XB = [(i * 256, (i + 1) * 256) for i in range(8)] + [(2048, NT)]


class XRow:
    def __init__(self, ap, name):
        self.ap = ap
        self.parts = [T(ap[:, a:b], "%s_%d" % (name, i)) for i, (a, b) in enumerate(XB)]

    def __getitem__(self, idx):
        return self.ap[idx]

    def ts(self, c0, c1):
        return [t for t, (a, b) in zip(self.parts, XB) if a < c1 and c0 < b]


def mk_consts(c, k):
    st = c.stack
    k.ident = c.sb([128, 128], F32, "ident")
    k.triU = c.sb([128, 128], F32, "triU")
    k.triS = c.sb([128, 128], F32, "triS")
    k.mask4 = c.sb([128, 4, 128], F32, "mask4")
    k.ones_bf = c.sb([128, 128], BF16, "ones_bf")
    k.eps_t = c.sb([128, 1], F32, "eps_t")
    k.one_t = c.sb([128, 1], F32, "one_t")
    tmp = c.sb([128, 128], F32, "ctmp")
    k.ones_f = tmp
    c.op("pool", lambda e: e.memset(tmp[:], 1.0), writes=[tmp])
    c.op("pool", lambda e: e.affine_select(out=k.ident[:], in_=tmp[:, :], pattern=[[-1, 128]], compare_op=ALU.is_equal, fill=0.0, base=0, channel_multiplier=1), reads=[tmp], writes=[k.ident])
    c.op("pool", lambda e: e.affine_select(out=k.mask4[:], in_=tmp[:, :].unsqueeze(1).to_broadcast([128, 4, 128]), pattern=[[0, 4], [1, 128]], compare_op=ALU.is_ge, fill=0.0, base=0, channel_multiplier=-1), reads=[tmp], writes=[k.mask4])
    tmp2 = c.sb([128, 128], F32, "ctmp2")
    c.op("pool", lambda e: e.memset(tmp2[:], -1.0 / 16.0), writes=[tmp2])
    c.op("pool", lambda e: e.affine_select(out=k.triU[:], in_=tmp2[:], pattern=[[1, 128]], compare_op=ALU.is_ge, fill=0.0, base=0, channel_multiplier=-1), reads=[tmp2], writes=[k.triU])
    c.op("pool", lambda e: e.affine_select(out=k.triS[:], in_=tmp2[:], pattern=[[-1, 128]], compare_op=ALU.is_gt, fill=0.0, base=0, channel_multiplier=1), reads=[tmp2], writes=[k.triS])
    c.op("dve", lambda e: e.memset(k.ones_bf[:], 1.0), writes=[k.ones_bf])
    c.op("dve", lambda e: e.memset(k.eps_t[:], EPS), writes=[k.eps_t])
    c.op("dve", lambda e: e.memset(k.one_t[:], 1.0), writes=[k.one_t])


def evac(c, k, out_t, out_ap, in_t, in_ap):
    k.ev = getattr(k, "ev", 0) + 1
    if k.ev % 2:
        c.op("act", lambda e: e.copy(out=out_ap, in_=in_ap), reads=[in_t], writes=[out_t])
    else:
        c.op("dve", lambda e: e.tensor_copy(out=out_ap, in_=in_ap), reads=[in_t], writes=[out_t])


def load_vec_table(c, k, specs, stack):
    out = {}
    rows = 0
    groups = [[]]
    for name, ap, nrows in specs:
        if rows + nrows > 128:
            groups.append([])
            rows = 0
        groups[-1].append((name, ap, nrows, rows))
        rows += nrows
    vts = [c.sb([128, 128], F32, "vt") for _ in groups]
    for gi, grp in enumerate(groups):
        stg = c.sb([128, 128], F32, "vstg", stack)
        c.op("dve", lambda e, stg=stg: e.memset(stg[:], 0.0), writes=[stg])
        for name, ap, nrows, r0 in grp:
            c.dma("sp", stg[r0:r0 + nrows, :], ap, writes=[stg])
        ps = k.psum.next()
        c.op("pe", lambda e, ps=ps, stg=stg: e.transpose(out=ps[:, 0:128], in_=stg[:, :], identity=k.ident[:, :]), reads=[stg, k.ident], writes=[ps])
        vt = vts[gi]
        evac(c, k, vt, vt[:, :], ps, ps[:, 0:128])
        for name, ap, nrows, r0 in grp:
            out[name] = (vt, r0, nrows)
    return out


def rmsnorm_block(c, k, c0, w, gname, xn, sq_eng="act"):
    gt, gc, _ = k.vt[gname]
    sq = xn
    for kt in range(8):
        if sq_eng == "act":
            c.op("act", lambda e, kt=kt, sq=sq: e.activation(out=sq[:, kt, 0:w], in_=k.X[kt][:, c0:c0 + w], func=AF.Square), reads=[k.X[kt].ts(c0, c0 + w)], writes=[sq])
        else:
            c.op(sq_eng, lambda e, kt=kt, sq=sq: e.tensor_tensor(out=sq[:, kt, 0:w], in0=k.X[kt][:, c0:c0 + w], in1=k.X[kt][:, c0:c0 + w], op=ALU.mult), reads=[k.X[kt].ts(c0, c0 + w)], writes=[sq])
    ps = k.psum.next()
    for kt in range(8):
        c.op("pe", lambda e, kt=kt, sq=sq, ps=ps: e.matmul(ps[:, 0:w], k.ones_bf[:, :], sq[:, kt, 0:w], start=(kt == 0), stop=(kt == 7)), reads=[sq, k.ones_bf], writes=[ps], accum=(kt > 0))
    rs = k.rspool.next()
    c.op("act", lambda e: e.activation(out=rs[:, 0:w], in_=ps[:, 0:w], func=AF.Ln, bias=k.eps_t[:, 0:1], scale=1.0 / 1024.0), reads=[ps, k.eps_t], writes=[rs])
    c.op("act", lambda e: e.activation(out=rs[:, 0:w], in_=rs[:, 0:w], func=AF.Exp, scale=-0.5), reads=[rs], writes=[rs])
    for kt in range(8):
        c.op("dve", lambda e, kt=kt: e.scalar_tensor_tensor(out=xn[:, kt, 0:w], in0=k.X[kt][:, c0:c0 + w], scalar=gt[:, gc + kt:gc + kt + 1], in1=rs[:, 0:w], op0=ALU.mult, op1=ALU.mult), reads=[k.X[kt].ts(c0, c0 + w), gt, rs], writes=[xn])


def proj(c, k, ps_ap, ps_t, w_t, w_ap_fn, xn, w, nk=8, xcols=None):
    lo, hi = xcols if xcols else (0, w)
    for kt in range(nk):
        c.op("pe", lambda e, kt=kt: e.matmul(ps_ap, w_ap_fn(kt), xn[:, kt, lo:hi], start=(kt == 0), stop=(kt == nk - 1)), reads=[w_t, xn], writes=[ps_t], accum=(kt > 0))


def proj_tok(c, k, ps_ap, ps_t, w_t, w_ap_fn, xn, lo, hi, nk=8):
    for kt in range(nk):
        c.op("pe", lambda e, kt=kt: e.matmul(ps_ap, xn[:, kt, lo:hi], w_ap_fn(kt), start=(kt == 0), stop=(kt == nk - 1)), reads=[w_t, xn], writes=[ps_t], accum=(kt > 0))


def phase0(c, k, D):
    nc = c.nc
    mk_consts(c, k)
    k.psum = Pool_(c, [128, 512], F32, 8, "psb", c.stack, psum=True)
    xs = c.stack.enter_context(nc.sbuf_tensor("x_res", [128, 8, NT], F32))
    k.X = [XRow(xs[:, kt, :], "x%d" % kt) for kt in range(8)]
    r128 = lambda ap: ap.rearrange("(k p) -> k p", p=128)
    specs = [("norm_mix_0", r128(D["norm_mix_0"]), 8), ("norm_mix_1", r128(D["norm_mix_1"]), 8),
             ("norm_ffn0", r128(D["norm_ffn"][0]), 8), ("norm_ffn1", r128(D["norm_ffn"][1]), 8),
             ("norm_final", r128(D["norm_final"]), 8),
             ("gla_norm", D["gla_norm_0"], 4), ("s5_d", r128(D["s5_d"].rearrange("g h -> (g h)")), 4),
             ("b_glu", r128(D["s5_b_glu"]), 4),
             ("lam_re", r128(D["s5_lam_re"].rearrange("g p -> (g p)")), 16),
             ("lam_im", r128(D["s5_lam_im"].rearrange("g p -> (g p)")), 16)]
    for l in range(2):
        for j in range(3):
            specs.append(("fcw%d_%d" % (l, j), r128(D["ffn_conv_w"][l, j]), 22))
        specs.append(("fcb%d" % l, r128(D["ffn_conv_b"][l]), 22))
    import os
    BIS = int(os.environ.get("BIS", "9"))
    with ExitStack() as st:
        if BIS >= 2:
            k.vt = load_vec_table(c, k, specs, st)
        stg = Pool_(c, [128, 4, 1024], F32, 2, "xstg", st)
        xp = D["x_prompt"]
        for g in range(min(4, BIS - 2) if BIS >= 3 else 0):
            s = stg.next()
            c.dma("sp", s[:], xp[512 * g:512 * (g + 1), :].rearrange("(j p) f -> p j f", p=128), writes=[s])
            for kt in range(8):
                ps = k.psum.next()
                for j in range(4):
                    c.op("pe", lambda e, ps=ps, s=s, j=j, kt=kt: e.transpose(out=ps[:, j * 128:(j + 1) * 128], in_=s[:, j, kt * 128:(kt + 1) * 128], identity=k.ident[:, :]), reads=[s, k.ident], writes=[ps], accum=(j > 0))
                evac(c, k, k.X[kt].ts(16 + 512 * g, 16 + 512 * (g + 1)), k.X[kt][:, 16 + 512 * g:16 + 512 * (g + 1)], ps, ps[:, :])
        s = stg.next()
        c.dma("sp", s[0:16, 0, :], D["meta_tokens"], writes=[s])
        c.dma("sp", s[0:16, 1, :], D["x_sample"], writes=[s])
        for kt in range(8):
            ps = k.psum.next()
            for j in range(2):
                c.op("pe", lambda e, ps=ps, s=s, j=j, kt=kt: e.transpose(out=ps[:, j * 16:(j + 1) * 16], in_=s[0:16, j, kt * 128:(kt + 1) * 128], identity=k.ident[0:16, 0:16]), reads=[s, k.ident], writes=[ps], accum=(j > 0))
            evac(c, k, k.X[kt].ts(0, 16), k.X[kt][:, 0:16], ps, ps[:, 0:16])
            evac(c, k, k.X[kt].ts(NP_TOK, NT), k.X[kt][:, NP_TOK:NT], ps, ps[:, 16:32])
        c.flush()
def load_w_bf16(c, k, dst_t, dst_ap, src_ap):
    c.dma("pool", dst_ap, src_ap, writes=[dst_t])


def gla_out_stage(c, k, o_ps, cw, gs, gs_lo, mixo, g_lo):
    gnt, gnc, _ = k.vt["gla_norm"]
    sq = k.g_sq.next()
    c.op("act", lambda e: e.activation(out=sq[:, :, 0:cw], in_=o_ps[:, 0:4 * 128].rearrange("p (h i) -> p h i", h=4)[:, :, 0:cw], func=AF.Square), reads=[o_ps], writes=[sq])
    ms = k.psum.next()
    for h in range(4):
        c.op("pe", lambda e, h=h: e.matmul(ms[:, h * 128:h * 128 + cw], k.ones_bf[:, :], sq[:, h, 0:cw], start=True, stop=True), reads=[sq, k.ones_bf], writes=[ms], accum=(h > 0))
    rs = k.g_rs.next()
    msv = ms[:, 0:512].rearrange("p (h i) -> p h i", h=4)[:, :, 0:cw]
    c.op("act", lambda e: e.activation(out=rs[:, :, 0:cw], in_=msv, func=AF.Ln, bias=k.eps_t[:, 0:1], scale=1.0 / 128.0), reads=[ms, k.eps_t], writes=[rs])
    c.op("act", lambda e: e.activation(out=rs[:, :, 0:cw], in_=rs[:, :, 0:cw], func=AF.Exp, scale=-0.5), reads=[rs], writes=[rs])
    on = k.g_on.next()
    c.op("dve", lambda e: e.tensor_tensor(out=on[:, :, 0:cw], in0=o_ps[:, 0:512].rearrange("p (h i) -> p h i", h=4)[:, :, 0:cw], in1=rs[:, :, 0:cw], op=ALU.mult), reads=[o_ps, rs], writes=[on])
    for h in range(4):
        c.op("dve", lambda e, h=h: e.scalar_tensor_tensor(out=mixo[:, h, g_lo:g_lo + cw], in0=on[:, h, 0:cw], scalar=gnt[:, gnc + h:gnc + h + 1], in1=gs[:, h, gs_lo:gs_lo + cw], op0=ALU.mult, op1=ALU.mult), reads=[on, gnt, gs], writes=[mixo])


def phase1(c, k, D, O):
    nc = c.nc
    with ExitStack() as st:
        wqk = c.sb([128, 8, 512], BF16, "wqk", st)
        wv = c.sb([128, 8, 512], BF16, "wv", st)
        wg = c.sb([128, 8, 512], BF16, "wg", st)
        wl = c.sb([128, 8, 16], BF16, "wl", st)
        W = D["w_in_0"].rearrange("(k p) n -> p k n", p=128)
        load_w_bf16(c, k, wqk, wqk[:], W[:, :, 0:512])
        load_w_bf16(c, k, wl, wl[:], W[:, :, 1536:1552])
        load_w_bf16(c, k, wv, wv[:], W[:, :, 512:1024])
        load_w_bf16(c, k, wg, wg[:], W[:, :, 1024:1536])
        wal = c.sb([17, 256], F32, "wal", st)
        c.dma("sp", wal[0:16, :], D["w_alpha_0"], writes=[wal])
        c.dma("sp", wal[16:17, :], D["b_alpha_0"].rearrange("(o n) -> o n", o=1), writes=[wal])
        S0 = c.sb([128, 16, 2, 128], F32, "S0", st)
        for hp in range(2):
            c.dma("sp", S0[:, :, hp, :], D["state_gla"][:, 2 * hp:2 * hp + 2].rearrange("s h d e -> (h d) s e"), writes=[S0])
        k.rspool = Pool_(c, [128, 512], F32, 1, "rs", st)
        xnp = Pool_(c, [128, 8, 512], BF16, 2, "xn", st)
        qT = Pool_(c, [128, 2, 512], F32, 2, "qT", st)
        kT = Pool_(c, [128, 2, 512], F32, 2, "kT", st)
        gsp = Pool_(c, [128, 4, 512], BF16, 2, "gs", st)
        lrT = c.sb([32, 512], F32, "lrT", st)
        c.op("dve", lambda e: e.memset(lrT[:], 1.0), writes=[lrT])
        sp_p = Pool_(c, [128, 256], F32, 2, "sp", st)
        Eq_p = Pool_(c, [128, 2, 128], F32, 2, "Eq", st)
        Ek_p = Pool_(c, [128, 2, 128], F32, 2, "Ek", st)
        Er_p = Pool_(c, [128, 256], F32, 2, "Er", st)
        qm_p = [[Pool_(c, [128, 128], BF16, 2, "qm", st) for h2 in range(2)] for hp in range(2)]
        for hp in range(2):
            for h2 in range(2):
                for t in qm_p[hp][h2].ts:
                    c.op("pool", lambda e, t=t: e.memset(t[:], 0.0), writes=[t])
        kt_p = [Pool_(c, [128, 128], BF16, 2, "ktl", st) for hp in range(2)]
        kh_p = Pool_(c, [128, 256], BF16, 2, "kh", st)
        vb_p = Pool_(c, [128, 512], BF16, 2, "vb", st)
        at_p = Pool_(c, [128, 4, 128], BF16, 2, "at", st)
        k.g_sq = Pool_(c, [128, 4, 128], BF16, 2, "gsq", st)
        k.g_rs = Pool_(c, [128, 4, 128], F32, 2, "grs", st)
        k.g_on = Pool_(c, [128, 4, 128], F32, 2, "gon", st)
        S = [c.sb([128, 128], F32, "S", st) for hp in range(2)]
        Sb = [c.sb([128, 128], BF16, "Sb", st) for hp in range(2)]
        for hp in range(2):
            c.op("dve", lambda e, hp=hp: e.memset(S[hp][:], 0.0), writes=[S[hp]])
            c.op("dve", lambda e, hp=hp: e.memset(Sb[hp][:], 0.0), writes=[Sb[hp]])
        mixo = k.mixo

        for (c0, w) in [(b0, b1 - b0) for b0, b1 in BLOCKS]:
            xn = xnp.next()
            rmsnorm_block(c, k, c0, w, "norm_mix_0", xn)
            q_sb, k_sb, gs = qT.next(), kT.next(), gsp.next()
            for hp in range(2):
                ps = k.psum.next()
                proj(c, k, ps[:, 0:w], ps, wqk, lambda kt, hp=hp: wqk[:, kt, hp * 128:(hp + 1) * 128], xn, w)
                evac(c, k, q_sb, q_sb[:, hp, 0:w], ps, ps[:, 0:w])
                ps = k.psum.next()
                proj(c, k, ps[:, 0:w], ps, wqk, lambda kt, hp=hp: wqk[:, kt, 256 + hp * 128:256 + (hp + 1) * 128], xn, w)
                evac(c, k, k_sb, k_sb[:, hp, 0:w], ps, ps[:, 0:w])
            for m in range(4):
                ps = k.psum.next()
                proj(c, k, ps[:, 0:w], ps, wg, lambda kt, m=m: wg[:, kt, m * 128:(m + 1) * 128], xn, w)
                c.op("act", lambda e, ps=ps, m=m: e.activation(out=gs[:, m, 0:w], in_=ps[:, 0:w], func=AF.Silu), reads=[ps], writes=[gs])
            ps = k.psum.next()
            proj(c, k, ps[0:16, 0:w], ps, wl, lambda kt: wl[:, kt, 0:16], xn, w)
            evac(c, k, lrT, lrT[0:16, 0:w], ps, ps[0:16, 0:w])

            chunks = [(i * 128, 128) for i in range(w // 128)] if w == 512 else [(0, 16)]
            for (cc, cw) in chunks:
                ktok = k.psum.next()
                proj_tok(c, k, ktok[0:cw, 0:256], ktok, wqk, lambda kt: wqk[:, kt, 256:512], xn, cc, cc + cw)
                vtok = k.psum.next()
                proj_tok(c, k, vtok[0:cw, 0:512], vtok, wv, lambda kt: wv[:, kt, :], xn, cc, cc + cw)
                zps = k.psum.next()
                c.op("pe", lambda e: e.matmul(zps[0:cw, 0:256], lrT[0:17, cc:cc + cw], wal[0:17, :], start=True, stop=True), reads=[lrT, wal], writes=[zps])
                sp = sp_p.next()
                c.op("act", lambda e: e.activation(out=sp[0:cw, :], in_=zps[0:cw, 0:256], func=AF.Exp, scale=-1.0), reads=[zps], writes=[sp])
                c.op("act", lambda e: e.activation(out=sp[0:cw, :], in_=sp[0:cw, :], func=AF.Ln, bias=k.one_t[0:cw, 0:1], scale=1.0), reads=[sp, k.one_t], writes=[sp])
                bf = k.psum.next()
                for hp in range(2):
                    c.op("pe", lambda e, hp=hp: e.matmul(bf[:, hp * 128:hp * 128 + cw], sp[0:cw, hp * 128:(hp + 1) * 128], k.triU[0:cw, 0:cw], start=True, stop=True), reads=[sp, k.triU], writes=[bf], accum=(hp > 0))
                brev = k.psum.next()
                c.op("pe", lambda e: e.matmul(brev[0:cw, 0:256], k.triS[0:cw, 0:cw], sp[0:cw, :], start=True, stop=True), reads=[sp, k.triS], writes=[brev])
                Eq, Ek, Er = Eq_p.next(), Ek_p.next(), Er_p.next()
                bfv = bf[:, 0:256].rearrange("p (h i) -> p h i", h=2)[:, :, 0:cw]
                c.op("act", lambda e: e.activation(out=Eq[:, :, 0:cw], in_=bfv, func=AF.Exp, scale=1.0), reads=[bf], writes=[Eq])
                c.op("act", lambda e: e.activation(out=Ek[:, :, 0:cw], in_=bfv, func=AF.Exp, scale=-1.0), reads=[bf], writes=[Ek])
                c.op("act", lambda e: e.activation(out=Er[0:cw, :], in_=brev[0:cw, 0:256], func=AF.Exp, scale=1.0), reads=[brev], writes=[Er])
                qm = [[qm_p[hp][h2].next() for h2 in range(2)] for hp in range(2)]
                ktl = [kt_p[hp].next() for hp in range(2)]
                for hp in range(2):
                    for h2 in range(2):
                        lo = 64 * h2
                        c.op("dve", lambda e, hp=hp, h2=h2, lo=lo: e.scalar_tensor_tensor(out=qm[hp][h2][lo:lo + 64, 0:cw], in0=q_sb[lo:lo + 64, hp, cc:cc + cw], scalar=0.125, in1=Eq[lo:lo + 64, hp, 0:cw], op0=ALU.mult, op1=ALU.mult), reads=[q_sb, Eq], writes=[qm[hp][h2]])
                    c.op("pool", lambda e, hp=hp: e.tensor_tensor(out=ktl[hp][:, 0:cw], in0=k_sb[:, hp, cc:cc + cw], in1=Ek[:, hp, 0:cw], op=ALU.mult), reads=[k_sb, Ek], writes=[ktl[hp]])
                kh = kh_p.next()
                c.op("dve", lambda e: e.tensor_tensor(out=kh[0:cw, :], in0=ktok[0:cw, 0:256], in1=Er[0:cw, :], op=ALU.mult), reads=[ktok, Er], writes=[kh])
                vb = vb_p.next()
                c.op("act", lambda e: e.copy(out=vb[0:cw, :], in_=vtok[0:cw, 0:512]), reads=[vtok], writes=[vb])
                atp = k.psum.next()
                for h in range(4):
                    hp, h2 = h // 2, h % 2
                    c.op("pe", lambda e, h=h, hp=hp, h2=h2: e.matmul(atp[0:cw, h * 128:h * 128 + cw], ktl[hp][:, 0:cw], qm[hp][h2][:, 0:cw], start=True, stop=True), reads=[ktl[hp], qm[hp][h2]], writes=[atp], accum=(h > 0))
                at = at_p.next()
                c.op("dve", lambda e: e.tensor_tensor(out=at[0:cw, :, 0:cw], in0=atp[0:cw, 0:512].rearrange("p (h i) -> p h i", h=4)[:, :, 0:cw], in1=k.mask4[0:cw, :, 0:cw], op=ALU.mult), reads=[atp, k.mask4], writes=[at])
                ops_ = k.psum.next()
                for h in range(4):
                    hp, h2 = h // 2, h % 2
                    c.op("pe", lambda e, h=h: e.matmul(ops_[:, h * 128:h * 128 + cw], vb[0:cw, h * 128:(h + 1) * 128], at[0:cw, h, 0:cw], start=True, stop=False), reads=[vb, at], writes=[ops_], accum=(h > 0))
                    c.op("pe", lambda e, h=h, hp=hp, h2=h2: e.matmul(ops_[:, h * 128:h * 128 + cw], Sb[hp][:, :], qm[hp][h2][:, 0:cw], start=False, stop=True), reads=[Sb[hp], qm[hp][h2]], writes=[ops_], accum=True)
                dS = k.psum.next()
                for hp in range(2):
                    c.op("pe", lambda e, hp=hp: e.matmul(dS[:, hp * 256:(hp + 1) * 256], kh[0:cw, hp * 128:(hp + 1) * 128], vb[0:cw, hp * 256:(hp + 1) * 256], start=True, stop=True), reads=[kh, vb], writes=[dS], accum=(hp > 0))
                for hp in range(2):
                    for h2 in range(2):
                        lo = 64 * h2
                        c.op("dve", lambda e, hp=hp, h2=h2, lo=lo: e.scalar_tensor_tensor(out=S[hp][lo:lo + 64, :], in0=S[hp][lo:lo + 64, :], scalar=Eq[lo:lo + 64, hp, cw - 1:cw], in1=dS[lo:lo + 64, hp * 256 + h2 * 128:hp * 256 + (h2 + 1) * 128], op0=ALU.mult, op1=ALU.add), reads=[S[hp], Eq, dS], writes=[S[hp]])
                    c.op("pool", lambda e, hp=hp: e.tensor_copy(out=Sb[hp][:], in_=S[hp][:]), reads=[S[hp]], writes=[Sb[hp]])
                gla_out_stage(c, k, ops_, cw, gs, cc, mixo, c0 + cc)

            if w == 32:
                for hp in range(2):
                    c.dma("sp", O["gla_prompt"][2 * hp:2 * hp + 2].rearrange("h d e -> (h d) e"), S[hp][:], reads=[S[hp]])
                selp = Pool_(c, [16, 128], F32, 2, "sel", st)
                vtok = k.psum.next()
                proj_tok(c, k, vtok[0:16, 0:512], vtok, wv, lambda kt: wv[:, kt, :], xn, 16, 32)
                vs = c.sb([16, 512], F32, "vs", st)
                evac(c, k, vs, vs[:, :], vtok, vtok[0:16, 0:512])
                zf = k.psum.next()
                for hp in range(2):
                    c.op("pe", lambda e, hp=hp: e.matmul(zf[:, hp * 16:(hp + 1) * 16], wal[0:17, hp * 128:(hp + 1) * 128], lrT[0:17, 16:32], start=True, stop=True), reads=[wal, lrT], writes=[zf], accum=(hp > 0))
                ea = c.sb([128, 32], F32, "ea", st)
                c.op("act", lambda e: e.activation(out=ea[:, :], in_=zf[:, 0:32], func=AF.Exp, scale=-1.0), reads=[zf], writes=[ea])
                c.op("act", lambda e: e.activation(out=ea[:, :], in_=ea[:, :], func=AF.Ln, bias=k.one_t[:, 0:1], scale=1.0), reads=[ea, k.one_t], writes=[ea])
                c.op("act", lambda e: e.activation(out=ea[:, :], in_=ea[:, :], func=AF.Exp, scale=-1.0 / 16.0), reads=[ea], writes=[ea])
                qs = [[c.sb([128, 16], F32, "qs", st) for h2 in range(2)] for hp in range(2)]
                for hp in range(2):
                    for h2 in range(2):
                        lo = 64 * h2
                        c.op("pool", lambda e, hp=hp, h2=h2: e.memset(qs[hp][h2][:], 0.0), writes=[qs[hp][h2]])
                        c.op("act", lambda e, hp=hp, h2=h2, lo=lo: e.mul(out=qs[hp][h2][lo:lo + 64, :], in_=q_sb[lo:lo + 64, hp, 16:32], mul=0.125), reads=[q_sb], writes=[qs[hp][h2]])
                Sn = S0
                os_ = k.psum.next()
                for s in range(16):
                    vbp = k.psum.next()
                    sel = selp.next()
                    c.op("pool", lambda e, s=s, sel=sel: e.affine_select(out=sel[:, :], in_=k.ones_f[0:16, :], pattern=[[0, 128]], compare_op=ALU.is_equal, fill=0.0, base=-s, channel_multiplier=1), reads=[k.ones_f], writes=[sel])
                    c.op("pe", lambda e, s=s, vbp=vbp, sel=sel: e.matmul(vbp[:, 0:512], sel[0:16, :], vs[0:16, :], start=True, stop=True), reads=[sel, vs], writes=[vbp])
                    for hp in range(2):
                        c.op("act", lambda e, s=s, hp=hp: e.activation(out=Sn[:, s, hp, :], in_=S0[:, s, hp, :], func=AF.Copy, scale=ea[:, hp * 16 + s:hp * 16 + s + 1]), reads=[S0, ea], writes=[Sn])
                        for h2 in range(2):
                            lo = 64 * h2
                            h = 2 * hp + h2
                            c.op("dve", lambda e, s=s, hp=hp, lo=lo, h=h, vbp=vbp: e.scalar_tensor_tensor(out=Sn[lo:lo + 64, s, hp, :], in0=vbp[lo:lo + 64, h * 128:(h + 1) * 128], scalar=k_sb[lo:lo + 64, hp, 16 + s:17 + s], in1=Sn[lo:lo + 64, s, hp, :], op0=ALU.mult, op1=ALU.add), reads=[vbp, k_sb, Sn], writes=[Sn])
                for s in range(16):
                    for h in range(4):
                        hp, h2 = h // 2, h % 2
                        c.op("pe", lambda e, s=s, h=h, hp=hp, h2=h2: e.matmul(os_[:, h * 128 + s:h * 128 + s + 1], Sn[:, s, hp, :], qs[hp][h2][:, s:s + 1], start=True, stop=True), reads=[Sn, qs[hp][h2]], writes=[os_], accum=(s + h > 0))
                for hp in range(2):
                    c.dma("sp", O["gla_sample"][:, 2 * hp:2 * hp + 2].rearrange("s h d e -> (h d) s e"), Sn[:, :, hp, :], reads=[Sn])
                gla_out_stage(c, k, os_, 16, gs, 16, mixo, c0 + 16)
        c.flush()
import math
TWO_PI = 2.0 * math.pi


def sin_of(c, k, out_t, out_ap, arg_ap, arg_t, shape, st, shift=0.0):
    key = tuple(shape)
    if key not in k.sr_cache:
        k.sr_cache[key] = (c.sb(shape, F32, "sr_t", st), c.sb(shape, I32, "sr_i", st), c.sb(shape, F32, "sr_r", st))
    t, ki, r = k.sr_cache[key]
    full = tuple(slice(None) for _ in shape)
    c.op("dve", lambda e: e.tensor_scalar(out=t[full], in0=arg_ap, scalar1=shift, scalar2=1.0 / TWO_PI, op0=ALU.add, op1=ALU.mult), reads=[arg_t], writes=[t])
    c.op("dve", lambda e: e.tensor_copy(out=ki[full], in_=t[full]), reads=[t], writes=[ki])
    c.op("dve", lambda e: e.tensor_copy(out=t[full], in_=ki[full]), reads=[ki], writes=[t])
    c.op("dve", lambda e: e.tensor_scalar(out=r[full], in0=arg_ap, scalar1=shift, scalar2=None, op0=ALU.add), reads=[arg_t], writes=[r])
    c.op("dve", lambda e: e.scalar_tensor_tensor(out=r[full], in0=t[full], scalar=-TWO_PI, in1=r[full], op0=ALU.mult, op1=ALU.add), reads=[t, r], writes=[r])
    c.op("dve", lambda e: e.tensor_scalar(out=t[full], in0=r[full], scalar1=math.pi, scalar2=None, op0=ALU.is_gt), reads=[r], writes=[t])
    c.op("dve", lambda e: e.scalar_tensor_tensor(out=r[full], in0=t[full], scalar=-TWO_PI, in1=r[full], op0=ALU.mult, op1=ALU.add), reads=[t, r], writes=[r])
    c.op("dve", lambda e: e.tensor_scalar(out=t[full], in0=r[full], scalar1=-math.pi, scalar2=None, op0=ALU.is_lt), reads=[r], writes=[t])
    c.op("dve", lambda e: e.scalar_tensor_tensor(out=r[full], in0=t[full], scalar=TWO_PI, in1=r[full], op0=ALU.mult, op1=ALU.add), reads=[t, r], writes=[r])
    c.op("dve", lambda e: e.tensor_scalar(out=r[full], in0=r[full], scalar1=math.pi, scalar2=-math.pi, op0=ALU.min, op1=ALU.max), reads=[r], writes=[r])
    c.op("act", lambda e: e.activation(out=out_ap, in_=r[full], func=AF.Sin), reads=[r], writes=[out_t])


class _StopPhase(Exception):
    pass


def phase2(c, k, D, O):
    try:
        _phase2(c, k, D, O)
    except _StopPhase:
        pass


def _phase2(c, k, D, O):
    import os
    BIS = int(os.environ.get("S5BIS", "0"))

    def chk(n):
        if BIS == n:
            c.flush()
            return True
        return False
    SEG = 128
    with ExitStack() as st:
        CT = c.sb([128, 16, SEG], F32, "CT", st)
        ST = c.sb([128, 16, SEG], F32, "ST", st)
        WB = c.sb([128, 4, 2, 128], F32, "WB", st)
        WB3 = c.sb([128, 4, 2, 128], F32, "WB3", st)
        WC16 = c.sb([128, 16, 2, 128], BF16, "WC16", st)
        WBa = c.sb([128, 4, 2, 128], F32, "WBa", st)
        WBa3 = c.sb([128, 4, 2, 128], F32, "WBa3", st)
        WCa = c.sb([128, 16, 2, 128], BF16, "WCa", st)
        G0t = c.sb([128, 4, 128], F32, "G0t", st)
        mag2 = c.sb([128, 16], F32, "mag2", st)
        c2 = c.sb([128, 16], F32, "c2", st)
        s2t = c.sb([128, 16], F32, "s2t", st)
        wu = c.sb([128, 8, 512], BF16, "wu", st)
        load_w_bf16(c, k, wu, wu[:], D["w_in_0"].rearrange("(k p) n -> p k n", p=128)[:, :, 1552:2064])
        k.rspool = Pool_(c, [128, 512], F32, 1, "rs", st)
        xn = c.sb([128, 8, 2 * SEG], BF16, "xn2", st)
        useg = c.sb([128, 4, 2 * SEG], F32, "useg", st)
        wglu = c.sb([128, 4, 512], BF16, "wglu", st)
        load_w_bf16(c, k, wglu, wglu[:], D["s5_w_glu"].rearrange("(k p) n -> p k n", p=128))
        mag = c.sb([128, 16], F32, "mag", st)
        cth = c.sb([128, 16], F32, "cth", st)
        sth = c.sb([128, 16], F32, "sth", st)
        abr = c.sb([128, 16], F32, "abr", st)
        abi = c.sb([128, 16], F32, "abi", st)
        xpr = c.sb([128, 16], F32, "xpr", st)
        xpi = c.sb([128, 16], F32, "xpi", st)
        xprT = [T(xpr.ap[:, a:a + 1], "xpr%d" % a) for a in range(16)]
        xpiT = [T(xpi.ap[:, a:a + 1], "xpi%d" % a) for a in range(16)]
        lrt, lrc, _ = k.vt["lam_re"]
        lit, lic, _ = k.vt["lam_im"]
        lr = lrt[:, lrc:lrc + 16]
        li = lit[:, lic:lic + 16]
        A2 = (slice(None), slice(None))
        with ExitStack() as s2:
            k.sr_cache = {}
            ldt = c.sb([128, 16], F32, "ldt", s2)
            for g2 in range(2):
                c.dma("sp", ldt[64 * g2:64 * g2 + 64, :], D["s5_log_dt"].rearrange("(a g) -> g a", g=2)[g2:g2 + 1, :].to_broadcast([64, 16]), writes=[ldt], allow_slow_non_contiguous=True)
            dt = c.sb([128, 16], F32, "dt", s2)
            th = c.sb([128, 16], F32, "th", s2)
            t1 = c.sb([128, 16], F32, "t1", s2)
            t2 = c.sb([128, 16], F32, "t2", s2)
            fre = c.sb([128, 16], F32, "fre", s2)
            fim = c.sb([128, 16], F32, "fim", s2)
            c.op("act", lambda e: e.activation(out=dt[A2], in_=ldt[A2], func=AF.Exp), reads=[ldt], writes=[dt])
            c.op("dve", lambda e: e.tensor_tensor(out=t1[A2], in0=lr, in1=dt[A2], op=ALU.mult), reads=[lrt, dt], writes=[t1])
            c.op("act", lambda e: e.activation(out=mag[A2], in_=t1[A2], func=AF.Exp), reads=[t1], writes=[mag])
            c.op("dve", lambda e: e.tensor_tensor(out=th[A2], in0=li, in1=dt[A2], op=ALU.mult), reads=[lit, dt], writes=[th])
            if chk(1):
                return
            sin_of(c, k, sth, sth[A2], th[A2], th, [128, 16], s2)
            sin_of(c, k, cth, cth[A2], th[A2], th, [128, 16], s2, shift=math.pi / 2)
            if chk(2):
                return
            c.op("dve", lambda e: e.tensor_tensor(out=abr[A2], in0=mag[A2], in1=cth[A2], op=ALU.mult), reads=[mag, cth], writes=[abr])
            c.op("dve", lambda e: e.tensor_tensor(out=abi[A2], in0=mag[A2], in1=sth[A2], op=ALU.mult), reads=[mag, sth], writes=[abi])
            nr = c.sb([128, 16], F32, "nr", s2)
            den = c.sb([128, 16], F32, "den", s2)
            c.op("dve", lambda e: e.tensor_scalar(out=nr[A2], in0=abr[A2], scalar1=-1.0, scalar2=None, op0=ALU.add), reads=[abr], writes=[nr])
            c.op("dve", lambda e: e.tensor_tensor(out=t1[A2], in0=lr, in1=lr, op=ALU.mult), reads=[lrt], writes=[t1])
            c.op("dve", lambda e: e.tensor_tensor(out=t2[A2], in0=li, in1=li, op=ALU.mult), reads=[lit], writes=[t2])
            c.op("dve", lambda e: e.tensor_tensor(out=den[A2], in0=t1[A2], in1=t2[A2], op=ALU.add), reads=[t1, t2], writes=[den])
            c.op("dve", lambda e: e.reciprocal(out=den[A2], in_=den[A2]), reads=[den], writes=[den])
            c.op("dve", lambda e: e.tensor_tensor(out=t1[A2], in0=nr[A2], in1=lr, op=ALU.mult), reads=[nr, lrt], writes=[t1])
            c.op("dve", lambda e: e.tensor_tensor(out=t2[A2], in0=abi[A2], in1=li, op=ALU.mult), reads=[abi, lit], writes=[t2])
            c.op("dve", lambda e: e.tensor_tensor(out=t1[A2], in0=t1[A2], in1=t2[A2], op=ALU.add), reads=[t1, t2], writes=[t1])
            c.op("dve", lambda e: e.tensor_tensor(out=fre[A2], in0=t1[A2], in1=den[A2], op=ALU.mult), reads=[t1, den], writes=[fre])
            c.op("dve", lambda e: e.tensor_tensor(out=t1[A2], in0=abi[A2], in1=lr, op=ALU.mult), reads=[abi, lrt], writes=[t1])
            c.op("dve", lambda e: e.tensor_tensor(out=t2[A2], in0=nr[A2], in1=li, op=ALU.mult), reads=[nr, lit], writes=[t2])
            c.op("dve", lambda e: e.tensor_tensor(out=t1[A2], in0=t1[A2], in1=t2[A2], op=ALU.subtract), reads=[t1, t2], writes=[t1])
            c.op("dve", lambda e: e.tensor_tensor(out=fim[A2], in0=t1[A2], in1=den[A2], op=ALU.mult), reads=[t1, den], writes=[fim])
            if chk(3):
                return
            sW = ExitStack()
            WC = c.sb([128, 16, 2, 128], F32, "WC", sW)
            BBe = [c.sb([128, 16, 32], F32, "BBe", sW) for _ in range(2)]
            s3 = ExitStack()
            bre = c.sb([128, 16, 16], F32, "bre", s3)
            bim = c.sb([128, 16, 16], F32, "bim", s3)
            c.dma("sp", bre[:], D["s5_b_re"].rearrange("(a g) p h -> (g p) a h", g=2), writes=[bre])
            c.dma("sp", bim[:], D["s5_b_im"].rearrange("(a g) p h -> (g p) a h", g=2), writes=[bim])
            u1 = c.sb([128, 16, 16], F32, "u1", s3)
            u2 = c.sb([128, 16, 16], F32, "u2", s3)
            A3 = (slice(None), slice(None), slice(None))
            frb = fre[A2].unsqueeze(2).to_broadcast([128, 16, 16])
            fib = fim[A2].unsqueeze(2).to_broadcast([128, 16, 16])
            for ri in range(2):
                c.op("pool", lambda e, ri=ri: e.memset(BBe[ri][A3], 0.0), writes=[BBe[ri]])
            c.op("dve", lambda e: e.tensor_tensor(out=u1[A3], in0=bre[A3], in1=frb, op=ALU.mult), reads=[bre, fre], writes=[u1])
            c.op("dve", lambda e: e.tensor_tensor(out=u2[A3], in0=bim[A3], in1=fib, op=ALU.mult), reads=[bim, fim], writes=[u2])
            for g2 in range(2):
                lo = 64 * g2
                c.op("dve", lambda e, lo=lo, g2=g2: e.tensor_tensor(out=BBe[0][lo:lo + 64, :, 16 * g2:16 * g2 + 16], in0=u1[lo:lo + 64], in1=u2[lo:lo + 64], op=ALU.subtract), reads=[u1, u2], writes=[BBe[0]])
            c.op("dve", lambda e: e.tensor_tensor(out=u1[A3], in0=bim[A3], in1=frb, op=ALU.mult), reads=[bim, fre], writes=[u1])
            c.op("dve", lambda e: e.tensor_tensor(out=u2[A3], in0=bre[A3], in1=fib, op=ALU.mult), reads=[bre, fim], writes=[u2])
            for g2 in range(2):
                lo = 64 * g2
                c.op("dve", lambda e, lo=lo, g2=g2: e.tensor_tensor(out=BBe[1][lo:lo + 64, :, 16 * g2:16 * g2 + 16], in0=u1[lo:lo + 64], in1=u2[lo:lo + 64], op=ALU.add), reads=[u1, u2], writes=[BBe[1]])
            c.flush()
            s3.close()
            s4 = ExitStack()
            def mk_wb(BBl, WBt, WB3t):
                for q in range(4):
                    for ri in range(2):
                        ps = k.psum.next()
                        c.op("pe", lambda e, q=q, ri=ri: e.transpose(out=ps[:, 0:128], in_=BBl[ri][:, 4 * q:4 * q + 4, :], identity=k.ident[:, :]), reads=[BBl[ri], k.ident], writes=[ps])
                        evac(c, k, WBt, WBt[:, q, ri, :], ps, ps[:, 0:128])
                        c.op("act", lambda e, q=q, ri=ri: e.copy(out=WB3t[64:128, q, ri, :], in_=ps[64:128, 0:128]), reads=[ps], writes=[WB3t])
                        c.op("act", lambda e, q=q, ri=ri: e.mul(out=WB3t[64:96, q, ri, :], in_=ps[64:96, 0:128], mul=0.0), reads=[ps], writes=[WB3t])
            mk_wb(BBe, WB, WB3)
            BBa = [c.sb([128, 16, 32], F32, "BBa", s4) for _ in range(2)]
            v1 = c.sb([128, 16, 32], F32, "v1", s4)
            v2 = c.sb([128, 16, 32], F32, "v2", s4)
            arb32 = abr[A2].unsqueeze(2).to_broadcast([128, 16, 32])
            aib32 = abi[A2].unsqueeze(2).to_broadcast([128, 16, 32])
            c.op("dve", lambda e: e.tensor_tensor(out=v1[A3], in0=BBe[0][A3], in1=arb32, op=ALU.mult), reads=[BBe[0], abr], writes=[v1])
            c.op("dve", lambda e: e.tensor_tensor(out=v2[A3], in0=BBe[1][A3], in1=aib32, op=ALU.mult), reads=[BBe[1], abi], writes=[v2])
            c.op("dve", lambda e: e.tensor_tensor(out=BBa[0][A3], in0=v1[A3], in1=v2[A3], op=ALU.subtract), reads=[v1, v2], writes=[BBa[0]])
            c.op("dve", lambda e: e.tensor_tensor(out=v1[A3], in0=BBe[1][A3], in1=arb32, op=ALU.mult), reads=[BBe[1], abr], writes=[v1])
            c.op("dve", lambda e: e.tensor_tensor(out=v2[A3], in0=BBe[0][A3], in1=aib32, op=ALU.mult), reads=[BBe[0], abi], writes=[v2])
            c.op("dve", lambda e: e.tensor_tensor(out=BBa[1][A3], in0=v1[A3], in1=v2[A3], op=ALU.add), reads=[v1, v2], writes=[BBa[1]])
            mk_wb(BBa, WBa, WBa3)
            c.flush()
            s4.close()
            maskC = c.sb([128, 2], F32, "maskC", sW)
            pi_ = c.sb([128, 1], I32, "pi", sW)
            gi_ = c.sb([128, 1], I32, "gi", sW)
            c.op("pool", lambda e: e.iota(out=pi_[:, :], pattern=[[0, 1]], base=0, channel_multiplier=1), writes=[pi_])
            c.op("dve", lambda e: e.tensor_scalar(out=gi_[:, :], in0=pi_[:, :], scalar1=4, scalar2=1, op0=ALU.arith_shift_right, op1=ALU.bitwise_and), reads=[pi_], writes=[gi_])
            c.op("dve", lambda e: e.tensor_copy(out=maskC[:, 1:2], in_=gi_[:, :]), reads=[gi_], writes=[maskC])
            c.op("dve", lambda e: e.tensor_scalar(out=maskC[:, 0:1], in0=maskC[:, 1:2], scalar1=-1.0, scalar2=1.0, op0=ALU.mult, op1=ALU.add), reads=[maskC], writes=[maskC])
            if chk(6):
                return
            c.op("pool", lambda e: e.memset(WC[:], 0.0), writes=[WC])
            s5 = ExitStack()
            for ri, nm in enumerate(["s5_c_re", "s5_c_im"]):
                ccl = c.sb([128, 4, 64], F32, "ccl", s5)
                c.dma("sp", ccl[:], D[nm].rearrange("(q r) h p -> (r h) q p", r=8), writes=[ccl])
                cce = c.sb([128, 4, 2, 64], F32, "cce", s5)
                c.op("dve", lambda e, ccl=ccl, cce=cce: e.tensor_tensor(out=cce[:], in0=ccl[:].unsqueeze(2).to_broadcast([128, 4, 2, 64]), in1=maskC[:, :].unsqueeze(1).unsqueeze(3).to_broadcast([128, 4, 2, 64]), op=ALU.mult), reads=[ccl, maskC], writes=[cce])
                for q in range(4):
                    ps = k.psum.next()
                    c.op("pe", lambda e, q=q, cce=cce: e.transpose(out=ps[:, 0:128], in_=cce[:, q, :, :], identity=k.ident[:, :]), reads=[cce, k.ident], writes=[ps])
                    for a4 in range(4):
                        c.op("act", lambda e, q=q, a4=a4, ri=ri: e.mul(out=WC[:, 4 * q + a4, ri, 32 * a4:32 * a4 + 32], in_=ps[:, 32 * a4:32 * a4 + 32], mul=(1.0 if ri == 0 else -1.0)), reads=[ps], writes=[WC])
            c.flush()
            s5.close()
            c.op("pool", lambda e: e.tensor_copy(out=WC16[:], in_=WC[:]), reads=[WC], writes=[WC16])
            nabi = c.sb([128, 16], F32, "nabi", sW)
            c.op("dve", lambda e: e.tensor_scalar(out=nabi[A2], in0=abi[A2], scalar1=-1.0, scalar2=None, op0=ALU.mult), reads=[abi], writes=[nabi])
            c.op("dve", lambda e: e.tensor_tensor(out=mag2[A2], in0=mag[A2], in1=mag[A2], op=ALU.mult), reads=[mag], writes=[mag2])
            tmpc = c.sb([128, 128], F32, "tmpc", sW)
            for a in range(16):
                c.op("dve", lambda e, a=a: e.tensor_scalar(out=tmpc[:, :], in0=WC[:, a, 0, :], scalar1=abr[:, a:a + 1], scalar2=None, op0=ALU.mult), reads=[WC, abr], writes=[tmpc])
                c.op("dve", lambda e, a=a: e.scalar_tensor_tensor(out=WCa[:, a, 0, :], in0=WC[:, a, 1, :], scalar=abi[:, a:a + 1], in1=tmpc[:, :], op0=ALU.mult, op1=ALU.add), reads=[WC, abi, tmpc], writes=[WCa])
                c.op("dve", lambda e, a=a: e.tensor_scalar(out=tmpc[:, :], in0=WC[:, a, 1, :], scalar1=abr[:, a:a + 1], scalar2=None, op0=ALU.mult), reads=[WC, abr], writes=[tmpc])
                c.op("dve", lambda e, a=a: e.scalar_tensor_tensor(out=WCa[:, a, 1, :], in0=WC[:, a, 0, :], scalar=nabi[:, a:a + 1], in1=tmpc[:, :], op0=ALU.mult, op1=ALU.add), reads=[WC, nabi, tmpc], writes=[WCa])
            p5 = c.sb([128, 1], I32, "p5", sW)
            p5f = c.sb([128, 1], F32, "p5f", sW)
            c5 = c.sb([128, SEG], I32, "c5", sW)
            c5f = c.sb([128, SEG], F32, "c5f", sW)
            maskB = c.sb([128, SEG], F32, "maskB", sW)
            c.op("dve", lambda e: e.tensor_scalar(out=p5[:, :], in0=pi_[:, :], scalar1=5, scalar2=None, op0=ALU.arith_shift_right), reads=[pi_], writes=[p5])
            c.op("dve", lambda e: e.tensor_copy(out=p5f[:, :], in_=p5[:, :]), reads=[p5], writes=[p5f])
            c5i = c.sb([128, SEG], I32, "c5i", sW)
            c.op("pool", lambda e: e.iota(out=c5i[A2], pattern=[[1, SEG]], base=0, channel_multiplier=0), writes=[c5i])
            c.op("dve", lambda e: e.tensor_scalar(out=c5[A2], in0=c5i[A2], scalar1=5, scalar2=None, op0=ALU.arith_shift_right), reads=[c5i], writes=[c5])
            c.op("dve", lambda e: e.tensor_copy(out=c5f[A2], in_=c5[A2]), reads=[c5], writes=[c5f])
            c.op("dve", lambda e: e.tensor_scalar(out=maskB[A2], in0=c5f[A2], scalar1=p5f[:, 0:1], scalar2=None, op0=ALU.is_equal), reads=[c5f, p5f], writes=[maskB])
            for q in range(4):
                ps = k.psum.next()
                for a4 in range(4):
                    for ri in range(2):
                        c.op("pe", lambda e, q=q, a4=a4, ri=ri: e.matmul(ps[:, 0:128], BBe[ri][:, 4 * q:4 * q + 4, :], WC[:, 4 * q + a4, ri, :], start=(a4 == 0 and ri == 0), stop=(a4 == 3 and ri == 1)), reads=[BBe[ri], WC], writes=[ps], accum=not (a4 == 0 and ri == 0))
                c.op("dve", lambda e, q=q: e.tensor_tensor(out=G0t[:, q, :], in0=ps[:, 0:128], in1=maskB[A2], op=ALU.mult), reads=[ps, maskB], writes=[G0t])
            c.flush()
            sW.close()
            jf = c.sb([128, SEG], F32, "jf", s2)
            ji = c.sb([128, SEG], I32, "ji", s2)
            c.op("pool", lambda e: e.iota(out=ji[A2], pattern=[[1, SEG]], base=0, channel_multiplier=0), writes=[ji])
            c.op("dve", lambda e: e.tensor_copy(out=jf[A2], in_=ji[A2]), reads=[ji], writes=[jf])
            if chk(8):
                return
            th2 = c.sb([128, 16], F32, "th2", s2)
            c.op("dve", lambda e: e.tensor_scalar(out=th2[A2], in0=th[A2], scalar1=2.0, scalar2=None, op0=ALU.mult), reads=[th], writes=[th2])
            arg = c.sb([128, 4, SEG], F32, "arg", s2)
            for ch in range(4):
                c.op("dve", lambda e, ch=ch: e.tensor_tensor(out=arg[A3], in0=jf[A2].unsqueeze(1).to_broadcast([128, 4, SEG]), in1=th2[:, 4 * ch:4 * ch + 4].unsqueeze(2).to_broadcast([128, 4, SEG]), op=ALU.mult), reads=[jf, th2], writes=[arg])
                sin_of(c, k, ST, ST[:, 4 * ch:4 * ch + 4, :], arg[A3], arg, [128, 4, SEG], s2)
                sin_of(c, k, CT, CT[:, 4 * ch:4 * ch + 4, :], arg[A3], arg, [128, 4, SEG], s2, shift=math.pi / 2)
            c.op("dve", lambda e: e.tensor_copy(out=c2[A2], in_=CT[:, :, 1]), reads=[CT], writes=[c2])
            c.op("dve", lambda e: e.tensor_copy(out=s2t[A2], in_=ST[:, :, 1]), reads=[ST], writes=[s2t])
            c.flush()
        import os
        if os.environ.get("S5STOP"):
            return

        c.op("dve", lambda e: e.memset(xpr[A2], 0.0), writes=[xprT])
        c.op("dve", lambda e: e.memset(xpi[A2], 0.0), writes=[xpiT])
        TOK = 2 * SEG
        ygf = c.sb([128, 4, TOK], F32, "ygf", st)
        ygb = c.sb([128, 4, TOK], BF16, "ygb", st)
        sg_p = Pool_(c, [128, TOK], F32, 2, "sg", st)
        zin_r = c.sb([128, 16], F32, "zinr", st)
        zin_i = c.sb([128, 16], F32, "zini", st)
        z1 = c.sb([128, 16], F32, "z1", st)
        z2 = c.sb([128, 16], F32, "z2", st)
        dtl, dc, _ = k.vt["s5_d"]
        bgt, bgc, _ = k.vt["b_glu"]
        mixy = k.mixy

        def u_proj(c0, w):
            rmsnorm_block(c, k, c0, w, "norm_mix_0", xn)
            for m in range(4):
                ps = k.psum.next()
                proj(c, k, ps[:, 0:w], ps, wu, lambda kt, m=m: wu[:, kt, m * 128:(m + 1) * 128], xn, w)
                evac(c, k, useg, useg[:, m, 0:w], ps, ps[:, 0:w])

        def y_tail(c0, w, yps, pair=False):
            n = w // 2
            for q in range(4):
                if pair:
                    c.op("dve", lambda e, q=q: e.scalar_tensor_tensor(out=ygf[:, q, 1:w:2], in0=useg[:, q, 1:w:2], scalar=dtl[:, dc + q:dc + q + 1], in1=yps[q][:, 0:n], op0=ALU.mult, op1=ALU.add), reads=[useg, dtl, yps[q]], writes=[ygf])
                    c.op("dve", lambda e, q=q: e.scalar_tensor_tensor(out=ygf[:, q, 0:w:2], in0=useg[:, q, 0:w:2], scalar=dtl[:, dc + q:dc + q + 1], in1=yps[q][:, SEG:SEG + n], op0=ALU.mult, op1=ALU.add), reads=[useg, dtl, yps[q]], writes=[ygf])
                else:
                    c.op("dve", lambda e, q=q: e.scalar_tensor_tensor(out=ygf[:, q, 0:w], in0=useg[:, q, 0:w], scalar=dtl[:, dc + q:dc + q + 1], in1=yps[q][:, 0:w], op0=ALU.mult, op1=ALU.add), reads=[useg, dtl, yps[q]], writes=[ygf])
                c.op("act", lambda e, q=q: e.activation(out=ygf[:, q, 0:w], in_=ygf[:, q, 0:w], func=GELU), reads=[ygf], writes=[ygf])
                c.op("act", lambda e, q=q: e.copy(out=ygb[:, q, 0:w], in_=ygf[:, q, 0:w]), reads=[ygf], writes=[ygb])
            for qo in range(4):
                ps = k.psum.next()
                for q in range(4):
                    c.op("pe", lambda e, q=q, qo=qo: e.matmul(ps[:, 0:w], wglu[:, q, qo * 128:(qo + 1) * 128], ygb[:, q, 0:w], start=(q == 0), stop=(q == 3)), reads=[wglu, ygb], writes=[ps], accum=(q > 0))
                sg = sg_p.next()
                c.op("act", lambda e, qo=qo: e.activation(out=sg[:, 0:w], in_=ps[:, 0:w], func=AF.Sigmoid, bias=bgt[:, bgc + qo:bgc + qo + 1], scale=1.0), reads=[ps, bgt], writes=[sg])
                c.op("dve", lambda e, qo=qo: e.tensor_tensor(out=mixy[:, qo, c0:c0 + w], in0=ygf[:, qo, 0:w], in1=sg[:, 0:w], op=ALU.mult), reads=[ygf, sg], writes=[mixy])

        allb = k.psum.ts
        ybanks = allb[0:4]
        k.psum.ts = allb[4:8]
        k.psum.i = 0
        segs = [(s0, min(TOK, NP_TOK - s0)) for s0 in range(0, NP_TOK, TOK)]
        with ExitStack() as sm:
            tp = Pool_(c, [128, SEG], F32, 12, "s5t", sm)
            wr_p = Pool_(c, [128, SEG], F32, 2, "wr", sm)
            wi_p = Pool_(c, [128, SEG], F32, 2, "wi", sm)
            zr_p = Pool_(c, [128, SEG], F32, 2, "zr", sm)
            zi_p = Pool_(c, [128, SEG], F32, 2, "zi", sm)
            xb_p = [Pool_(c, [128, SEG + 4], F32, 2, "xb", sm) for _ in range(2)]
            xh_p = [Pool_(c, [128, SEG + 4], BF16, 6, "xh", sm) for _ in range(2)]
            for (c0, w) in segs:
                n = w // 2
                u_proj(c0, w)
                c.op("dve", lambda e: e.tensor_tensor(out=z1[A2], in0=c2[A2], in1=xpr[A2], op=ALU.mult), reads=[c2, xprT], writes=[z1])
                c.op("dve", lambda e: e.tensor_tensor(out=z2[A2], in0=s2t[A2], in1=xpi[A2], op=ALU.mult), reads=[s2t, xpiT], writes=[z2])
                c.op("dve", lambda e: e.tensor_tensor(out=zin_r[A2], in0=z1[A2], in1=z2[A2], op=ALU.subtract), reads=[z1, z2], writes=[zin_r])
                c.op("dve", lambda e: e.tensor_tensor(out=z1[A2], in0=s2t[A2], in1=xpr[A2], op=ALU.mult), reads=[s2t, xprT], writes=[z1])
                c.op("dve", lambda e: e.tensor_tensor(out=z2[A2], in0=c2[A2], in1=xpi[A2], op=ALU.mult), reads=[c2, xpiT], writes=[z2])
                c.op("dve", lambda e: e.tensor_tensor(out=zin_i[A2], in0=z1[A2], in1=z2[A2], op=ALU.add), reads=[z1, z2], writes=[zin_i])
                xbs = {}
                for a in range(16):
                    q, a4 = a // 4, a % 4
                    lo = 32 * a4
                    bps = k.psum.next()
                    for ri in range(2):
                        if a4 < 3:
                            Wa_, Wb_, r0, r1 = WBa, WB, lo, lo + 32
                        else:
                            Wa_, Wb_, r0, r1 = WBa3, WB3, 64, 128
                        c.op("pe", lambda e, ri=ri, Wa_=Wa_, r0=r0, r1=r1: e.matmul(bps[:, ri * SEG:ri * SEG + n], Wa_[r0:r1, q, ri, :], useg[r0:r1, q, 0:w:2], start=True, stop=False), reads=[Wa_, useg], writes=[bps], accum=(ri > 0))
                        c.op("pe", lambda e, ri=ri, Wb_=Wb_, r0=r0, r1=r1: e.matmul(bps[:, ri * SEG:ri * SEG + n], Wb_[r0:r1, q, ri, :], useg[r0:r1, q, 1:w:2], start=False, stop=True), reads=[Wb_, useg], writes=[bps], accum=True)
                    br, bi = bps[:, 0:n], bps[:, SEG:SEG + n]
                    ct, st_ = CT[:, a, 0:n], ST[:, a, 0:n]
                    t1, t2, t3, t4 = tp.next(), tp.next(), tp.next(), tp.next()
                    c.op("dve", lambda e: e.tensor_tensor(out=t1[:, 0:n], in0=br, in1=ct, op=ALU.mult), reads=[bps, CT], writes=[t1])
                    c.op("dve", lambda e: e.tensor_tensor(out=t2[:, 0:n], in0=bi, in1=st_, op=ALU.mult), reads=[bps, ST], writes=[t2])
                    c.op("dve", lambda e: e.tensor_tensor(out=t3[:, 0:n], in0=bi, in1=ct, op=ALU.mult), reads=[bps, CT], writes=[t3])
                    c.op("dve", lambda e: e.tensor_tensor(out=t4[:, 0:n], in0=br, in1=st_, op=ALU.mult), reads=[bps, ST], writes=[t4])
                    wr, wi = wr_p.next(), wi_p.next()
                    c.op("pool", lambda e: e.tensor_tensor(out=wr[:, 0:n], in0=t1[:, 0:n], in1=t2[:, 0:n], op=ALU.add), reads=[t1, t2], writes=[wr])
                    c.op("pool", lambda e: e.tensor_tensor(out=wi[:, 0:n], in0=t3[:, 0:n], in1=t4[:, 0:n], op=ALU.subtract), reads=[t3, t4], writes=[wi])
                    zr, zi = zr_p.next(), zi_p.next()
                    mg = mag2[:, a:a + 1].to_broadcast([128, n])
                    c.op("dve", lambda e: e.tensor_tensor_scan(out=zr[:, 0:n], data0=mg, data1=wr[:, 0:n], initial=zin_r[:, a:a + 1], op0=ALU.mult, op1=ALU.add), reads=[mag2, wr, zin_r], writes=[zr])
                    c.op("dve", lambda e: e.tensor_tensor_scan(out=zi[:, 0:n], data0=mg, data1=wi[:, 0:n], initial=zin_i[:, a:a + 1], op0=ALU.mult, op1=ALU.add), reads=[mag2, wi, zin_i], writes=[zi])
                    u1, u2, u3, u4 = tp.next(), tp.next(), t1, t2
                    c.op("pool", lambda e: e.tensor_tensor(out=u1[:, 0:n], in0=zr[:, 0:n], in1=ct, op=ALU.mult), reads=[zr, CT], writes=[u1])
                    c.op("pool", lambda e: e.tensor_tensor(out=u2[:, 0:n], in0=zi[:, 0:n], in1=st_, op=ALU.mult), reads=[zi, ST], writes=[u2])
                    c.op("pool", lambda e: e.tensor_tensor(out=u3[:, 0:n], in0=zr[:, 0:n], in1=st_, op=ALU.mult), reads=[zr, ST], writes=[u3])
                    c.op("pool", lambda e: e.tensor_tensor(out=u4[:, 0:n], in0=zi[:, 0:n], in1=ct, op=ALU.mult), reads=[zi, CT], writes=[u4])
                    xbr, xbi = xb_p[0].next(), xb_p[1].next()
                    c.op("act", lambda e, a=a: e.copy(out=xbr[:, 0:1], in_=xpr[:, a:a + 1]), reads=[xprT[a]], writes=[xbr])
                    c.op("act", lambda e, a=a: e.copy(out=xbi[:, 0:1], in_=xpi[:, a:a + 1]), reads=[xpiT[a]], writes=[xbi])
                    c.op("dve", lambda e: e.tensor_tensor(out=xbr[:, 1:1 + n], in0=u1[:, 0:n], in1=u2[:, 0:n], op=ALU.subtract), reads=[u1, u2], writes=[xbr])
                    c.op("dve", lambda e: e.tensor_tensor(out=xbi[:, 1:1 + n], in0=u3[:, 0:n], in1=u4[:, 0:n], op=ALU.add), reads=[u3, u4], writes=[xbi])
                    c.op("act", lambda e, a=a: e.copy(out=xpr[:, a:a + 1], in_=xbr[:, n:n + 1]), reads=[xbr], writes=[xprT[a]])
                    c.op("act", lambda e, a=a: e.copy(out=xpi[:, a:a + 1], in_=xbi[:, n:n + 1]), reads=[xbi], writes=[xpiT[a]])
                    xhr, xhi = xh_p[0].next(), xh_p[1].next()
                    c.op("act", lambda e: e.copy(out=xhr[:, 0:1 + n], in_=xbr[:, 0:1 + n]), reads=[xbr], writes=[xhr])
                    c.op("act", lambda e: e.copy(out=xhi[:, 0:1 + n], in_=xbi[:, 0:1 + n]), reads=[xbi], writes=[xhi])
                    xbs[a] = (xhr, xhi)
                    if a4 == 3:
                        yb = ybanks[q]
                        first = True
                        for aa in range(4 * q, 4 * q + 4):
                            for ri in range(2):
                                xs_ = xbs[aa][ri]
                                last = (aa == 4 * q + 3 and ri == 1)
                                c.op("pe", lambda e, aa=aa, ri=ri, xs_=xs_, first=first, last=last: e.matmul(yb[:, 0:n], WC16[:, aa, ri, :], xs_[:, 1:1 + n], start=first, stop=last), reads=[WC16, xs_], writes=[yb], accum=not first)
                                first = False
                        first = True
                        for aa in range(4 * q, 4 * q + 4):
                            for ri in range(2):
                                xs_ = xbs[aa][ri]
                                c.op("pe", lambda e, aa=aa, ri=ri, xs_=xs_, first=first: e.matmul(yb[:, SEG:SEG + n], WCa[:, aa, ri, :], xs_[:, 0:n], start=first, stop=False), reads=[WCa, xs_], writes=[yb], accum=True)
                                first = False
                        c.op("pe", lambda e: e.matmul(yb[:, SEG:SEG + n], G0t[:, q, :], useg[:, q, 0:w:2], start=False, stop=True), reads=[G0t, useg], writes=[yb], accum=True)
                y_tail(c0, w, ybanks, pair=True)
            c.flush()

        if chk(17):
            return
        for nm, xp_, xpT_ in (("s5_re_prompt", xpr, xprT), ("s5_im_prompt", xpi, xpiT)):
            ps = k.psum.next()
            c.op("pe", lambda e, xp_=xp_: e.transpose(out=ps[0:16, 0:128], in_=xp_[:, :], identity=k.ident[:, :]), reads=[xpT_, k.ident], writes=[ps])
            so = c.sb([16, 128], F32, "s5o", st)
            evac(c, k, so, so[:, :], ps, ps[0:16, 0:128])
            c.dma("sp", O[nm].rearrange("(a g) p -> a (g p)", g=2), so[:, :], reads=[so])

        if chk(18):
            return
        x0 = []
        sgl = c.sb([32, 2048], F32, "s5ld", st)
        for nm in ("state_s5_re", "state_s5_im"):
            sg = sgl
            c.dma("sp", sg[0:16, :], D[nm].rearrange("s g p -> s (g p)"), writes=[sg])
            xt = c.sb([128, 16, 16], F32, "x0", st)
            ps = k.psum.next()
            for a in range(16):
                c.op("pe", lambda e, a=a: e.transpose(out=ps[:, a * 16:(a + 1) * 16], in_=sg[0:16, a * 128:(a + 1) * 128], identity=k.ident[0:16, 0:16]), reads=[sg, k.ident], writes=[ps], accum=(a > 0))
            evac(c, k, xt, xt[:, :, :], ps, ps[:, 0:256].rearrange("p (a s) -> p a s", s=16))
            x0.append(xt)
        x0r, x0i = x0
        A3 = (slice(None), slice(None), slice(None))
        arb = abr[A2].unsqueeze(2).to_broadcast([128, 16, 16])
        aib = abi[A2].unsqueeze(2).to_broadcast([128, 16, 16])
        s1 = c.sb([128, 16, 16], F32, "s1", st)
        s2_ = c.sb([128, 16, 16], F32, "s2", st)
        xnr = c.sb([128, 16, 16], F32, "xnr", st)
        xni = c.sb([128, 16, 16], F32, "xni", st)
        u_proj(NP_TOK, 16)
        bps4 = [k.psum.next() for _ in range(4)]
        for a in range(16):
            q, a4 = a // 4, a % 4
            lo = 32 * a4
            bp = bps4[a4]
            for ri in range(2):
                col = (q * 2 + ri) * 16
                if a4 < 3:
                    c.op("pe", lambda e, ri=ri, q=q, lo=lo, bp=bp, col=col: e.matmul(bp[:, col:col + 16], WB[lo:lo + 32, q, ri, :], useg[lo:lo + 32, q, 0:16], start=True, stop=True), reads=[WB, useg], writes=[bp], accum=(q + ri > 0))
                else:
                    c.op("pe", lambda e, ri=ri, q=q, bp=bp, col=col: e.matmul(bp[:, col:col + 16], WB3[64:128, q, ri, :], useg[64:128, q, 0:16], start=True, stop=True), reads=[WB3, useg], writes=[bp], accum=(q + ri > 0))
        bu = c.sb([128, 16, 2, 16], F32, "bu", st)
        for a4 in range(4):
            c.op("act", lambda e, a4=a4: e.copy(out=bu[:, a4:16:4, :, :], in_=bps4[a4][:, 0:128].rearrange("p (q r s) -> p q r s", r=2, s=16)), reads=[bps4[a4]], writes=[bu])
        bpv = bu
        bps = bu
        c.op("dve", lambda e: e.tensor_tensor(out=s1[A3], in0=x0r[A3], in1=arb, op=ALU.mult), reads=[x0r, abr], writes=[s1])
        c.op("dve", lambda e: e.tensor_tensor(out=s2_[A3], in0=x0i[A3], in1=aib, op=ALU.mult), reads=[x0i, abi], writes=[s2_])
        c.op("dve", lambda e: e.tensor_tensor(out=s1[A3], in0=s1[A3], in1=s2_[A3], op=ALU.subtract), reads=[s1, s2_], writes=[s1])
        c.op("dve", lambda e: e.tensor_tensor(out=xnr[A3], in0=bpv[:, :, 0, :], in1=s1[A3], op=ALU.add), reads=[bps, s1], writes=[xnr])
        c.op("dve", lambda e: e.tensor_tensor(out=s1[A3], in0=x0i[A3], in1=arb, op=ALU.mult), reads=[x0i, abr], writes=[s1])
        c.op("dve", lambda e: e.tensor_tensor(out=s2_[A3], in0=x0r[A3], in1=aib, op=ALU.mult), reads=[x0r, abi], writes=[s2_])
        c.op("dve", lambda e: e.tensor_tensor(out=s1[A3], in0=s1[A3], in1=s2_[A3], op=ALU.add), reads=[s1, s2_], writes=[s1])
        c.op("dve", lambda e: e.tensor_tensor(out=xni[A3], in0=bpv[:, :, 1, :], in1=s1[A3], op=ALU.add), reads=[bps, s1], writes=[xni])
        yps = []
        xsh = [c.sb([128, 16, 16], BF16, "xsh", st) for _ in range(2)]
        c.op("act", lambda e: e.copy(out=xsh[0][A3], in_=xnr[A3]), reads=[xnr], writes=[xsh[0]])
        c.op("act", lambda e: e.copy(out=xsh[1][A3], in_=xni[A3]), reads=[xni], writes=[xsh[1]])
        for q in range(4):
            yp = ybanks[q]
            for a4 in range(4):
                a = 4 * q + a4
                for ri in range(2):
                    xs_ = xsh[ri]
                    c.op("pe", lambda e, a=a, ri=ri, xs_=xs_: e.matmul(yp[:, 0:16], WC16[:, a, ri, :], xs_[:, a, :], start=(a4 == 0 and ri == 0), stop=(a4 == 3 and ri == 1)), reads=[WC16, xs_], writes=[yp], accum=not (a4 == 0 and ri == 0))
            yps.append(yp)
        y_tail(NP_TOK, 16, yps, pair=False)
        for nm, xt in (("s5_re_sample", xnr), ("s5_im_sample", xni)):
            stg = store_T(c, k, xt, lambda t, xt=xt: xt[:, t, :], 16, 128, 16, None, st, "s5st", stg=sgl)
            c.dma("sp", O[nm].rearrange("s g p -> s (g p)"), stg[0:16, :], reads=[stg])
        k.psum.ts = allb
        k.psum.i = 0
        c.flush()
GELU = AF.Gelu_apprx_tanh


def add_to_x(c, k, m, c0, w, ps):
    c.op("dve", lambda e: e.tensor_tensor(out=k.X[m][:, c0:c0 + w], in0=ps[:, 0:w], in1=k.X[m][:, c0:c0 + w], op=ALU.add), reads=[ps, k.X[m].ts(c0, c0 + w)], writes=[k.X[m].ts(c0, c0 + w)])


def phase3(c, k, D, O):
    with ExitStack() as st:
        wo = c.sb([128, 8, 1024], BF16, "wo", st)
        W = D["w_out_0"].rearrange("(k p) n -> p k n", p=128)
        woT = [T(wo.ap[:, :, 128 * m:128 * (m + 1)], "woT%d" % m) for m in range(8)]
        for m in range(8):
            load_w_bf16(c, k, woT[m], wo[:, :, 128 * m:128 * (m + 1)], W[:, :, 128 * m:128 * (m + 1)])
        for (c0, c1) in BLOCKS:
            w = c1 - c0
            for m in range(8):
                ps = k.psum.next()
                for kt in range(8):
                    src = k.mixo if kt < 4 else k.mixy
                    c.op("pe", lambda e, kt=kt, src=src: e.matmul(ps[:, 0:w], wo[:, kt, m * 128:(m + 1) * 128], src[:, kt % 4, c0:c0 + w], start=(kt == 0), stop=(kt == 7)), reads=[woT[m], src], writes=[ps], accum=(kt > 0))
                add_to_x(c, k, m, c0, w, ps)
        c.flush()


def store_T(c, k, src_t, src_ap_fn, ntile, pw, n, dst_rows_fn, stack, name, stg=None):
    if stg is None:
        stg = c.sb([32 if n <= 32 else 128, ntile * pw], F32, name, stack)
    per = 512 // pw
    for t0 in range(0, ntile, per):
        ps = k.psum.next()
        nt = min(per, ntile - t0)
        for i in range(nt):
            c.op("pe", lambda e, i=i: e.transpose(out=ps[0:n, i * pw:(i + 1) * pw], in_=src_ap_fn(t0 + i), identity=k.ident[0:pw, 0:pw]), reads=[src_t, k.ident], writes=[ps], accum=(i > 0))
        evac(c, k, stg, stg[0:n, t0 * pw:(t0 + nt) * pw], ps, ps[0:n, 0:nt * pw])
    return stg


def phase_ffn(c, k, D, O, l):
    GROUPS = [(0, 4), (4, 4), (8, 4), (12, 4), (16, 4), (20, 2)]
    with ExitStack() as st:
        xn = c.sb([128, 8, NT], BF16, "fxn", st)
        k.rspool = Pool_(c, [128, 512], F32, 2, "rs", st)
        cbs = c.sb([32, D_FF], F32, "cbs", st)
        cbuf = c.sb([128, 22, 32], F32, "cbuf", st)
        gtail = c.sb([128, 22, 18], F32, "gtail", st)
        wup_p = Pool_(c, [128, 8, 2, 512], BF16, 2, "wup", st)
        wdn_p = Pool_(c, [128, 4, 1024], BF16, 2, "wdn", st)
        gbuf = [c.sb([128, 516], F32, "gbuf", st) for m in range(4)]
        acc_p = Pool_(c, [128, 512], F32, 2, "acc", st)
        ga_p = Pool_(c, [128, 512], F32, 2, "ga", st)
        hb_p = [Pool_(c, [128, 512], BF16, 2, "hb", st) for m in range(4)]
        Wup = D["ffn_w_up"][l].rearrange("(k p) (t n) -> p k t n", p=128, t=2)
        Wdn = D["ffn_w_down"][l].rearrange("(k p) n -> p k n", p=128)

        def load_group(g):
            t0, nt = GROUPS[g]
            wu, wd = wup_p.next(), wdn_p.next()
            for t in range(2):
                load_w_bf16(c, k, wu, wu[:, :, t, 0:128 * nt], Wup[:, :, t, 128 * t0:128 * (t0 + nt)])
            load_w_bf16(c, k, wd, wd[:, 0:nt, :], Wdn[:, t0:t0 + nt, :])
            return wu, wd
        nxt = load_group(0)
        c.dma("sp", cbs[0:32, :], D["cache_ffn_conv"][l].rearrange("s t f -> (s t) f"), writes=[cbs])
        for t0 in (0, 16):
            ps = k.psum.next()
            nt = min(16, 22 - t0)
            for i in range(nt):
                c.op("pe", lambda e, i=i: e.transpose(out=ps[:, i * 32:(i + 1) * 32], in_=cbs[0:32, (t0 + i) * 128:(t0 + i + 1) * 128], identity=k.ident[0:32, 0:32]), reads=[cbs, k.ident], writes=[ps], accum=(i > 0))
            evac(c, k, cbuf, cbuf[:, t0:t0 + nt, :], ps, ps[:, 0:nt * 32].rearrange("p (a b) -> p a b", b=32))
        xnb = {c0: T(xn.ap[:, :, c0:c1], "fxn%d" % c0) for (c0, c1) in BLOCKS}
        for (c0, c1) in BLOCKS:
            rmsnorm_block(c, k, c0, c1 - c0, "norm_ffn%d" % l, xnb[c0])
        w0t, w0c, _ = k.vt["fcw%d_0" % l]
        w1t, w1c, _ = k.vt["fcw%d_1" % l]
        w2t, w2c, _ = k.vt["fcw%d_2" % l]
        bt, bc, _ = k.vt["fcb%d" % l]
        for g in range(len(GROUPS)):
            t0g, ntg = GROUPS[g]
            wu, wd = nxt
            if g + 1 < len(GROUPS):
                nxt = load_group(g + 1)
            for m in range(ntg):
                c.op("pool", lambda e, m=m: e.memset(gbuf[m][:, 0:2], 0.0), writes=[gbuf[m]])
            for (c0, c1) in BLOCKS:
                w = c1 - c0
                wp = w if w == 512 else 16
                hbs = []
                for m in range(ntg):
                    hm = t0g + m
                    gps = k.psum.next()
                    proj(c, k, gps[:, 0:w], gps, wu, lambda kt, m=m: wu[:, kt, 0, m * 128:(m + 1) * 128], xnb[c0], w)
                    vps = k.psum.next()
                    proj(c, k, vps[:, 0:w], vps, wu, lambda kt, m=m: wu[:, kt, 1, m * 128:(m + 1) * 128], xnb[c0], w)
                    gb = gbuf[m]
                    c.op("act", lambda e: e.copy(out=gb[:, 2:2 + w], in_=gps[:, 0:w]), reads=[gps], writes=[gb])
                    acc = acc_p.next()
                    c.op("dve", lambda e: e.tensor_scalar(out=acc[:, 0:wp], in0=gb[:, 2:2 + wp], scalar1=w2t[:, w2c + hm:w2c + hm + 1], scalar2=bt[:, bc + hm:bc + hm + 1], op0=ALU.mult, op1=ALU.add), reads=[gb, w2t, bt], writes=[acc])
                    c.op("dve", lambda e: e.scalar_tensor_tensor(out=acc[:, 0:wp], in0=gb[:, 1:1 + wp], scalar=w1t[:, w1c + hm:w1c + hm + 1], in1=acc[:, 0:wp], op0=ALU.mult, op1=ALU.add), reads=[gb, w1t, acc], writes=[acc])
                    c.op("dve", lambda e: e.scalar_tensor_tensor(out=acc[:, 0:wp], in0=gb[:, 0:wp], scalar=w0t[:, w0c + hm:w0c + hm + 1], in1=acc[:, 0:wp], op0=ALU.mult, op1=ALU.add), reads=[gb, w0t, acc], writes=[acc])
                    if w == 512:
                        c.op("pool", lambda e: e.tensor_copy(out=gb[:, 0:2], in_=gb[:, 512:514]), reads=[gb], writes=[gb])
                    else:
                        cbv = cbuf[:, hm, :].rearrange("p (s t) -> p s t", t=2)
                        c.op("dve", lambda e: e.tensor_scalar(out=acc[:, 16:32], in0=gb[:, 18:34], scalar1=w2t[:, w2c + hm:w2c + hm + 1], scalar2=bt[:, bc + hm:bc + hm + 1], op0=ALU.mult, op1=ALU.add), reads=[gb, w2t, bt], writes=[acc])
                        c.op("dve", lambda e: e.scalar_tensor_tensor(out=acc[:, 16:32], in0=cbv[:, :, 1], scalar=w1t[:, w1c + hm:w1c + hm + 1], in1=acc[:, 16:32], op0=ALU.mult, op1=ALU.add), reads=[cbuf, w1t, acc], writes=[acc])
                        c.op("dve", lambda e: e.scalar_tensor_tensor(out=acc[:, 16:32], in0=cbv[:, :, 0], scalar=w0t[:, w0c + hm:w0c + hm + 1], in1=acc[:, 16:32], op0=ALU.mult, op1=ALU.add), reads=[cbuf, w0t, acc], writes=[acc])
                        c.op("pool", lambda e: e.tensor_copy(out=gtail[:, hm, :], in_=gb[:, 16:34]), reads=[gb], writes=[gtail])
                    ga = ga_p.next()
                    c.op("act", lambda e: e.activation(out=ga[:, 0:w], in_=acc[:, 0:w], func=GELU), reads=[acc], writes=[ga])
                    hb = hb_p[m].next()
                    c.op("dve", lambda e: e.tensor_tensor(out=hb[:, 0:w], in0=vps[:, 0:w], in1=ga[:, 0:w], op=ALU.mult), reads=[vps, ga], writes=[hb])
                    hbs.append(hb)
                for mo in range(8):
                    ps = k.psum.next()
                    for m in range(ntg):
                        c.op("pe", lambda e, m=m: e.matmul(ps[:, 0:w], wd[:, m, mo * 128:(mo + 1) * 128], hbs[m][:, 0:w], start=(m == 0), stop=(m == ntg - 1)), reads=[wd, hbs[m]], writes=[ps], accum=(m > 0))
                    add_to_x(c, k, mo, c0, w, ps)
        stg = store_T(c, k, gtail, lambda t: gtail[:, t, :], 22, 128, 18, None, st, "gts", stg=cbs)
        c.dma("sp", O["ffn_conv_prompt"][l], stg[0:2, :], reads=[stg])
        c.dma("sp", O["ffn_conv_sample"][l][:, 1, :], stg[2:18, :], reads=[stg])
        c.dma("sp", O["ffn_conv_sample"][l][:, 0, :], D["cache_ffn_conv"][l][:, 1, :])
        c.flush()


class T_view:
    def __init__(self, t, c0):
        self.t, self.c0 = t, c0

    def __getitem__(self, idx):
        p, kt, sl = idx
        return self.t.ap[p, kt, self.c0 + (sl.start or 0):self.c0 + sl.stop]


def phase7(c, k, D, O):
    with ExitStack() as st:
        k.rspool = Pool_(c, [128, 512], F32, 2, "rs", st)
        xnf_p = Pool_(c, [128, 8, 512], F32, 2, "xnf", st)
        sq_p = Pool_(c, [128, 8, 512], BF16, 2, "sq7", st)
        stg_p = Pool_(c, [128, 4, 1024], F32, 2, "ystg", st)
        gt, gc, _ = k.vt["norm_final"]
        for (c0, c1) in BLOCKS:
            w = c1 - c0
            xnf, sq = xnf_p.next(), sq_p.next()
            for kt in range(8):
                c.op("act", lambda e, kt=kt: e.activation(out=sq[:, kt, 0:w], in_=k.X[kt][:, c0:c0 + w], func=AF.Square), reads=[k.X[kt].ts(c0, c0 + w)], writes=[sq])
            ps = k.psum.next()
            for kt in range(8):
                c.op("pe", lambda e, kt=kt: e.matmul(ps[:, 0:w], k.ones_bf[:, :], sq[:, kt, 0:w], start=(kt == 0), stop=(kt == 7)), reads=[sq, k.ones_bf], writes=[ps], accum=(kt > 0))
            rs = k.rspool.next()
            c.op("act", lambda e: e.activation(out=rs[:, 0:w], in_=ps[:, 0:w], func=AF.Ln, bias=k.eps_t[:, 0:1], scale=1.0 / 1024.0), reads=[ps, k.eps_t], writes=[rs])
            c.op("act", lambda e: e.activation(out=rs[:, 0:w], in_=rs[:, 0:w], func=AF.Exp, scale=-0.5), reads=[rs], writes=[rs])
            for kt in range(8):
                c.op("dve", lambda e, kt=kt: e.scalar_tensor_tensor(out=xnf[:, kt, 0:w], in0=k.X[kt][:, c0:c0 + w], scalar=gt[:, gc + kt:gc + kt + 1], in1=rs[:, 0:w], op0=ALU.mult, op1=ALU.mult), reads=[k.X[kt].ts(c0, c0 + w), gt, rs], writes=[xnf])
            stg = stg_p.next()
            nj = (w + 127) // 128
            for j in range(nj):
                tw = min(128, w - 128 * j)
                for half in range(2):
                    ps2 = k.psum.next()
                    for q in range(4):
                        kt = half * 4 + q
                        c.op("pe", lambda e, q=q, kt=kt: e.transpose(out=ps2[0:tw, q * 128:(q + 1) * 128], in_=xnf[:, kt, 128 * j:128 * j + tw], identity=k.ident[:, :]), reads=[xnf, k.ident], writes=[ps2], accum=(q > 0))
                    evac(c, k, stg, stg[0:tw, j, half * 512:(half + 1) * 512], ps2, ps2[0:tw, 0:512])
            if w == 512:
                if c0 == 0:
                    c.dma("sp", O["y_prompt"][0:112, :], stg[16:128, 0, :], reads=[stg])
                    c.dma("sp", O["y_prompt"][112:496, :].rearrange("(j p) f -> p j f", p=128), stg[:, 1:4, :], reads=[stg])
                else:
                    c.dma("sp", O["y_prompt"][c0 - 16:c0 + 496, :].rearrange("(j p) f -> p j f", p=128), stg[:, :, :], reads=[stg])
            else:
                c.dma("sp", O["y_prompt"][2032:2048, :], stg[0:16, 0, :], reads=[stg])
                c.dma("sp", O["y_sample"][:, :], stg[16:32, 0, :], reads=[stg])
        c.flush()
def phase5(c, k, D, O):
    with ExitStack() as st:
        wg1 = c.sb([128, 8, 1536], BF16, "wg1", st)
        wx1 = c.sb([128, 8, 1536], BF16, "wx1", st)
        W = D["w_in_1"].rearrange("(k p) n -> p k n", p=128)
        wx1c = [T(wx1.ap[:, :, 384 * i:384 * (i + 1)], "wx1c%d" % i) for i in range(4)]
        wg1c = [T(wg1.ap[:, :, 384 * i:384 * (i + 1)], "wg1c%d" % i) for i in range(4)]
        for i in range(4):
            load_w_bf16(c, k, wx1c[i], wx1[:, :, 384 * i:384 * (i + 1)], W[:, :, 1536 + 384 * i:1536 + 384 * (i + 1)])
            load_w_bf16(c, k, wg1c[i], wg1[:, :, 384 * i:384 * (i + 1)], W[:, :, 384 * i:384 * (i + 1)])
        wa = c.sb([96, 16, 96], BF16, "wa", st)
        wxx = c.sb([96, 16, 96], BF16, "wxx", st)
        load_w_bf16(c, k, wa, wa[:], D["rnn_w_a"].rearrange("n c d -> c n d"))
        load_w_bf16(c, k, wxx, wxx[:], D["rnn_w_x"].rearrange("n c d -> c n d"))
        wo1 = c.sb([96, 16, 1024], BF16, "wo1", st)
        Wo = D["w_out_1"].rearrange("(n c) m -> c n m", c=96)
        load_w_bf16(c, k, wo1, wo1[:, 0:8, :], Wo[:, 0:8, :])
        load_w_bf16(c, k, wo1, wo1[:, 8:16, :], Wo[:, 8:16, :])
        rv = c.sb([96, 128], F32, "rv", st)
        sl = c.sb([96, 16], F32, "sl", st)
        hist = c.sb([96, 16, 48], F32, "hist", st)
        h0 = c.sb([96, 16, 16], F32, "h0", st)
        carry = c.sb([96, 16, 3], F32, "carry", st)
        hprev = c.sb([96, 16], F32, "hprev", st)
        hs = c.sb([96, 16, 16], F32, "hs", st)
        xtail = c.sb([96, 16, 19], F32, "xtail", st)
        carryT = [T(carry.ap[:, n, :], "carry%d" % n) for n in range(16)]
        hprevT = [T(hprev.ap[:, n:n + 1], "hprev%d" % n) for n in range(16)]
        c.op("dve", lambda e: e.memset(carry[:], 0.0), writes=[carryT])
        c.op("dve", lambda e: e.memset(hprev[:], 0.0), writes=[hprevT])
        stgA = c.sb([128, 1536], F32, "stgA", st)
        r96 = lambda ap: ap.rearrange("(n c) -> n c", c=96)
        c.op("pool", lambda e: e.memset(stgA[:, 0:96], 0.0), writes=[stgA])
        for j in range(4):
            c.dma("sp", stgA[16 * j:16 * j + 16, 0:96], r96(D["rnn_conv_w"][j]), writes=[stgA])
        c.dma("sp", stgA[64:80, 0:96], r96(D["rnn_conv_b"]), writes=[stgA])
        c.dma("sp", stgA[80:96, 0:96], D["rnn_b_a"], writes=[stgA])
        c.dma("sp", stgA[96:112, 0:96], D["rnn_b_x"], writes=[stgA])
        c.dma("sp", stgA[112:128, 0:96], r96(D["rnn_lam"]), writes=[stgA])
        ps = k.psum.next()
        c.op("pe", lambda e: e.transpose(out=ps[0:96, 0:128], in_=stgA[:, 0:96], identity=k.ident[:, :]), reads=[stgA, k.ident], writes=[ps])
        evac(c, k, rv, rv[:, :], ps, ps[0:96, 0:128])
        CW = lambda j, n: rv[:, 16 * j + n:16 * j + n + 1]
        CB = lambda n: rv[:, 64 + n:65 + n]
        BA = lambda n: rv[:, 80 + n:81 + n]
        BX = lambda n: rv[:, 96 + n:97 + n]
        c.op("act", lambda e: e.activation(out=sl[:, :], in_=rv[:, 112:128], func=AF.Exp, scale=-1.0), reads=[rv], writes=[sl])
        c.op("act", lambda e: e.activation(out=sl[:, :], in_=sl[:, :], func=AF.Ln, bias=k.one_t[0:96, 0:1], scale=1.0), reads=[sl, k.one_t], writes=[sl])
        c.op("act", lambda e: e.mul(out=sl[:, :], in_=sl[:, :], mul=-8.0), reads=[sl], writes=[sl])
        c.dma("sp", stgA[0:48, :], D["cache_rglru_conv"].rearrange("s t f -> (s t) f"), writes=[stgA])
        for t0 in (0, 8):
            ps = k.psum.next()
            for i in range(8):
                c.op("pe", lambda e, i=i: e.transpose(out=ps[0:96, i * 48:(i + 1) * 48], in_=stgA[0:48, (t0 + i) * 96:(t0 + i + 1) * 96], identity=k.ident[0:48, 0:48]), reads=[stgA, k.ident], writes=[ps], accum=(i > 0))
            evac(c, k, hist, hist[:, t0:t0 + 8, :], ps, ps[0:96, 0:384].rearrange("p (a b) -> p a b", b=48))
        stgB = stgA
        c.dma("sp", stgB[0:16, :], D["state_rglru"], writes=[stgB])
        ps = k.psum.next()
        for n in range(16):
            c.op("pe", lambda e, n=n: e.transpose(out=ps[0:96, n * 16:(n + 1) * 16], in_=stgB[0:16, n * 96:(n + 1) * 96], identity=k.ident[0:16, 0:16]), reads=[stgB, k.ident], writes=[ps], accum=(n > 0))
        evac(c, k, h0, h0[:, :, :], ps, ps[0:96, 0:256].rearrange("p (a b) -> p a b", b=16))

        k.rspool = Pool_(c, [128, 512], F32, 1, "rs", st)
        xn_p = Pool_(c, [128, 8, 256], BF16, 2, "xn5", st)
        ND = 3
        xrw_p = Pool_(c, [96, 260], F32, 2, "xrw", st)
        xc_p = Pool_(c, [96, 256], F32, 2, "xc", st)
        xcb_p = Pool_(c, [96, 256], BF16, 2, "xcb", st)
        tr_p = Pool_(c, [96, 256], F32, ND, "tr", st)
        ti_p = Pool_(c, [96, 256], F32, 2, "ti", st)
        aa_p = Pool_(c, [96, 256], F32, 2, "aa", st)
        a2_p = Pool_(c, [96, 256], F32, 2, "a2", st)
        hh_p = Pool_(c, [96, 256], F32, 2, "hh", st)
        gg_p = Pool_(c, [96, 256], F32, 2, "gg", st)
        ybuf_p = Pool_(c, [96, 16, 256], BF16, 1, "ybuf", st)
        hb_ = c.sb([96, 32], F32, "hb_", st)
        hsl = c.sb([96, 16], F32, "hsl", st)
        c.op("act", lambda e: e.mul(out=hb_[:, :], in_=rv[:, 80:112], mul=0.5), reads=[rv], writes=[hb_])
        c.op("act", lambda e: e.mul(out=hsl[:, :], in_=sl[:, :], mul=0.5), reads=[sl], writes=[hsl])
        B5 = [(i * 256, (i + 1) * 256) for i in range(8)] + [(2048, 2080)]
        for (c0, c1) in B5:
            w = c1 - c0
            wp = w if w != 32 else 16
            xn = xn_p.next()
            ybuf = ybuf_p.next()
            rmsnorm_block(c, k, c0, w, "norm_mix_1", xn)
            for n in range(16):
                xrw, xc, xcb, tr, ti, aa, a2, hh, gg = (p.next() for p in (xrw_p, xc_p, xcb_p, tr_p, ti_p, aa_p, a2_p, hh_p, gg_p))
                bkA = k.psum.next()
                proj(c, k, bkA[0:96, 0:w], bkA, wx1c[n // 4], lambda kt, n=n: wx1[:, kt, 96 * n:96 * (n + 1)], xn, w)
                c.op("act", lambda e: e.copy(out=xrw[:, 3:3 + w], in_=bkA[0:96, 0:w]), reads=[bkA], writes=[xrw])
                c.op("pool", lambda e, n=n: e.tensor_copy(out=xrw[:, 0:3], in_=carry[:, n, :]), reads=[carryT[n]], writes=[xrw])
                c.op("dve", lambda e, n=n: e.tensor_scalar(out=xc[:, 0:wp], in0=xrw[:, 3:3 + wp], scalar1=CW(3, n), scalar2=CB(n), op0=ALU.mult, op1=ALU.add), reads=[xrw, rv], writes=[xc])
                for j in range(3):
                    c.op("dve", lambda e, n=n, j=j: e.scalar_tensor_tensor(out=xc[:, 0:wp], in0=xrw[:, j:j + wp], scalar=CW(j, n), in1=xc[:, 0:wp], op0=ALU.mult, op1=ALU.add), reads=[xrw, rv, xc], writes=[xc])
                if w != 32:
                    c.op("pool", lambda e, n=n: e.tensor_copy(out=carry[:, n, :], in_=xrw[:, w:w + 3]), reads=[xrw], writes=[carryT[n]])
                else:
                    c.op("pool", lambda e, n=n: e.tensor_copy(out=xtail[:, n, :], in_=xrw[:, 16:35]), reads=[xrw], writes=[xtail])
                    hv = hist[:, n, :].rearrange("p (s t) -> p s t", t=3)
                    c.op("dve", lambda e, n=n: e.tensor_scalar(out=xc[:, 16:32], in0=xrw[:, 19:35], scalar1=CW(3, n), scalar2=CB(n), op0=ALU.mult, op1=ALU.add), reads=[xrw, rv], writes=[xc])
                    for j in range(3):
                        c.op("dve", lambda e, n=n, j=j, hv=hv: e.scalar_tensor_tensor(out=xc[:, 16:32], in0=hv[:, :, j], scalar=CW(j, n), in1=xc[:, 16:32], op0=ALU.mult, op1=ALU.add), reads=[hist, rv, xc], writes=[xc])
                c.op("pool", lambda e: e.tensor_copy(out=xcb[:, 0:w], in_=xc[:, 0:w]), reads=[xc], writes=[xcb])
                bkB = k.psum.next()
                c.op("pe", lambda e, n=n: e.matmul(bkB[0:96, 0:w], wa[:, n, :], xcb[:, 0:w], start=True, stop=True), reads=[wa, xcb], writes=[bkB])
                c.op("pe", lambda e, n=n: e.matmul(bkB[0:96, 256:256 + w], wxx[:, n, :], xcb[:, 0:w], start=True, stop=True), reads=[wxx, xcb], writes=[bkB], accum=True)
                c.op("act", lambda e, n=n: e.activation(out=tr[:, 0:w], in_=bkB[0:96, 0:w], func=AF.Tanh, bias=hb_[:, n:n + 1], scale=0.5), reads=[bkB, hb_], writes=[tr])
                c.op("act", lambda e, n=n: e.activation(out=ti[:, 0:w], in_=bkB[0:96, 256:256 + w], func=AF.Tanh, bias=hb_[:, 16 + n:17 + n], scale=0.5), reads=[bkB, hb_], writes=[ti])
                c.op("act", lambda e, n=n: e.activation(out=aa[:, 0:w], in_=tr[:, 0:w], func=AF.Exp, bias=hsl[:, n:n + 1], scale=hsl[:, n:n + 1]), reads=[tr, hsl], writes=[aa])
                c.op("act", lambda e, n=n: e.activation(out=a2[:, 0:w], in_=tr[:, 0:w], func=AF.Exp, bias=sl[:, n:n + 1], scale=sl[:, n:n + 1]), reads=[tr, sl], writes=[a2])
                c.op("act", lambda e: e.activation(out=a2[:, 0:w], in_=a2[:, 0:w], func=AF.Ln, bias=k.one_t[0:96, 0:1], scale=-1.0), reads=[a2, k.one_t], writes=[a2])
                c.op("act", lambda e: e.activation(out=a2[:, 0:w], in_=a2[:, 0:w], func=AF.Exp, scale=0.5), reads=[a2], writes=[a2])
                c.op("dve", lambda e: e.scalar_tensor_tensor(out=ti[:, 0:w], in0=ti[:, 0:w], scalar=1.0, in1=xc[:, 0:w], op0=ALU.add, op1=ALU.mult), reads=[ti, xc], writes=[ti])
                c.op("dve", lambda e: e.scalar_tensor_tensor(out=ti[:, 0:w], in0=ti[:, 0:w], scalar=0.5, in1=a2[:, 0:w], op0=ALU.mult, op1=ALU.mult), reads=[ti, a2], writes=[ti])
                c.op("dve", lambda e, n=n: e.tensor_tensor_scan(out=hh[:, 0:wp], data0=aa[:, 0:wp], data1=ti[:, 0:wp], initial=hprev[:, n:n + 1], op0=ALU.mult, op1=ALU.add), reads=[aa, ti, hprevT[n]], writes=[hh])
                c.op("pool", lambda e, n=n: e.tensor_copy(out=hprev[:, n:n + 1], in_=hh[:, wp - 1:wp]), reads=[hh], writes=[hprevT[n]])
                if w == 32:
                    c.op("dve", lambda e, n=n: e.tensor_tensor(out=hh[:, 16:32], in0=aa[:, 16:32], in1=h0[:, n, :], op=ALU.mult), reads=[aa, h0], writes=[hh])
                    c.op("dve", lambda e: e.tensor_tensor(out=hh[:, 16:32], in0=hh[:, 16:32], in1=ti[:, 16:32], op=ALU.add), reads=[hh, ti], writes=[hh])
                    c.op("pool", lambda e, n=n: e.tensor_copy(out=hs[:, n, :], in_=hh[:, 16:32]), reads=[hh], writes=[hs])
                bkC = k.psum.next()
                proj(c, k, bkC[0:96, 0:w], bkC, wg1c[n // 4], lambda kt, n=n: wg1[:, kt, 96 * n:96 * (n + 1)], xn, w)
                c.op("act", lambda e: e.activation(out=gg[:, 0:w], in_=bkC[0:96, 0:w], func=GELU), reads=[bkC], writes=[gg])
                c.op("dve", lambda e, n=n: e.tensor_tensor(out=ybuf[:, n, 0:w], in0=hh[:, 0:w], in1=gg[:, 0:w], op=ALU.mult), reads=[hh, gg], writes=[ybuf])
            for m in range(8):
                ps = k.psum.next()
                for n in range(16):
                    c.op("pe", lambda e, n=n, m=m: e.matmul(ps[:, 0:w], wo1[:, n, m * 128:(m + 1) * 128], ybuf[:, n, 0:w], start=(n == 0), stop=(n == 15)), reads=[wo1, ybuf], writes=[ps], accum=(n > 0))
                add_to_x(c, k, m, c0, w, ps)
        ps = k.psum.next()
        c.op("pe", lambda e: e.transpose(out=ps[0:16, 0:96], in_=hprev[:, :], identity=k.ident[0:96, 0:96]), reads=[hprevT, k.ident], writes=[ps])
        ho = c.sb([16, 96], F32, "ho", st)
        evac(c, k, ho, ho[:, :], ps, ps[0:16, 0:96])
        c.dma("sp", O["rglru_prompt"].rearrange("(n c) -> n c", c=96), ho[:, :], reads=[ho])
        stg = store_T(c, k, hs, lambda t: hs[:, t, :], 16, 96, 16, None, st, "hsst", stg=stgA)
        c.dma("sp", O["rglru_sample"], stg[0:16, 0:1536], reads=[stg])
        stg2 = stgA
        store_T(c, k, xtail, lambda t: xtail[:, t, :], 16, 96, 19, None, st, "xtst", stg=stg2)
        c.dma("sp", O["rglru_conv_prompt"], stg2[0:3, :], reads=[stg2])
        c.dma("sp", O["rglru_conv_sample"][:, 2, :], stg2[3:19, :], reads=[stg2])
        c.dma("sp", O["rglru_conv_sample"][:, 0:2, :], D["cache_rglru_conv"][:, 1:3, :])
        c.flush()
IN_SHAPES = {
    "x_prompt": [2048, 1024], "x_sample": [16, 1024], "state_gla": [16, 4, 64, 128],
    "state_s5_re": [16, 32, 64], "state_s5_im": [16, 32, 64], "state_rglru": [16, 1536],
    "cache_rglru_conv": [16, 3, 1536], "cache_ffn_conv": [2, 16, 2, 2816], "meta_tokens": [16, 1024],
    "norm_mix_0": [1024], "w_in_0": [1024, 2064], "w_alpha_0": [16, 256], "b_alpha_0": [256],
    "gla_norm_0": [4, 128], "s5_lam_re": [32, 64], "s5_lam_im": [32, 64], "s5_log_dt": [32],
    "s5_b_re": [32, 64, 16], "s5_b_im": [32, 64, 16], "s5_c_re": [32, 16, 64], "s5_c_im": [32, 16, 64],
    "s5_d": [32, 16], "s5_w_glu": [512, 512], "s5_b_glu": [512], "w_out_0": [1024, 1024],
    "norm_mix_1": [1024], "w_in_1": [1024, 3072], "rnn_conv_w": [4, 1536], "rnn_conv_b": [1536],
    "rnn_w_a": [16, 96, 96], "rnn_b_a": [16, 96], "rnn_w_x": [16, 96, 96], "rnn_b_x": [16, 96],
    "rnn_lam": [1536], "w_out_1": [1536, 1024], "norm_ffn": [2, 1024], "ffn_w_up": [2, 1024, 5632],
    "ffn_conv_w": [2, 3, 2816], "ffn_conv_b": [2, 2816], "ffn_w_down": [2, 2816, 1024], "norm_final": [1024],
}
OUT_SHAPES = {
    "y_prompt": [2048, 1024], "y_sample": [16, 1024], "gla_prompt": [4, 64, 128], "gla_sample": [16, 4, 64, 128],
    "s5_re_prompt": [32, 64], "s5_re_sample": [16, 32, 64], "s5_im_prompt": [32, 64], "s5_im_sample": [16, 32, 64],
    "rglru_prompt": [1536], "rglru_sample": [16, 1536], "rglru_conv_prompt": [3, 1536],
    "rglru_conv_sample": [16, 3, 1536], "ffn_conv_prompt": [2, 2, 2816], "ffn_conv_sample": [2, 16, 2, 2816],
}
SHARDED = {"x_prompt": 0, "x_sample": 0, "state_gla": 0, "state_s5_re": 0, "state_s5_im": 0, "state_rglru": 0,
           "cache_rglru_conv": 0, "cache_ffn_conv": 1}


def build(upto=99, debug=False):
    nc = bass.Bass("TRN2", target_bir_lowering=False)
    D = {n: nc.dram_tensor(n, s, F32, kind="ExternalInput").ap() for n, s in IN_SHAPES.items()}
    O = {n: nc.dram_tensor(n, s, F32, kind="ExternalOutput").ap() for n, s in OUT_SHAPES.items()}
    if debug:
        O["dbg_x"] = nc.dram_tensor("dbg_x", [128, 8, NT], F32, kind="ExternalOutput").ap()
        O["dbg_mixo"] = nc.dram_tensor("dbg_mixo", [128, 4, NT], BF16, kind="ExternalOutput").ap()
        O["dbg_mixy"] = nc.dram_tensor("dbg_mixy", [128, 4, NT], BF16, kind="ExternalOutput").ap()
    with ExitStack() as st:
        c = Ctx(nc, st)
        k = K()
        c.init()
        phase0(c, k, D)
        with ExitStack() as st1:
            k.mixo = c.sb([128, 4, NT], BF16, "mixo", st1)
            if upto >= 1:
                phase1(c, k, D, O)
            k.mixy = c.sb([128, 4, NT], BF16, "mixy", st1)
            if upto >= 2:
                phase2(c, k, D, O)
            if debug:
                c.dma("sp", O["dbg_mixo"], k.mixo[:], reads=[k.mixo])
                c.dma("sp", O["dbg_mixy"], k.mixy[:], reads=[k.mixy])
                c.flush()
            if upto >= 3:
                phase3(c, k, D, O)
        if upto >= 4:
            phase_ffn(c, k, D, O, 0)
        if upto >= 5:
            phase5(c, k, D, O)
        if upto >= 6:
            phase_ffn(c, k, D, O, 1)
        if upto >= 7:
            phase7(c, k, D, O)
        if debug:
            for kt in range(8):
                c.dma("sp", O["dbg_x"][:, kt, :], k.X[kt][:, :], reads=[k.X[kt].ts(0, NT)])
        c.flush(final=True)
    return nc


def make_in_maps(inputs, n=8):
    maps = []
    for i in range(n):
        m = {}
        for name in IN_SHAPES:
            a = np.asarray(inputs[name], dtype=np.float32)
            if name in SHARDED:
                ax = SHARDED[name]
                if name == "x_prompt":
                    a = a[i]
                elif ax == 0:
                    a = a[16 * i:16 * (i + 1)]
                else:
                    a = a[:, 16 * i:16 * (i + 1)]
                if name == "x_sample":
                    a = a.reshape(16, 1024)
            m[name] = np.ascontiguousarray(a)
        maps.append(m)
    return maps


def kernel(**inputs):
    nc = build()
    res = run_bass_kernel_spmd(nc, make_in_maps(inputs), core_ids=list(range(8)))
    R = res.results
    cat = lambda n: np.concatenate([np.asarray(r[n]) for r in R], axis=0)
    stk = lambda n: np.stack([np.asarray(r[n]) for r in R], axis=0)
    y_prompt = stk("y_prompt")
    y_sample = cat("y_sample").reshape(128, 1, 1024)
    out = (y_prompt, y_sample, stk("gla_prompt"), cat("gla_sample"), stk("s5_re_prompt"), cat("s5_re_sample"),
           stk("s5_im_prompt"), cat("s5_im_sample"), stk("rglru_prompt"), cat("rglru_sample"),
           stk("rglru_conv_prompt"), cat("rglru_conv_sample"),
           np.stack([np.asarray(r["ffn_conv_prompt"]) for r in R], axis=1),
           np.concatenate([np.asarray(r["ffn_conv_sample"]) for r in R], axis=1))
    return tuple(np.ascontiguousarray(o.astype(np.float32)) for o in out)
```

```python
import heapq
import numpy as np
import concourse.bass as bass
import concourse.mybir as mybir
from contextlib import ExitStack

F32 = mybir.dt.float32
BF16 = mybir.dt.bfloat16
I32 = mybir.dt.int32
AF = mybir.ActivationFunctionType
ALU = mybir.AluOpType

N_DMA_SEMS = 8
N_SW_SEMS = 88

_ACT_SETS = {
    "Exp": ("lnexp", "exp"), "Tanh": ("gelu_t", "exp", "sig"), "Ln": ("lnexp",), "Sigmoid": ("sig",),
    "Sqrt": ("sqrt",), "Sin": ("trig", "silu"), "Silu": ("silu",), "Gelu_apprx_tanh": ("gelu_t",),
    "Gelu": ("gelu",),
}


class T:
    __slots__ = ("ap", "writer", "readers", "name")

    ALL = []

    def __init__(self, ap, name=""):
        T.ALL.append(self)
        self.ap = ap
        self.writer = None
        self.readers = []
        self.name = name

    def __getitem__(self, idx):
        return self.ap[idx]


class _Rec:
    def __init__(self):
        self.call = None

    def __getattr__(self, name):
        def f(*a, **kw):
            self.call = (name, a, kw)
            return self
        return f


def _nfree(ap):
    n = 1
    for d in tuple(ap.shape)[1:]:
        n *= int(d)
    return n


class _Op:
    __slots__ = ("i", "eng", "fn", "kind", "cost", "lat", "sync", "order", "succ", "npred", "ready",
                 "start", "fin", "tok", "tbl", "pos")


class Ctx:
    ENGS = ("pe", "act", "dve", "pool", "sp")

    def __init__(self, nc, stack):
        self.nc = nc
        self.stack = stack
        self.sems = {}
        for e in self.ENGS:
            self.sems[e] = stack.enter_context(nc.semaphore("s_" + e))
        for i in range(N_DMA_SEMS):
            self.sems["d%d" % i] = stack.enter_context(nc.semaphore("d%d" % i))
        for i in range(N_SW_SEMS):
            self.sems["w%d" % i] = stack.enter_context(nc.semaphore("w%d" % i))
        self.sw_next = 0
        T.ALL.clear()
        self.cnt = {e: 0 for e in self.ENGS}
        self.dval = [0] * N_DMA_SEMS
        self.drr = 0
        self.seen = {e: {} for e in self.ENGS}
        self.ops = []
        self.uid = 0
        self.reorder = True

    def init(self):
        nc = self.nc
        sems = list(self.sems.values())
        with nc.Block() as block:
            @block.sync
            def _(e):
                for s in sems:
                    e.sem_clear(s)

    def sb(self, shape, dtype, name=None, stack=None):
        self.uid += 1
        name = (name or "t") + "_%d" % self.uid
        t = (stack or self.stack).enter_context(self.nc.sbuf_tensor(name, list(shape), dtype))
        return T(t, name)

    def ps(self, shape, dtype=F32, name=None, stack=None):
        self.uid += 1
        name = (name or "p") + "_%d" % self.uid
        t = (stack or self.stack).enter_context(self.nc.psum_tensor(name, list(shape), dtype))
        return T(t, name)

    def _new(self, eng, fn, kind, cost, lat, reads, writes, accum, tbl=None):
        def flat(lst):
            out = []
            for t in lst:
                if isinstance(t, (list, tuple)):
                    out.extend(flat(t))
                else:
                    t = getattr(t, "t", t)
                    if isinstance(t, (list, tuple)):
                        out.extend(flat(t))
                    elif isinstance(t, T):
                        out.append(t)
            return out
        reads, writes = flat(reads), flat(writes)
        o = _Op()
        o.i = len(self.ops)
        o.eng, o.fn, o.kind, o.cost, o.lat, o.tbl = eng, fn, kind, cost, lat, tbl
        sync, order = set(), set()
        for t in reads:
            if t.writer is not None:
                sync.add(t.writer)
        for t in writes:
            if t.writer is not None:
                if accum and self.ops[t.writer].eng == eng:
                    order.add(t.writer)
                else:
                    sync.add(t.writer)
            sync.update(t.readers)
        sync.discard(o.i)
        o.sync, o.order = sync, order - sync
        o.succ = []
        self.ops.append(o)
        for t in reads:
            t.readers.append(o.i)
        for t in writes:
            if accum and t.writer is not None and self.ops[t.writer].eng == eng:
                t.writer = o.i
            else:
                t.writer = o.i
                t.readers = []
        return o

    def op(self, eng, fn, reads=(), writes=(), accum=False):
        rec = _Rec()
        fn(rec)
        name, a, kw = rec.call

        def fn2(e, name=name, a=a, kw=kw):
            return getattr(e, name)(*a, **kw)
        out = kw.get("out", a[0] if a else None)
        tbl = None
        try:
            n = _nfree(out)
        except Exception:
            n = 128
        if eng == "pe":
            rhs = kw.get("rhs", a[2] if len(a) > 2 else None)
            if name == "matmul" and rhs is not None:
                n = _nfree(rhs)
                if rhs.dtype == F32:
                    n *= 4
            cost = 0.03 + max(n, 64) / 2400.0
        elif eng == "dve":
            mult = 2.0 if name in ("tensor_tensor_scan", "scalar_tensor_tensor") else 1.0
            if name == "reciprocal":
                mult = 8.0
            cost = 0.08 + mult * max(n, 64) / 960.0
        elif eng == "act":
            cost = 0.12 + max(n, 64) / 1200.0
            f = kw.get("func")
            if f is not None:
                tbl = _ACT_SETS.get(str(f).split(".")[-1])
        else:
            cost = 0.25 + max(n, 64) / 600.0
        self._new(eng, fn2, "c", cost, cost, reads, writes, accum, tbl)

    def dma(self, q, out, in_, reads=(), writes=(), **kw):
        def fn(e, out=out, in_=in_, kw=kw):
            return e.dma_start(out=out, in_=in_, **kw)
        try:
            nbytes = _nfree(out) * int(tuple(out.shape)[0]) * 4
        except Exception:
            nbytes = 65536
        lat = 2.0 + nbytes / 150000.0
        self._new(q, fn, "sw" if q == "pool" else "hw", 0.1 if q != "pool" else 1.0, lat, reads, writes, False)

    def _schedule(self, ops):
        for o in ops:
            o.npred = 0
            o.ready = 0.0
        for o in ops:
            for p in (o.sync | o.order):
                self.ops[p].succ.append(o.i)
                o.npred += 1
        bl = [0.0] * len(ops)
        for o in reversed(ops):
            m = 0.0
            for sidx in o.succ:
                if bl[sidx] > m:
                    m = bl[sidx]
            bl[o.i] = m + (o.cost if o.kind == "c" else o.lat)
        self._bl = bl
        future = {e: [] for e in self.ENGS}
        now = {e: [] for e in self.ENGS}
        avail = {e: 0.0 for e in self.ENGS}
        cur_tbl = [None]
        for o in ops:
            if o.npred == 0:
                heapq.heappush(future[o.eng], (0.0, o.i))
        order = {e: [] for e in self.ENGS}
        glob = []
        left = len(ops)
        while left:
            best, be = None, None
            for e in self.ENGS:
                f, nw = future[e], now[e]
                while f and f[0][0] <= avail[e]:
                    ii_ = heapq.heappop(f)[1]
                    heapq.heappush(nw, (-bl[ii_], ii_))
                if nw:
                    st = avail[e]
                elif f:
                    st = f[0][0]
                else:
                    continue
                if best is None or st < best:
                    best, be = st, e
            e = be
            if now[e]:
                i = heapq.heappop(now[e])[1]
                if e == "act":
                    o0 = self.ops[i]
                    if o0.tbl is not None and cur_tbl[0] is not None and cur_tbl[0] not in o0.tbl:
                        held = [i]
                        pick = None
                        for _ in range(12):
                            if not now[e]:
                                break
                            j = heapq.heappop(now[e])[1]
                            oj = self.ops[j]
                            if oj.tbl is None or cur_tbl[0] in oj.tbl:
                                pick = j
                                break
                            held.append(j)
                        for h in held:
                            heapq.heappush(now[e], (-bl[h], h))
                        lazy = None
                        if pick is not None:
                            i = pick
                        else:
                            best_r = None
                            for (r_, j_) in future[e]:
                                oj = self.ops[j_]
                                if r_ <= avail[e] + 3.5 and oj.tbl is not None and cur_tbl[0] in oj.tbl:
                                    if best_r is None or r_ < best_r:
                                        best_r, lazy = r_, j_
                            if lazy is not None:
                                future[e].remove((best_r, lazy))
                                heapq.heapify(future[e])
                                i = lazy
                            else:
                                i = heapq.heappop(now[e])[1]
                        if lazy is not None:
                            start = max(avail[e], best_r)
                        else:
                            start = avail[e]
                    else:
                        start = avail[e]
                else:
                    start = avail[e]
            else:
                start, i = heapq.heappop(future[e])
            o = self.ops[i]
            cost = o.cost
            if e == "act" and o.tbl is not None:
                if cur_tbl[0] is None or cur_tbl[0] not in o.tbl:
                    cost += 1.3
                    cur_tbl[0] = o.tbl[0]
            o.start = start
            avail[e] = start + cost
            o.fin = start + (o.lat if o.kind != "c" else cost)
            order[e].append(o)
            glob.append(o)
            left -= 1
            for s in o.succ:
                so = self.ops[s]
                so.npred -= 1
                if o.fin > so.ready:
                    so.ready = o.fin
                if so.npred == 0:
                    heapq.heappush(future[so.eng], (so.ready, so.i))
        return order, glob

    def _inorder(self, ops):
        order = {e: [] for e in self.ENGS}
        for o in ops:
            order[o.eng].append(o)
        return order, list(ops)

    def flush(self, final=False):
        nc = self.nc
        ops = self.ops
        if self.reorder:
            order, glob = self._schedule(ops)
        else:
            order, glob = self._inorder(ops)
        for e in self.ENGS:
            for o in order[e]:
                if o.kind == "c":
                    self.cnt[e] += 1
                    o.tok = (e, self.cnt[e])
        extra = {}
        for o in glob:
            if o.kind == "hw":
                j = self.drr
                self.drr = (self.drr + 1) % N_DMA_SEMS
                if self.dval[j] > 0:
                    extra[o.i] = ("d%d" % j, self.dval[j])
                self.dval[j] += 16
                o.tok = ("d%d" % j, self.dval[j])
            elif o.kind == "sw":
                assert self.sw_next < N_SW_SEMS, "out of SW DMA semaphores"
                o.tok = ("w%d" % self.sw_next, 16)
                self.sw_next += 1
        sems = self.sems
        prog = {}
        for e in self.ENGS:
            seen = self.seen[e]
            lst = []
            for o in order[e]:
                need = {}
                toks = [self.ops[p].tok for p in o.sync]
                if o.i in extra:
                    toks.append(extra[o.i])
                for (k, v) in toks:
                    if seen.get(k, 0) >= v:
                        continue
                    if need.get(k, 0) < v:
                        need[k] = v
                for k, v in need.items():
                    seen[k] = v
                lst.append((list(need.items()), o.fn, o.tok))
            prog[e] = lst
        fin = {}
        for e in self.ENGS:
            if self.cnt[e] > self.seen["sp"].get(e, 0):
                fin[e] = self.cnt[e]
        for j in range(N_DMA_SEMS):
            k = "d%d" % j
            if self.dval[j] > self.seen["sp"].get(k, 0):
                fin[k] = self.dval[j]
        for o in glob:
            if o.kind == "sw" and self.seen["sp"].get(o.tok[0], 0) < 16:
                fin[o.tok[0]] = 16
        for k, v in fin.items():
            self.seen["sp"][k] = v
        prog["sp"].append((list(fin.items()), None, None))
        for t in T.ALL:
            t.writer = None
            t.readers = []
        self.ops = []

        def run(e, lst):
            for waits, fn, tok in lst:
                for k, v in waits:
                    e.wait_ge(sems[k], v)
                if fn is not None:
                    ins = fn(e)
                    ins.then_inc(sems[tok[0]], 16 if tok[0][1:].isdigit() else 1)

        with nc.Block() as block:
            @block.sync
            def _(e):
                run(e, prog["sp"])
            if prog["pe"]:
                @block.tensor
                def _(e):
                    run(e, prog["pe"])
            if prog["act"]:
                @block.scalar
                def _(e):
                    run(e, prog["act"])
            if prog["dve"]:
                @block.vector
                def _(e):
                    run(e, prog["dve"])
            if prog["pool"]:
                @block.gpsimd
                def _(e):
                    run(e, prog["pool"])
from concourse.bass_utils import run_bass_kernel_spmd
EPS = 1e-6
NT = 2080
NP_TOK = 2064
BLOCKS = [(0, 512), (512, 1024), (1024, 1536), (1536, 2048), (2048, 2080)]
IN0 = 2064
D_FF = 2816


class Pool_:
    def __init__(self, c, shape, dtype, n, name, stack, psum=False):
        self.ts = [(c.ps if psum else c.sb)(shape, dtype, name, stack) for _ in range(n)]
        self.i = 0

    def next(self):
        t = self.ts[self.i]
        self.i = (self.i + 1) % len(self.ts)
        return t


class K:
    pass


XB = [(i * 256, (i + 1) * 256) for i in range(8)] + [(2048, NT)]


class XRow:
    def __init__(self, ap, name):
        self.ap = ap
        self.parts = [T(ap[:, a:b], "%s_%d" % (name, i)) for i, (a, b) in enumerate(XB)]

    def __getitem__(self, idx):
        return self.ap[idx]

    def ts(self, c0, c1):
        return [t for t, (a, b) in zip(self.parts, XB) if a < c1 and c0 < b]


def mk_consts(c, k):
    st = c.stack
    k.ident = c.sb([128, 128], F32, "ident")
    k.triU = c.sb([128, 128], F32, "triU")
    k.triS = c.sb([128, 128], F32, "triS")
    k.mask4 = c.sb([128, 4, 128], F32, "mask4")
    k.ones_bf = c.sb([128, 128], BF16, "ones_bf")
    k.eps_t = c.sb([128, 1], F32, "eps_t")
    k.one_t = c.sb([128, 1], F32, "one_t")
    tmp = c.sb([128, 128], F32, "ctmp")
    k.ones_f = tmp
    c.op("pool", lambda e: e.memset(tmp[:], 1.0), writes=[tmp])
    c.op("pool", lambda e: e.affine_select(out=k.ident[:], in_=tmp[:, :], pattern=[[-1, 128]], compare_op=ALU.is_equal, fill=0.0, base=0, channel_multiplier=1), reads=[tmp], writes=[k.ident])
    c.op("pool", lambda e: e.affine_select(out=k.mask4[:], in_=tmp[:, :].unsqueeze(1).to_broadcast([128, 4, 128]), pattern=[[0, 4], [1, 128]], compare_op=ALU.is_ge, fill=0.0, base=0, channel_multiplier=-1), reads=[tmp], writes=[k.mask4])
    tmp2 = c.sb([128, 128], F32, "ctmp2")
    c.op("pool", lambda e: e.memset(tmp2[:], -1.0 / 16.0), writes=[tmp2])
    c.op("pool", lambda e: e.affine_select(out=k.triU[:], in_=tmp2[:], pattern=[[1, 128]], compare_op=ALU.is_ge, fill=0.0, base=0, channel_multiplier=-1), reads=[tmp2], writes=[k.triU])
    c.op("pool", lambda e: e.affine_select(out=k.triS[:], in_=tmp2[:], pattern=[[-1, 128]], compare_op=ALU.is_gt, fill=0.0, base=0, channel_multiplier=1), reads=[tmp2], writes=[k.triS])
    c.op("dve", lambda e: e.memset(k.ones_bf[:], 1.0), writes=[k.ones_bf])
    c.op("dve", lambda e: e.memset(k.eps_t[:], EPS), writes=[k.eps_t])
    c.op("dve", lambda e: e.memset(k.one_t[:], 1.0), writes=[k.one_t])


def evac(c, k, out_t, out_ap, in_t, in_ap):
    k.ev = getattr(k, "ev", 0) + 1
    if k.ev % 2:
        c.op("act", lambda e: e.copy(out=out_ap, in_=in_ap), reads=[in_t], writes=[out_t])
    else:
        c.op("dve", lambda e: e.tensor_copy(out=out_ap, in_=in_ap), reads=[in_t], writes=[out_t])


def load_vec_table(c, k, specs, stack):
    out = {}
    rows = 0
    groups = [[]]
    for name, ap, nrows in specs:
        if rows + nrows > 128:
            groups.append([])
            rows = 0
        groups[-1].append((name, ap, nrows, rows))
        rows += nrows
    vts = [c.sb([128, 128], F32, "vt") for _ in groups]
    for gi, grp in enumerate(groups):
        stg = c.sb([128, 128], F32, "vstg", stack)
        c.op("dve", lambda e, stg=stg: e.memset(stg[:], 0.0), writes=[stg])
        for name, ap, nrows, r0 in grp:
            c.dma("sp", stg[r0:r0 + nrows, :], ap, writes=[stg])
        ps = k.psum.next()
        c.op("pe", lambda e, ps=ps, stg=stg: e.transpose(out=ps[:, 0:128], in_=stg[:, :], identity=k.ident[:, :]), reads=[stg, k.ident], writes=[ps])
        vt = vts[gi]
        evac(c, k, vt, vt[:, :], ps, ps[:, 0:128])
        for name, ap, nrows, r0 in grp:
            out[name] = (vt, r0, nrows)
    return out


def rmsnorm_block(c, k, c0, w, gname, xn, sq_eng="act"):
    gt, gc, _ = k.vt[gname]
    sq = xn
    for kt in range(8):
        if sq_eng == "act":
            c.op("act", lambda e, kt=kt, sq=sq: e.activation(out=sq[:, kt, 0:w], in_=k.X[kt][:, c0:c0 + w], func=AF.Square), reads=[k.X[kt].ts(c0, c0 + w)], writes=[sq])
        else:
            c.op(sq_eng, lambda e, kt=kt, sq=sq: e.tensor_tensor(out=sq[:, kt, 0:w], in0=k.X[kt][:, c0:c0 + w], in1=k.X[kt][:, c0:c0 + w], op=ALU.mult), reads=[k.X[kt].ts(c0, c0 + w)], writes=[sq])
    ps = k.psum.next()
    for kt in range(8):
        c.op("pe", lambda e, kt=kt, sq=sq, ps=ps: e.matmul(ps[:, 0:w], k.ones_bf[:, :], sq[:, kt, 0:w], start=(kt == 0), stop=(kt == 7)), reads=[sq, k.ones_bf], writes=[ps], accum=(kt > 0))
    rs = k.rspool.next()
    c.op("act", lambda e: e.activation(out=rs[:, 0:w], in_=ps[:, 0:w], func=AF.Ln, bias=k.eps_t[:, 0:1], scale=1.0 / 1024.0), reads=[ps, k.eps_t], writes=[rs])
    c.op("act", lambda e: e.activation(out=rs[:, 0:w], in_=rs[:, 0:w], func=AF.Exp, scale=-0.5), reads=[rs], writes=[rs])
    for kt in range(8):
        c.op("dve", lambda e, kt=kt: e.scalar_tensor_tensor(out=xn[:, kt, 0:w], in0=k.X[kt][:, c0:c0 + w], scalar=gt[:, gc + kt:gc + kt + 1], in1=rs[:, 0:w], op0=ALU.mult, op1=ALU.mult), reads=[k.X[kt].ts(c0, c0 + w), gt, rs], writes=[xn])


def proj(c, k, ps_ap, ps_t, w_t, w_ap_fn, xn, w, nk=8, xcols=None):
    lo, hi = xcols if xcols else (0, w)
    for kt in range(nk):
        c.op("pe", lambda e, kt=kt: e.matmul(ps_ap, w_ap_fn(kt), xn[:, kt, lo:hi], start=(kt == 0), stop=(kt == nk - 1)), reads=[w_t, xn], writes=[ps_t], accum=(kt > 0))


def proj_tok(c, k, ps_ap, ps_t, w_t, w_ap_fn, xn, lo, hi, nk=8):
    for kt in range(nk):
        c.op("pe", lambda e, kt=kt: e.matmul(ps_ap, xn[:, kt, lo:hi], w_ap_fn(kt), start=(kt == 0), stop=(kt == nk - 1)), reads=[w_t, xn], writes=[ps_t], accum=(kt > 0))


def phase0(c, k, D):
    nc = c.nc
    mk_consts(c, k)
    k.psum = Pool_(c, [128, 512], F32, 8, "psb", c.stack, psum=True)
    xs = c.stack.enter_context(nc.sbuf_tensor("x_res", [128, 8, NT], F32))
    k.X = [XRow(xs[:, kt, :], "x%d" % kt) for kt in range(8)]
    r128 = lambda ap: ap.rearrange("(k p) -> k p", p=128)
    specs = [("norm_mix_0", r128(D["norm_mix_0"]), 8), ("norm_mix_1", r128(D["norm_mix_1"]), 8),
             ("norm_ffn0", r128(D["norm_ffn"][0]), 8), ("norm_ffn1", r128(D["norm_ffn"][1]), 8),
             ("norm_final", r128(D["norm_final"]), 8),
             ("gla_norm", D["gla_norm_0"], 4), ("s5_d", r128(D["s5_d"].rearrange("g h -> (g h)")), 4),
             ("b_glu", r128(D["s5_b_glu"]), 4),
             ("lam_re", r128(D["s5_lam_re"].rearrange("g p -> (g p)")), 16),
             ("lam_im", r128(D["s5_lam_im"].rearrange("g p -> (g p)")), 16)]
    for l in range(2):
        for j in range(3):
            specs.append(("fcw%d_%d" % (l, j), r128(D["ffn_conv_w"][l, j]), 22))
        specs.append(("fcb%d" % l, r128(D["ffn_conv_b"][l]), 22))
    import os
    BIS = int(os.environ.get("BIS", "9"))
    with ExitStack() as st:
        if BIS >= 2:
            k.vt = load_vec_table(c, k, specs, st)
        stg = Pool_(c, [128, 4, 1024], F32, 2, "xstg", st)
        xp = D["x_prompt"]
        for g in range(min(4, BIS - 2) if BIS >= 3 else 0):
            s = stg.next()
            c.dma("sp", s[:], xp[512 * g:512 * (g + 1), :].rearrange("(j p) f -> p j f", p=128), writes=[s])
            for kt in range(8):
                ps = k.psum.next()
                for j in range(4):
                    c.op("pe", lambda e, ps=ps, s=s, j=j, kt=kt: e.transpose(out=ps[:, j * 128:(j + 1) * 128], in_=s[:, j, kt * 128:(kt + 1) * 128], identity=k.ident[:, :]), reads=[s, k.ident], writes=[ps], accum=(j > 0))
                evac(c, k, k.X[kt].ts(16 + 512 * g, 16 + 512 * (g + 1)), k.X[kt][:, 16 + 512 * g:16 + 512 * (g + 1)], ps, ps[:, :])
        s = stg.next()
        c.dma("sp", s[0:16, 0, :], D["meta_tokens"], writes=[s])
        c.dma("sp", s[0:16, 1, :], D["x_sample"], writes=[s])
        for kt in range(8):
            ps = k.psum.next()
            for j in range(2):
                c.op("pe", lambda e, ps=ps, s=s, j=j, kt=kt: e.transpose(out=ps[:, j * 16:(j + 1) * 16], in_=s[0:16, j, kt * 128:(kt + 1) * 128], identity=k.ident[0:16, 0:16]), reads=[s, k.ident], writes=[ps], accum=(j > 0))
            evac(c, k, k.X[kt].ts(0, 16), k.X[kt][:, 0:16], ps, ps[:, 0:16])
            evac(c, k, k.X[kt].ts(NP_TOK, NT), k.X[kt][:, NP_TOK:NT], ps, ps[:, 16:32])
        c.flush()
def load_w_bf16(c, k, dst_t, dst_ap, src_ap):
    c.dma("pool", dst_ap, src_ap, writes=[dst_t])


def gla_out_stage(c, k, o_ps, cw, gs, gs_lo, mixo, g_lo):
    gnt, gnc, _ = k.vt["gla_norm"]
    sq = k.g_sq.next()
    c.op("act", lambda e: e.activation(out=sq[:, :, 0:cw], in_=o_ps[:, 0:4 * 128].rearrange("p (h i) -> p h i", h=4)[:, :, 0:cw], func=AF.Square), reads=[o_ps], writes=[sq])
    ms = k.psum.next()
    for h in range(4):
        c.op("pe", lambda e, h=h: e.matmul(ms[:, h * 128:h * 128 + cw], k.ones_bf[:, :], sq[:, h, 0:cw], start=True, stop=True), reads=[sq, k.ones_bf], writes=[ms], accum=(h > 0))
    rs = k.g_rs.next()
    msv = ms[:, 0:512].rearrange("p (h i) -> p h i", h=4)[:, :, 0:cw]
    c.op("act", lambda e: e.activation(out=rs[:, :, 0:cw], in_=msv, func=AF.Ln, bias=k.eps_t[:, 0:1], scale=1.0 / 128.0), reads=[ms, k.eps_t], writes=[rs])
    c.op("act", lambda e: e.activation(out=rs[:, :, 0:cw], in_=rs[:, :, 0:cw], func=AF.Exp, scale=-0.5), reads=[rs], writes=[rs])
    on = k.g_on.next()
    c.op("dve", lambda e: e.tensor_tensor(out=on[:, :, 0:cw], in0=o_ps[:, 0:512].rearrange("p (h i) -> p h i", h=4)[:, :, 0:cw], in1=rs[:, :, 0:cw], op=ALU.mult), reads=[o_ps, rs], writes=[on])
    for h in range(4):
        c.op("dve", lambda e, h=h: e.scalar_tensor_tensor(out=mixo[:, h, g_lo:g_lo + cw], in0=on[:, h, 0:cw], scalar=gnt[:, gnc + h:gnc + h + 1], in1=gs[:, h, gs_lo:gs_lo + cw], op0=ALU.mult, op1=ALU.mult), reads=[on, gnt, gs], writes=[mixo])


def phase1(c, k, D, O):
    nc = c.nc
    with ExitStack() as st:
        wqk = c.sb([128, 8, 512], BF16, "wqk", st)
        wv = c.sb([128, 8, 512], BF16, "wv", st)
        wg = c.sb([128, 8, 512], BF16, "wg", st)
        wl = c.sb([128, 8, 16], BF16, "wl", st)
        W = D["w_in_0"].rearrange("(k p) n -> p k n", p=128)
        load_w_bf16(c, k, wqk, wqk[:], W[:, :, 0:512])
        load_w_bf16(c, k, wl, wl[:], W[:, :, 1536:1552])
        load_w_bf16(c, k, wv, wv[:], W[:, :, 512:1024])
        load_w_bf16(c, k, wg, wg[:], W[:, :, 1024:1536])
        wal = c.sb([17, 256], F32, "wal", st)
        c.dma("sp", wal[0:16, :], D["w_alpha_0"], writes=[wal])
        c.dma("sp", wal[16:17, :], D["b_alpha_0"].rearrange("(o n) -> o n", o=1), writes=[wal])
        S0 = c.sb([128, 16, 2, 128], F32, "S0", st)
        for hp in range(2):
            c.dma("sp", S0[:, :, hp, :], D["state_gla"][:, 2 * hp:2 * hp + 2].rearrange("s h d e -> (h d) s e"), writes=[S0])
        k.rspool = Pool_(c, [128, 512], F32, 1, "rs", st)
        xnp = Pool_(c, [128, 8, 512], BF16, 2, "xn", st)
        qT = Pool_(c, [128, 2, 512], F32, 2, "qT", st)
        kT = Pool_(c, [128, 2, 512], F32, 2, "kT", st)
        gsp = Pool_(c, [128, 4, 512], BF16, 2, "gs", st)
        lrT = c.sb([32, 512], F32, "lrT", st)
        c.op("dve", lambda e: e.memset(lrT[:], 1.0), writes=[lrT])
        sp_p = Pool_(c, [128, 256], F32, 2, "sp", st)
        Eq_p = Pool_(c, [128, 2, 128], F32, 2, "Eq", st)
        Ek_p = Pool_(c, [128, 2, 128], F32, 2, "Ek", st)
        Er_p = Pool_(c, [128, 256], F32, 2, "Er", st)
        qm_p = [[Pool_(c, [128, 128], BF16, 3, "qm", st) for h2 in range(2)] for hp in range(2)]
        for hp in range(2):
            for h2 in range(2):
                for t in qm_p[hp][h2].ts:
                    c.op("pool", lambda e, t=t: e.memset(t[:], 0.0), writes=[t])
        kt_p = [Pool_(c, [128, 128], BF16, 3, "ktl", st) for hp in range(2)]
        kh_p = Pool_(c, [128, 256], BF16, 3, "kh", st)
        vb_p = Pool_(c, [128, 512], BF16, 2, "vb", st)
        at_p = Pool_(c, [128, 4, 128], BF16, 3, "at", st)
        k.g_sq = Pool_(c, [128, 4, 128], BF16, 3, "gsq", st)
        k.g_rs = Pool_(c, [128, 4, 128], F32, 2, "grs", st)
        k.g_on = Pool_(c, [128, 4, 128], F32, 2, "gon", st)
        S = [c.sb([128, 128], F32, "S", st) for hp in range(2)]
        Sb = [c.sb([128, 128], BF16, "Sb", st) for hp in range(2)]
        for hp in range(2):
            c.op("dve", lambda e, hp=hp: e.memset(S[hp][:], 0.0), writes=[S[hp]])
            c.op("dve", lambda e, hp=hp: e.memset(Sb[hp][:], 0.0), writes=[Sb[hp]])
        mixo = k.mixo

        for (c0, w) in [(b0, b1 - b0) for b0, b1 in BLOCKS]:
            xn = xnp.next()
            rmsnorm_block(c, k, c0, w, "norm_mix_0", xn)
            q_sb, k_sb, gs = qT.next(), kT.next(), gsp.next()
            for hp in range(2):
                ps = k.psum.next()
                proj(c, k, ps[:, 0:w], ps, wqk, lambda kt, hp=hp: wqk[:, kt, hp * 128:(hp + 1) * 128], xn, w)
                evac(c, k, q_sb, q_sb[:, hp, 0:w], ps, ps[:, 0:w])
                ps = k.psum.next()
                proj(c, k, ps[:, 0:w], ps, wqk, lambda kt, hp=hp: wqk[:, kt, 256 + hp * 128:256 + (hp + 1) * 128], xn, w)
                evac(c, k, k_sb, k_sb[:, hp, 0:w], ps, ps[:, 0:w])
            for m in range(4):
                ps = k.psum.next()
                proj(c, k, ps[:, 0:w], ps, wg, lambda kt, m=m: wg[:, kt, m * 128:(m + 1) * 128], xn, w)
                c.op("act", lambda e, ps=ps, m=m: e.activation(out=gs[:, m, 0:w], in_=ps[:, 0:w], func=AF.Silu), reads=[ps], writes=[gs])
            ps = k.psum.next()
            proj(c, k, ps[0:16, 0:w], ps, wl, lambda kt: wl[:, kt, 0:16], xn, w)
            evac(c, k, lrT, lrT[0:16, 0:w], ps, ps[0:16, 0:w])

            chunks = [(i * 128, 128) for i in range(w // 128)] if w == 512 else [(0, 16)]
            for (cc, cw) in chunks:
                ktok = k.psum.next()
                proj_tok(c, k, ktok[0:cw, 0:256], ktok, wqk, lambda kt: wqk[:, kt, 256:512], xn, cc, cc + cw)
                vtok = k.psum.next()
                proj_tok(c, k, vtok[0:cw, 0:512], vtok, wv, lambda kt: wv[:, kt, :], xn, cc, cc + cw)
                zps = k.psum.next()
                c.op("pe", lambda e: e.matmul(zps[0:cw, 0:256], lrT[0:17, cc:cc + cw], wal[0:17, :], start=True, stop=True), reads=[lrT, wal], writes=[zps])
                sp = sp_p.next()
                c.op("act", lambda e: e.activation(out=sp[0:cw, :], in_=zps[0:cw, 0:256], func=AF.Exp, scale=-1.0), reads=[zps], writes=[sp])
                c.op("act", lambda e: e.activation(out=sp[0:cw, :], in_=sp[0:cw, :], func=AF.Ln, bias=k.one_t[0:cw, 0:1], scale=1.0), reads=[sp, k.one_t], writes=[sp])
                bf = k.psum.next()
                for hp in range(2):
                    c.op("pe", lambda e, hp=hp: e.matmul(bf[:, hp * 128:hp * 128 + cw], sp[0:cw, hp * 128:(hp + 1) * 128], k.triU[0:cw, 0:cw], start=True, stop=True), reads=[sp, k.triU], writes=[bf], accum=(hp > 0))
                brev = k.psum.next()
                c.op("pe", lambda e: e.matmul(brev[0:cw, 0:256], k.triS[0:cw, 0:cw], sp[0:cw, :], start=True, stop=True), reads=[sp, k.triS], writes=[brev])
                Eq, Ek, Er = Eq_p.next(), Ek_p.next(), Er_p.next()
                bfv = bf[:, 0:256].rearrange("p (h i) -> p h i", h=2)[:, :, 0:cw]
                c.op("act", lambda e: e.activation(out=Eq[:, :, 0:cw], in_=bfv, func=AF.Exp, scale=1.0), reads=[bf], writes=[Eq])
                c.op("act", lambda e: e.activation(out=Ek[:, :, 0:cw], in_=bfv, func=AF.Exp, scale=-1.0), reads=[bf], writes=[Ek])
                c.op("act", lambda e: e.activation(out=Er[0:cw, :], in_=brev[0:cw, 0:256], func=AF.Exp, scale=1.0), reads=[brev], writes=[Er])
                qm = [[qm_p[hp][h2].next() for h2 in range(2)] for hp in range(2)]
                ktl = [kt_p[hp].next() for hp in range(2)]
                for hp in range(2):
                    for h2 in range(2):
                        lo = 64 * h2
                        c.op("dve", lambda e, hp=hp, h2=h2, lo=lo: e.scalar_tensor_tensor(out=qm[hp][h2][lo:lo + 64, 0:cw], in0=q_sb[lo:lo + 64, hp, cc:cc + cw], scalar=0.125, in1=Eq[lo:lo + 64, hp, 0:cw], op0=ALU.mult, op1=ALU.mult), reads=[q_sb, Eq], writes=[qm[hp][h2]])
                    c.op("pool", lambda e, hp=hp: e.tensor_tensor(out=ktl[hp][:, 0:cw], in0=k_sb[:, hp, cc:cc + cw], in1=Ek[:, hp, 0:cw], op=ALU.mult), reads=[k_sb, Ek], writes=[ktl[hp]])
                kh = kh_p.next()
                c.op("dve", lambda e: e.tensor_tensor(out=kh[0:cw, :], in0=ktok[0:cw, 0:256], in1=Er[0:cw, :], op=ALU.mult), reads=[ktok, Er], writes=[kh])
                vb = vb_p.next()
                c.op("act", lambda e: e.copy(out=vb[0:cw, :], in_=vtok[0:cw, 0:512]), reads=[vtok], writes=[vb])
                atp = k.psum.next()
                for h in range(4):
                    hp, h2 = h // 2, h % 2
                    c.op("pe", lambda e, h=h, hp=hp, h2=h2: e.matmul(atp[0:cw, h * 128:h * 128 + cw], ktl[hp][:, 0:cw], qm[hp][h2][:, 0:cw], start=True, stop=True), reads=[ktl[hp], qm[hp][h2]], writes=[atp], accum=(h > 0))
                at = at_p.next()
                c.op("dve", lambda e: e.tensor_tensor(out=at[0:cw, :, 0:cw], in0=atp[0:cw, 0:512].rearrange("p (h i) -> p h i", h=4)[:, :, 0:cw], in1=k.mask4[0:cw, :, 0:cw], op=ALU.mult), reads=[atp, k.mask4], writes=[at])
                ops_ = k.psum.next()
                for h in range(4):
                    hp, h2 = h // 2, h % 2
                    c.op("pe", lambda e, h=h: e.matmul(ops_[:, h * 128:h * 128 + cw], vb[0:cw, h * 128:(h + 1) * 128], at[0:cw, h, 0:cw], start=True, stop=False), reads=[vb, at], writes=[ops_], accum=(h > 0))
                    c.op("pe", lambda e, h=h, hp=hp, h2=h2: e.matmul(ops_[:, h * 128:h * 128 + cw], Sb[hp][:, :], qm[hp][h2][:, 0:cw], start=False, stop=True), reads=[Sb[hp], qm[hp][h2]], writes=[ops_], accum=True)
                dS = k.psum.next()
                for hp in range(2):
                    c.op("pe", lambda e, hp=hp: e.matmul(dS[:, hp * 256:(hp + 1) * 256], kh[0:cw, hp * 128:(hp + 1) * 128], vb[0:cw, hp * 256:(hp + 1) * 256], start=True, stop=True), reads=[kh, vb], writes=[dS], accum=(hp > 0))
                for hp in range(2):
                    for h2 in range(2):
                        lo = 64 * h2
                        c.op("dve", lambda e, hp=hp, h2=h2, lo=lo: e.scalar_tensor_tensor(out=S[hp][lo:lo + 64, :], in0=S[hp][lo:lo + 64, :], scalar=Eq[lo:lo + 64, hp, cw - 1:cw], in1=dS[lo:lo + 64, hp * 256 + h2 * 128:hp * 256 + (h2 + 1) * 128], op0=ALU.mult, op1=ALU.add), reads=[S[hp], Eq, dS], writes=[S[hp]])
                    c.op("pool", lambda e, hp=hp: e.tensor_copy(out=Sb[hp][:], in_=S[hp][:]), reads=[S[hp]], writes=[Sb[hp]])
                gla_out_stage(c, k, ops_, cw, gs, cc, mixo, c0 + cc)

            if w == 32:
                for hp in range(2):
                    c.dma("sp", O["gla_prompt"][2 * hp:2 * hp + 2].rearrange("h d e -> (h d) e"), S[hp][:], reads=[S[hp]])
                selp = Pool_(c, [16, 128], F32, 2, "sel", st)
                vtok = k.psum.next()
                proj_tok(c, k, vtok[0:16, 0:512], vtok, wv, lambda kt: wv[:, kt, :], xn, 16, 32)
                vs = c.sb([16, 512], F32, "vs", st)
                evac(c, k, vs, vs[:, :], vtok, vtok[0:16, 0:512])
                zf = k.psum.next()
                for hp in range(2):
                    c.op("pe", lambda e, hp=hp: e.matmul(zf[:, hp * 16:(hp + 1) * 16], wal[0:17, hp * 128:(hp + 1) * 128], lrT[0:17, 16:32], start=True, stop=True), reads=[wal, lrT], writes=[zf], accum=(hp > 0))
                ea = c.sb([128, 32], F32, "ea", st)
                c.op("act", lambda e: e.activation(out=ea[:, :], in_=zf[:, 0:32], func=AF.Exp, scale=-1.0), reads=[zf], writes=[ea])
                c.op("act", lambda e: e.activation(out=ea[:, :], in_=ea[:, :], func=AF.Ln, bias=k.one_t[:, 0:1], scale=1.0), reads=[ea, k.one_t], writes=[ea])
                c.op("act", lambda e: e.activation(out=ea[:, :], in_=ea[:, :], func=AF.Exp, scale=-1.0 / 16.0), reads=[ea], writes=[ea])
                qs = [[c.sb([128, 16], F32, "qs", st) for h2 in range(2)] for hp in range(2)]
                for hp in range(2):
                    for h2 in range(2):
                        lo = 64 * h2
                        c.op("pool", lambda e, hp=hp, h2=h2: e.memset(qs[hp][h2][:], 0.0), writes=[qs[hp][h2]])
                        c.op("act", lambda e, hp=hp, h2=h2, lo=lo: e.mul(out=qs[hp][h2][lo:lo + 64, :], in_=q_sb[lo:lo + 64, hp, 16:32], mul=0.125), reads=[q_sb], writes=[qs[hp][h2]])
                Sn = S0
                os_ = k.psum.next()
                for s in range(16):
                    vbp = k.psum.next()
                    sel = selp.next()
                    c.op("pool", lambda e, s=s, sel=sel: e.affine_select(out=sel[:, :], in_=k.ones_f[0:16, :], pattern=[[0, 128]], compare_op=ALU.is_equal, fill=0.0, base=-s, channel_multiplier=1), reads=[k.ones_f], writes=[sel])
                    c.op("pe", lambda e, s=s, vbp=vbp, sel=sel: e.matmul(vbp[:, 0:512], sel[0:16, :], vs[0:16, :], start=True, stop=True), reads=[sel, vs], writes=[vbp])
                    for hp in range(2):
                        c.op("act", lambda e, s=s, hp=hp: e.activation(out=Sn[:, s, hp, :], in_=S0[:, s, hp, :], func=AF.Copy, scale=ea[:, hp * 16 + s:hp * 16 + s + 1]), reads=[S0, ea], writes=[Sn])
                        for h2 in range(2):
                            lo = 64 * h2
                            h = 2 * hp + h2
                            c.op("dve", lambda e, s=s, hp=hp, lo=lo, h=h, vbp=vbp: e.scalar_tensor_tensor(out=Sn[lo:lo + 64, s, hp, :], in0=vbp[lo:lo + 64, h * 128:(h + 1) * 128], scalar=k_sb[lo:lo + 64, hp, 16 + s:17 + s], in1=Sn[lo:lo + 64, s, hp, :], op0=ALU.mult, op1=ALU.add), reads=[vbp, k_sb, Sn], writes=[Sn])
                for s in range(16):
                    for h in range(4):
                        hp, h2 = h // 2, h % 2
                        c.op("pe", lambda e, s=s, h=h, hp=hp, h2=h2: e.matmul(os_[:, h * 128 + s:h * 128 + s + 1], Sn[:, s, hp, :], qs[hp][h2][:, s:s + 1], start=True, stop=True), reads=[Sn, qs[hp][h2]], writes=[os_], accum=(s + h > 0))
                for hp in range(2):
                    c.dma("sp", O["gla_sample"][:, 2 * hp:2 * hp + 2].rearrange("s h d e -> (h d) s e"), Sn[:, :, hp, :], reads=[Sn])
                gla_out_stage(c, k, os_, 16, gs, 16, mixo, c0 + 16)
        c.flush()
import math
TWO_PI = 2.0 * math.pi


def sin_of(c, k, out_t, out_ap, arg_ap, arg_t, shape, st, shift=0.0):
    key = tuple(shape)
    if key not in k.sr_cache:
        k.sr_cache[key] = (c.sb(shape, F32, "sr_t", st), c.sb(shape, I32, "sr_i", st), c.sb(shape, F32, "sr_r", st))
    t, ki, r = k.sr_cache[key]
    full = tuple(slice(None) for _ in shape)
    c.op("dve", lambda e: e.tensor_scalar(out=t[full], in0=arg_ap, scalar1=shift, scalar2=1.0 / TWO_PI, op0=ALU.add, op1=ALU.mult), reads=[arg_t], writes=[t])
    c.op("dve", lambda e: e.tensor_copy(out=ki[full], in_=t[full]), reads=[t], writes=[ki])
    c.op("dve", lambda e: e.tensor_copy(out=t[full], in_=ki[full]), reads=[ki], writes=[t])
    c.op("dve", lambda e: e.tensor_scalar(out=r[full], in0=arg_ap, scalar1=shift, scalar2=None, op0=ALU.add), reads=[arg_t], writes=[r])
    c.op("dve", lambda e: e.scalar_tensor_tensor(out=r[full], in0=t[full], scalar=-TWO_PI, in1=r[full], op0=ALU.mult, op1=ALU.add), reads=[t, r], writes=[r])
    c.op("dve", lambda e: e.tensor_scalar(out=t[full], in0=r[full], scalar1=math.pi, scalar2=None, op0=ALU.is_gt), reads=[r], writes=[t])
    c.op("dve", lambda e: e.scalar_tensor_tensor(out=r[full], in0=t[full], scalar=-TWO_PI, in1=r[full], op0=ALU.mult, op1=ALU.add), reads=[t, r], writes=[r])
    c.op("dve", lambda e: e.tensor_scalar(out=t[full], in0=r[full], scalar1=-math.pi, scalar2=None, op0=ALU.is_lt), reads=[r], writes=[t])
    c.op("dve", lambda e: e.scalar_tensor_tensor(out=r[full], in0=t[full], scalar=TWO_PI, in1=r[full], op0=ALU.mult, op1=ALU.add), reads=[t, r], writes=[r])
    c.op("dve", lambda e: e.tensor_scalar(out=r[full], in0=r[full], scalar1=math.pi, scalar2=-math.pi, op0=ALU.min, op1=ALU.max), reads=[r], writes=[r])
    c.op("act", lambda e: e.activation(out=out_ap, in_=r[full], func=AF.Sin), reads=[r], writes=[out_t])


class _StopPhase(Exception):
    pass


def phase2(c, k, D, O):
    try:
        _phase2(c, k, D, O)
    except _StopPhase:
        pass


def _phase2(c, k, D, O):
    import os
    BIS = int(os.environ.get("S5BIS", "0"))

    def chk(n):
        if BIS == n:
            c.flush()
            return True
        return False
    SEG = 128
    with ExitStack() as st:
        CT = c.sb([128, 16, SEG], F32, "CT", st)
        ST = c.sb([128, 16, SEG], F32, "ST", st)
        WB = c.sb([128, 4, 2, 128], F32, "WB", st)
        WB3 = c.sb([128, 4, 2, 128], F32, "WB3", st)
        WC16 = c.sb([128, 16, 2, 128], BF16, "WC16", st)
        WBa = c.sb([128, 4, 2, 128], F32, "WBa", st)
        WBa3 = c.sb([128, 4, 2, 128], F32, "WBa3", st)
        WCa = c.sb([128, 16, 2, 128], BF16, "WCa", st)
        G0t = c.sb([128, 4, 128], F32, "G0t", st)
        mag2 = c.sb([128, 16], F32, "mag2", st)
        c2 = c.sb([128, 16], F32, "c2", st)
        s2t = c.sb([128, 16], F32, "s2t", st)
        wu = c.sb([128, 8, 512], BF16, "wu", st)
        load_w_bf16(c, k, wu, wu[:], D["w_in_0"].rearrange("(k p) n -> p k n", p=128)[:, :, 1552:2064])
        k.rspool = Pool_(c, [128, 512], F32, 1, "rs", st)
        xn = c.sb([128, 8, 2 * SEG], BF16, "xn2", st)
        useg = c.sb([128, 4, 2 * SEG], F32, "useg", st)
        wglu = c.sb([128, 4, 512], BF16, "wglu", st)
        load_w_bf16(c, k, wglu, wglu[:], D["s5_w_glu"].rearrange("(k p) n -> p k n", p=128))
        mag = c.sb([128, 16], F32, "mag", st)
        cth = c.sb([128, 16], F32, "cth", st)
        sth = c.sb([128, 16], F32, "sth", st)
        abr = c.sb([128, 16], F32, "abr", st)
        abi = c.sb([128, 16], F32, "abi", st)
        xpr = c.sb([128, 16], F32, "xpr", st)
        xpi = c.sb([128, 16], F32, "xpi", st)
        xprT = [T(xpr.ap[:, a:a + 1], "xpr%d" % a) for a in range(16)]
        xpiT = [T(xpi.ap[:, a:a + 1], "xpi%d" % a) for a in range(16)]
        lrt, lrc, _ = k.vt["lam_re"]
        lit, lic, _ = k.vt["lam_im"]
        lr = lrt[:, lrc:lrc + 16]
        li = lit[:, lic:lic + 16]
        A2 = (slice(None), slice(None))
        with ExitStack() as s2:
            k.sr_cache = {}
            ldt = c.sb([128, 16], F32, "ldt", s2)
            for g2 in range(2):
                c.dma("sp", ldt[64 * g2:64 * g2 + 64, :], D["s5_log_dt"].rearrange("(a g) -> g a", g=2)[g2:g2 + 1, :].to_broadcast([64, 16]), writes=[ldt], allow_slow_non_contiguous=True)
            dt = c.sb([128, 16], F32, "dt", s2)
            th = c.sb([128, 16], F32, "th", s2)
            t1 = c.sb([128, 16], F32, "t1", s2)
            t2 = c.sb([128, 16], F32, "t2", s2)
            fre = c.sb([128, 16], F32, "fre", s2)
            fim = c.sb([128, 16], F32, "fim", s2)
            c.op("act", lambda e: e.activation(out=dt[A2], in_=ldt[A2], func=AF.Exp), reads=[ldt], writes=[dt])
            c.op("dve", lambda e: e.tensor_tensor(out=t1[A2], in0=lr, in1=dt[A2], op=ALU.mult), reads=[lrt, dt], writes=[t1])
            c.op("act", lambda e: e.activation(out=mag[A2], in_=t1[A2], func=AF.Exp), reads=[t1], writes=[mag])
            c.op("dve", lambda e: e.tensor_tensor(out=th[A2], in0=li, in1=dt[A2], op=ALU.mult), reads=[lit, dt], writes=[th])
            if chk(1):
                return
            sin_of(c, k, sth, sth[A2], th[A2], th, [128, 16], s2)
            sin_of(c, k, cth, cth[A2], th[A2], th, [128, 16], s2, shift=math.pi / 2)
            if chk(2):
                return
            c.op("dve", lambda e: e.tensor_tensor(out=abr[A2], in0=mag[A2], in1=cth[A2], op=ALU.mult), reads=[mag, cth], writes=[abr])
            c.op("dve", lambda e: e.tensor_tensor(out=abi[A2], in0=mag[A2], in1=sth[A2], op=ALU.mult), reads=[mag, sth], writes=[abi])
            nr = c.sb([128, 16], F32, "nr", s2)
            den = c.sb([128, 16], F32, "den", s2)
            c.op("dve", lambda e: e.tensor_scalar(out=nr[A2], in0=abr[A2], scalar1=-1.0, scalar2=None, op0=ALU.add), reads=[abr], writes=[nr])
            c.op("dve", lambda e: e.tensor_tensor(out=t1[A2], in0=lr, in1=lr, op=ALU.mult), reads=[lrt], writes=[t1])
            c.op("dve", lambda e: e.tensor_tensor(out=t2[A2], in0=li, in1=li, op=ALU.mult), reads=[lit], writes=[t2])
            c.op("dve", lambda e: e.tensor_tensor(out=den[A2], in0=t1[A2], in1=t2[A2], op=ALU.add), reads=[t1, t2], writes=[den])
            c.op("dve", lambda e: e.reciprocal(out=den[A2], in_=den[A2]), reads=[den], writes=[den])
            c.op("dve", lambda e: e.tensor_tensor(out=t1[A2], in0=nr[A2], in1=lr, op=ALU.mult), reads=[nr, lrt], writes=[t1])
            c.op("dve", lambda e: e.tensor_tensor(out=t2[A2], in0=abi[A2], in1=li, op=ALU.mult), reads=[abi, lit], writes=[t2])
            c.op("dve", lambda e: e.tensor_tensor(out=t1[A2], in0=t1[A2], in1=t2[A2], op=ALU.add), reads=[t1, t2], writes=[t1])
            c.op("dve", lambda e: e.tensor_tensor(out=fre[A2], in0=t1[A2], in1=den[A2], op=ALU.mult), reads=[t1, den], writes=[fre])
            c.op("dve", lambda e: e.tensor_tensor(out=t1[A2], in0=abi[A2], in1=lr, op=ALU.mult), reads=[abi, lrt], writes=[t1])
            c.op("dve", lambda e: e.tensor_tensor(out=t2[A2], in0=nr[A2], in1=li, op=ALU.mult), reads=[nr, lit], writes=[t2])
            c.op("dve", lambda e: e.tensor_tensor(out=t1[A2], in0=t1[A2], in1=t2[A2], op=ALU.subtract), reads=[t1, t2], writes=[t1])
            c.op("dve", lambda e: e.tensor_tensor(out=fim[A2], in0=t1[A2], in1=den[A2], op=ALU.mult), reads=[t1, den], writes=[fim])
            if chk(3):
                return
            sW = ExitStack()
            WC = c.sb([128, 16, 2, 128], F32, "WC", sW)
            BBe = [c.sb([128, 16, 32], F32, "BBe", sW) for _ in range(2)]
            s3 = ExitStack()
            bre = c.sb([128, 16, 16], F32, "bre", s3)
            bim = c.sb([128, 16, 16], F32, "bim", s3)
            c.dma("sp", bre[:], D["s5_b_re"].rearrange("(a g) p h -> (g p) a h", g=2), writes=[bre])
            c.dma("sp", bim[:], D["s5_b_im"].rearrange("(a g) p h -> (g p) a h", g=2), writes=[bim])
            u1 = c.sb([128, 16, 16], F32, "u1", s3)
            u2 = c.sb([128, 16, 16], F32, "u2", s3)
            A3 = (slice(None), slice(None), slice(None))
            frb = fre[A2].unsqueeze(2).to_broadcast([128, 16, 16])
            fib = fim[A2].unsqueeze(2).to_broadcast([128, 16, 16])
            for ri in range(2):
                c.op("pool", lambda e, ri=ri: e.memset(BBe[ri][A3], 0.0), writes=[BBe[ri]])
            c.op("dve", lambda e: e.tensor_tensor(out=u1[A3], in0=bre[A3], in1=frb, op=ALU.mult), reads=[bre, fre], writes=[u1])
            c.op("dve", lambda e: e.tensor_tensor(out=u2[A3], in0=bim[A3], in1=fib, op=ALU.mult), reads=[bim, fim], writes=[u2])
            for g2 in range(2):
                lo = 64 * g2
                c.op("dve", lambda e, lo=lo, g2=g2: e.tensor_tensor(out=BBe[0][lo:lo + 64, :, 16 * g2:16 * g2 + 16], in0=u1[lo:lo + 64], in1=u2[lo:lo + 64], op=ALU.subtract), reads=[u1, u2], writes=[BBe[0]])
            c.op("dve", lambda e: e.tensor_tensor(out=u1[A3], in0=bim[A3], in1=frb, op=ALU.mult), reads=[bim, fre], writes=[u1])
            c.op("dve", lambda e: e.tensor_tensor(out=u2[A3], in0=bre[A3], in1=fib, op=ALU.mult), reads=[bre, fim], writes=[u2])
            for g2 in range(2):
                lo = 64 * g2
                c.op("dve", lambda e, lo=lo, g2=g2: e.tensor_tensor(out=BBe[1][lo:lo + 64, :, 16 * g2:16 * g2 + 16], in0=u1[lo:lo + 64], in1=u2[lo:lo + 64], op=ALU.add), reads=[u1, u2], writes=[BBe[1]])
            c.flush()
            s3.close()
            s4 = ExitStack()
            def mk_wb(BBl, WBt, WB3t):
                for q in range(4):
                    for ri in range(2):
                        ps = k.psum.next()
                        c.op("pe", lambda e, q=q, ri=ri: e.transpose(out=ps[:, 0:128], in_=BBl[ri][:, 4 * q:4 * q + 4, :], identity=k.ident[:, :]), reads=[BBl[ri], k.ident], writes=[ps])
                        evac(c, k, WBt, WBt[:, q, ri, :], ps, ps[:, 0:128])
                        c.op("act", lambda e, q=q, ri=ri: e.copy(out=WB3t[64:128, q, ri, :], in_=ps[64:128, 0:128]), reads=[ps], writes=[WB3t])
                        c.op("act", lambda e, q=q, ri=ri: e.mul(out=WB3t[64:96, q, ri, :], in_=ps[64:96, 0:128], mul=0.0), reads=[ps], writes=[WB3t])
            mk_wb(BBe, WB, WB3)
            BBa = [c.sb([128, 16, 32], F32, "BBa", s4) for _ in range(2)]
            v1 = c.sb([128, 16, 32], F32, "v1", s4)
            v2 = c.sb([128, 16, 32], F32, "v2", s4)
            arb32 = abr[A2].unsqueeze(2).to_broadcast([128, 16, 32])
            aib32 = abi[A2].unsqueeze(2).to_broadcast([128, 16, 32])
            c.op("dve", lambda e: e.tensor_tensor(out=v1[A3], in0=BBe[0][A3], in1=arb32, op=ALU.mult), reads=[BBe[0], abr], writes=[v1])
            c.op("dve", lambda e: e.tensor_tensor(out=v2[A3], in0=BBe[1][A3], in1=aib32, op=ALU.mult), reads=[BBe[1], abi], writes=[v2])
            c.op("dve", lambda e: e.tensor_tensor(out=BBa[0][A3], in0=v1[A3], in1=v2[A3], op=ALU.subtract), reads=[v1, v2], writes=[BBa[0]])
            c.op("dve", lambda e: e.tensor_tensor(out=v1[A3], in0=BBe[1][A3], in1=arb32, op=ALU.mult), reads=[BBe[1], abr], writes=[v1])
            c.op("dve", lambda e: e.tensor_tensor(out=v2[A3], in0=BBe[0][A3], in1=aib32, op=ALU.mult), reads=[BBe[0], abi], writes=[v2])
            c.op("dve", lambda e: e.tensor_tensor(out=BBa[1][A3], in0=v1[A3], in1=v2[A3], op=ALU.add), reads=[v1, v2], writes=[BBa[1]])
            mk_wb(BBa, WBa, WBa3)
            c.flush()
            s4.close()
            maskC = c.sb([128, 2], F32, "maskC", sW)
            pi_ = c.sb([128, 1], I32, "pi", sW)
            gi_ = c.sb([128, 1], I32, "gi", sW)
            c.op("pool", lambda e: e.iota(out=pi_[:, :], pattern=[[0, 1]], base=0, channel_multiplier=1), writes=[pi_])
            c.op("dve", lambda e: e.tensor_scalar(out=gi_[:, :], in0=pi_[:, :], scalar1=4, scalar2=1, op0=ALU.arith_shift_right, op1=ALU.bitwise_and), reads=[pi_], writes=[gi_])
            c.op("dve", lambda e: e.tensor_copy(out=maskC[:, 1:2], in_=gi_[:, :]), reads=[gi_], writes=[maskC])
            c.op("dve", lambda e: e.tensor_scalar(out=maskC[:, 0:1], in0=maskC[:, 1:2], scalar1=-1.0, scalar2=1.0, op0=ALU.mult, op1=ALU.add), reads=[maskC], writes=[maskC])
            if chk(6):
                return
            c.op("pool", lambda e: e.memset(WC[:], 0.0), writes=[WC])
            s5 = ExitStack()
            for ri, nm in enumerate(["s5_c_re", "s5_c_im"]):
                ccl = c.sb([128, 4, 64], F32, "ccl", s5)
                c.dma("sp", ccl[:], D[nm].rearrange("(q r) h p -> (r h) q p", r=8), writes=[ccl])
                cce = c.sb([128, 4, 2, 64], F32, "cce", s5)
                c.op("dve", lambda e, ccl=ccl, cce=cce: e.tensor_tensor(out=cce[:], in0=ccl[:].unsqueeze(2).to_broadcast([128, 4, 2, 64]), in1=maskC[:, :].unsqueeze(1).unsqueeze(3).to_broadcast([128, 4, 2, 64]), op=ALU.mult), reads=[ccl, maskC], writes=[cce])
                for q in range(4):
                    ps = k.psum.next()
                    c.op("pe", lambda e, q=q, cce=cce: e.transpose(out=ps[:, 0:128], in_=cce[:, q, :, :], identity=k.ident[:, :]), reads=[cce, k.ident], writes=[ps])
                    for a4 in range(4):
                        c.op("act", lambda e, q=q, a4=a4, ri=ri: e.mul(out=WC[:, 4 * q + a4, ri, 32 * a4:32 * a4 + 32], in_=ps[:, 32 * a4:32 * a4 + 32], mul=(1.0 if ri == 0 else -1.0)), reads=[ps], writes=[WC])
            c.flush()
            s5.close()
            c.op("pool", lambda e: e.tensor_copy(out=WC16[:], in_=WC[:]), reads=[WC], writes=[WC16])
            nabi = c.sb([128, 16], F32, "nabi", sW)
            c.op("dve", lambda e: e.tensor_scalar(out=nabi[A2], in0=abi[A2], scalar1=-1.0, scalar2=None, op0=ALU.mult), reads=[abi], writes=[nabi])
            c.op("dve", lambda e: e.tensor_tensor(out=mag2[A2], in0=mag[A2], in1=mag[A2], op=ALU.mult), reads=[mag], writes=[mag2])
            tmpc = c.sb([128, 128], F32, "tmpc", sW)
            for a in range(16):
                c.op("dve", lambda e, a=a: e.tensor_scalar(out=tmpc[:, :], in0=WC[:, a, 0, :], scalar1=abr[:, a:a + 1], scalar2=None, op0=ALU.mult), reads=[WC, abr], writes=[tmpc])
                c.op("dve", lambda e, a=a: e.scalar_tensor_tensor(out=WCa[:, a, 0, :], in0=WC[:, a, 1, :], scalar=abi[:, a:a + 1], in1=tmpc[:, :], op0=ALU.mult, op1=ALU.add), reads=[WC, abi, tmpc], writes=[WCa])
                c.op("dve", lambda e, a=a: e.tensor_scalar(out=tmpc[:, :], in0=WC[:, a, 1, :], scalar1=abr[:, a:a + 1], scalar2=None, op0=ALU.mult), reads=[WC, abr], writes=[tmpc])
                c.op("dve", lambda e, a=a: e.scalar_tensor_tensor(out=WCa[:, a, 1, :], in0=WC[:, a, 0, :], scalar=nabi[:, a:a + 1], in1=tmpc[:, :], op0=ALU.mult, op1=ALU.add), reads=[WC, nabi, tmpc], writes=[WCa])
            p5 = c.sb([128, 1], I32, "p5", sW)
            p5f = c.sb([128, 1], F32, "p5f", sW)
            c5 = c.sb([128, SEG], I32, "c5", sW)
            c5f = c.sb([128, SEG], F32, "c5f", sW)
            maskB = c.sb([128, SEG], F32, "maskB", sW)
            c.op("dve", lambda e: e.tensor_scalar(out=p5[:, :], in0=pi_[:, :], scalar1=5, scalar2=None, op0=ALU.arith_shift_right), reads=[pi_], writes=[p5])
            c.op("dve", lambda e: e.tensor_copy(out=p5f[:, :], in_=p5[:, :]), reads=[p5], writes=[p5f])
            c5i = c.sb([128, SEG], I32, "c5i", sW)
            c.op("pool", lambda e: e.iota(out=c5i[A2], pattern=[[1, SEG]], base=0, channel_multiplier=0), writes=[c5i])
            c.op("dve", lambda e: e.tensor_scalar(out=c5[A2], in0=c5i[A2], scalar1=5, scalar2=None, op0=ALU.arith_shift_right), reads=[c5i], writes=[c5])
            c.op("dve", lambda e: e.tensor_copy(out=c5f[A2], in_=c5[A2]), reads=[c5], writes=[c5f])
            c.op("dve", lambda e: e.tensor_scalar(out=maskB[A2], in0=c5f[A2], scalar1=p5f[:, 0:1], scalar2=None, op0=ALU.is_equal), reads=[c5f, p5f], writes=[maskB])
            for q in range(4):
                ps = k.psum.next()
                for a4 in range(4):
                    for ri in range(2):
                        c.op("pe", lambda e, q=q, a4=a4, ri=ri: e.matmul(ps[:, 0:128], BBe[ri][:, 4 * q:4 * q + 4, :], WC[:, 4 * q + a4, ri, :], start=(a4 == 0 and ri == 0), stop=(a4 == 3 and ri == 1)), reads=[BBe[ri], WC], writes=[ps], accum=not (a4 == 0 and ri == 0))
                c.op("dve", lambda e, q=q: e.tensor_tensor(out=G0t[:, q, :], in0=ps[:, 0:128], in1=maskB[A2], op=ALU.mult), reads=[ps, maskB], writes=[G0t])
            c.flush()
            sW.close()
            jf = c.sb([128, SEG], F32, "jf", s2)
            ji = c.sb([128, SEG], I32, "ji", s2)
            c.op("pool", lambda e: e.iota(out=ji[A2], pattern=[[1, SEG]], base=0, channel_multiplier=0), writes=[ji])
            c.op("dve", lambda e: e.tensor_copy(out=jf[A2], in_=ji[A2]), reads=[ji], writes=[jf])
            if chk(8):
                return
            th2 = c.sb([128, 16], F32, "th2", s2)
            c.op("dve", lambda e: e.tensor_scalar(out=th2[A2], in0=th[A2], scalar1=2.0, scalar2=None, op0=ALU.mult), reads=[th], writes=[th2])
            arg = c.sb([128, 4, SEG], F32, "arg", s2)
            for ch in range(4):
                c.op("dve", lambda e, ch=ch: e.tensor_tensor(out=arg[A3], in0=jf[A2].unsqueeze(1).to_broadcast([128, 4, SEG]), in1=th2[:, 4 * ch:4 * ch + 4].unsqueeze(2).to_broadcast([128, 4, SEG]), op=ALU.mult), reads=[jf, th2], writes=[arg])
                sin_of(c, k, ST, ST[:, 4 * ch:4 * ch + 4, :], arg[A3], arg, [128, 4, SEG], s2)
                sin_of(c, k, CT, CT[:, 4 * ch:4 * ch + 4, :], arg[A3], arg, [128, 4, SEG], s2, shift=math.pi / 2)
            c.op("dve", lambda e: e.tensor_copy(out=c2[A2], in_=CT[:, :, 1]), reads=[CT], writes=[c2])
            c.op("dve", lambda e: e.tensor_copy(out=s2t[A2], in_=ST[:, :, 1]), reads=[ST], writes=[s2t])
            c.flush()
        import os
        if os.environ.get("S5STOP"):
            return

        c.op("dve", lambda e: e.memset(xpr[A2], 0.0), writes=[xprT])
        c.op("dve", lambda e: e.memset(xpi[A2], 0.0), writes=[xpiT])
        TOK = 2 * SEG
        ygf = c.sb([128, 4, TOK], F32, "ygf", st)
        ygb = c.sb([128, 4, TOK], BF16, "ygb", st)
        sg_p = Pool_(c, [128, TOK], F32, 2, "sg", st)
        zin_r = c.sb([128, 16], F32, "zinr", st)
        zin_i = c.sb([128, 16], F32, "zini", st)
        z1 = c.sb([128, 16], F32, "z1", st)
        z2 = c.sb([128, 16], F32, "z2", st)
        dtl, dc, _ = k.vt["s5_d"]
        bgt, bgc, _ = k.vt["b_glu"]
        mixy = k.mixy

        def u_proj(c0, w):
            rmsnorm_block(c, k, c0, w, "norm_mix_0", xn)
            for m in range(4):
                ps = k.psum.next()
                proj(c, k, ps[:, 0:w], ps, wu, lambda kt, m=m: wu[:, kt, m * 128:(m + 1) * 128], xn, w)
                evac(c, k, useg, useg[:, m, 0:w], ps, ps[:, 0:w])

        def y_tail(c0, w, yps, pair=False):
            n = w // 2
            for q in range(4):
                if pair:
                    c.op("dve", lambda e, q=q: e.scalar_tensor_tensor(out=ygf[:, q, 1:w:2], in0=useg[:, q, 1:w:2], scalar=dtl[:, dc + q:dc + q + 1], in1=yps[q][:, 0:n], op0=ALU.mult, op1=ALU.add), reads=[useg, dtl, yps[q]], writes=[ygf])
                    c.op("dve", lambda e, q=q: e.scalar_tensor_tensor(out=ygf[:, q, 0:w:2], in0=useg[:, q, 0:w:2], scalar=dtl[:, dc + q:dc + q + 1], in1=yps[q][:, SEG:SEG + n], op0=ALU.mult, op1=ALU.add), reads=[useg, dtl, yps[q]], writes=[ygf])
                else:
                    c.op("dve", lambda e, q=q: e.scalar_tensor_tensor(out=ygf[:, q, 0:w], in0=useg[:, q, 0:w], scalar=dtl[:, dc + q:dc + q + 1], in1=yps[q][:, 0:w], op0=ALU.mult, op1=ALU.add), reads=[useg, dtl, yps[q]], writes=[ygf])
                c.op("act", lambda e, q=q: e.activation(out=ygf[:, q, 0:w], in_=ygf[:, q, 0:w], func=GELU), reads=[ygf], writes=[ygf])
                c.op("act", lambda e, q=q: e.copy(out=ygb[:, q, 0:w], in_=ygf[:, q, 0:w]), reads=[ygf], writes=[ygb])
            for qo in range(4):
                ps = k.psum.next()
                for q in range(4):
                    c.op("pe", lambda e, q=q, qo=qo: e.matmul(ps[:, 0:w], wglu[:, q, qo * 128:(qo + 1) * 128], ygb[:, q, 0:w], start=(q == 0), stop=(q == 3)), reads=[wglu, ygb], writes=[ps], accum=(q > 0))
                sg = sg_p.next()
                c.op("act", lambda e, qo=qo: e.activation(out=sg[:, 0:w], in_=ps[:, 0:w], func=AF.Sigmoid, bias=bgt[:, bgc + qo:bgc + qo + 1], scale=1.0), reads=[ps, bgt], writes=[sg])
                c.op("dve", lambda e, qo=qo: e.tensor_tensor(out=mixy[:, qo, c0:c0 + w], in0=ygf[:, qo, 0:w], in1=sg[:, 0:w], op=ALU.mult), reads=[ygf, sg], writes=[mixy])

        allb = k.psum.ts
        ybanks = allb[0:4]
        k.psum.ts = allb[4:8]
        k.psum.i = 0
        segs = [(s0, min(TOK, NP_TOK - s0)) for s0 in range(0, NP_TOK, TOK)]
        with ExitStack() as sm:
            tp = Pool_(c, [128, SEG], F32, 12, "s5t", sm)
            wr_p = Pool_(c, [128, SEG], F32, 2, "wr", sm)
            wi_p = Pool_(c, [128, SEG], F32, 2, "wi", sm)
            zr_p = Pool_(c, [128, SEG], F32, 2, "zr", sm)
            zi_p = Pool_(c, [128, SEG], F32, 2, "zi", sm)
            xb_p = [Pool_(c, [128, SEG + 4], F32, 2, "xb", sm) for _ in range(2)]
            xh_p = [Pool_(c, [128, SEG + 4], BF16, 6, "xh", sm) for _ in range(2)]
            for (c0, w) in segs:
                n = w // 2
                u_proj(c0, w)
                c.op("dve", lambda e: e.tensor_tensor(out=z1[A2], in0=c2[A2], in1=xpr[A2], op=ALU.mult), reads=[c2, xprT], writes=[z1])
                c.op("dve", lambda e: e.tensor_tensor(out=z2[A2], in0=s2t[A2], in1=xpi[A2], op=ALU.mult), reads=[s2t, xpiT], writes=[z2])
                c.op("dve", lambda e: e.tensor_tensor(out=zin_r[A2], in0=z1[A2], in1=z2[A2], op=ALU.subtract), reads=[z1, z2], writes=[zin_r])
                c.op("dve", lambda e: e.tensor_tensor(out=z1[A2], in0=s2t[A2], in1=xpr[A2], op=ALU.mult), reads=[s2t, xprT], writes=[z1])
                c.op("dve", lambda e: e.tensor_tensor(out=z2[A2], in0=c2[A2], in1=xpi[A2], op=ALU.mult), reads=[c2, xpiT], writes=[z2])
                c.op("dve", lambda e: e.tensor_tensor(out=zin_i[A2], in0=z1[A2], in1=z2[A2], op=ALU.add), reads=[z1, z2], writes=[zin_i])
                xbs = {}
                for a in range(16):
                    q, a4 = a // 4, a % 4
                    lo = 32 * a4
                    bps = k.psum.next()
                    for ri in range(2):
                        if a4 < 3:
                            Wa_, Wb_, r0, r1 = WBa, WB, lo, lo + 32
                        else:
                            Wa_, Wb_, r0, r1 = WBa3, WB3, 64, 128
                        c.op("pe", lambda e, ri=ri, Wa_=Wa_, r0=r0, r1=r1: e.matmul(bps[:, ri * SEG:ri * SEG + n], Wa_[r0:r1, q, ri, :], useg[r0:r1, q, 0:w:2], start=True, stop=False), reads=[Wa_, useg], writes=[bps], accum=(ri > 0))
                        c.op("pe", lambda e, ri=ri, Wb_=Wb_, r0=r0, r1=r1: e.matmul(bps[:, ri * SEG:ri * SEG + n], Wb_[r0:r1, q, ri, :], useg[r0:r1, q, 1:w:2], start=False, stop=True), reads=[Wb_, useg], writes=[bps], accum=True)
                    br, bi = bps[:, 0:n], bps[:, SEG:SEG + n]
                    ct, st_ = CT[:, a, 0:n], ST[:, a, 0:n]
                    t1, t2, t3, t4 = tp.next(), tp.next(), tp.next(), tp.next()
                    c.op("dve", lambda e: e.tensor_tensor(out=t1[:, 0:n], in0=br, in1=ct, op=ALU.mult), reads=[bps, CT], writes=[t1])
                    c.op("dve", lambda e: e.tensor_tensor(out=t2[:, 0:n], in0=bi, in1=st_, op=ALU.mult), reads=[bps, ST], writes=[t2])
                    c.op("dve", lambda e: e.tensor_tensor(out=t3[:, 0:n], in0=bi, in1=ct, op=ALU.mult), reads=[bps, CT], writes=[t3])
                    c.op("dve", lambda e: e.tensor_tensor(out=t4[:, 0:n], in0=br, in1=st_, op=ALU.mult), reads=[bps, ST], writes=[t4])
                    wr, wi = wr_p.next(), wi_p.next()
                    c.op("pool", lambda e: e.tensor_tensor(out=wr[:, 0:n], in0=t1[:, 0:n], in1=t2[:, 0:n], op=ALU.add), reads=[t1, t2], writes=[wr])
                    c.op("pool", lambda e: e.tensor_tensor(out=wi[:, 0:n], in0=t3[:, 0:n], in1=t4[:, 0:n], op=ALU.subtract), reads=[t3, t4], writes=[wi])
                    zr, zi = zr_p.next(), zi_p.next()
                    mg = mag2[:, a:a + 1].to_broadcast([128, n])
                    c.op("dve", lambda e: e.tensor_tensor_scan(out=zr[:, 0:n], data0=mg, data1=wr[:, 0:n], initial=zin_r[:, a:a + 1], op0=ALU.mult, op1=ALU.add), reads=[mag2, wr, zin_r], writes=[zr])
                    c.op("dve", lambda e: e.tensor_tensor_scan(out=zi[:, 0:n], data0=mg, data1=wi[:, 0:n], initial=zin_i[:, a:a + 1], op0=ALU.mult, op1=ALU.add), reads=[mag2, wi, zin_i], writes=[zi])
                    u1, u2, u3, u4 = tp.next(), tp.next(), t1, t2
                    c.op("pool", lambda e: e.tensor_tensor(out=u1[:, 0:n], in0=zr[:, 0:n], in1=ct, op=ALU.mult), reads=[zr, CT], writes=[u1])
                    c.op("pool", lambda e: e.tensor_tensor(out=u2[:, 0:n], in0=zi[:, 0:n], in1=st_, op=ALU.mult), reads=[zi, ST], writes=[u2])
                    c.op("pool", lambda e: e.tensor_tensor(out=u3[:, 0:n], in0=zr[:, 0:n], in1=st_, op=ALU.mult), reads=[zr, ST], writes=[u3])
                    c.op("pool", lambda e: e.tensor_tensor(out=u4[:, 0:n], in0=zi[:, 0:n], in1=ct, op=ALU.mult), reads=[zi, CT], writes=[u4])
                    xbr, xbi = xb_p[0].next(), xb_p[1].next()
                    c.op("act", lambda e, a=a: e.copy(out=xbr[:, 0:1], in_=xpr[:, a:a + 1]), reads=[xprT[a]], writes=[xbr])
                    c.op("act", lambda e, a=a: e.copy(out=xbi[:, 0:1], in_=xpi[:, a:a + 1]), reads=[xpiT[a]], writes=[xbi])
                    c.op("dve", lambda e: e.tensor_tensor(out=xbr[:, 1:1 + n], in0=u1[:, 0:n], in1=u2[:, 0:n], op=ALU.subtract), reads=[u1, u2], writes=[xbr])
                    c.op("dve", lambda e: e.tensor_tensor(out=xbi[:, 1:1 + n], in0=u3[:, 0:n], in1=u4[:, 0:n], op=ALU.add), reads=[u3, u4], writes=[xbi])
                    c.op("act", lambda e, a=a: e.copy(out=xpr[:, a:a + 1], in_=xbr[:, n:n + 1]), reads=[xbr], writes=[xprT[a]])
                    c.op("act", lambda e, a=a: e.copy(out=xpi[:, a:a + 1], in_=xbi[:, n:n + 1]), reads=[xbi], writes=[xpiT[a]])
                    xhr, xhi = xh_p[0].next(), xh_p[1].next()
                    c.op("act", lambda e: e.copy(out=xhr[:, 0:1 + n], in_=xbr[:, 0:1 + n]), reads=[xbr], writes=[xhr])
                    c.op("act", lambda e: e.copy(out=xhi[:, 0:1 + n], in_=xbi[:, 0:1 + n]), reads=[xbi], writes=[xhi])
                    xbs[a] = (xhr, xhi)
                    if a4 == 3:
                        yb = ybanks[q]
                        first = True
                        for aa in range(4 * q, 4 * q + 4):
                            for ri in range(2):
                                xs_ = xbs[aa][ri]
                                last = (aa == 4 * q + 3 and ri == 1)
                                c.op("pe", lambda e, aa=aa, ri=ri, xs_=xs_, first=first, last=last: e.matmul(yb[:, 0:n], WC16[:, aa, ri, :], xs_[:, 1:1 + n], start=first, stop=last), reads=[WC16, xs_], writes=[yb], accum=not first)
                                first = False
                        first = True
                        for aa in range(4 * q, 4 * q + 4):
                            for ri in range(2):
                                xs_ = xbs[aa][ri]
                                c.op("pe", lambda e, aa=aa, ri=ri, xs_=xs_, first=first: e.matmul(yb[:, SEG:SEG + n], WCa[:, aa, ri, :], xs_[:, 0:n], start=first, stop=False), reads=[WCa, xs_], writes=[yb], accum=True)
                                first = False
                        c.op("pe", lambda e: e.matmul(yb[:, SEG:SEG + n], G0t[:, q, :], useg[:, q, 0:w:2], start=False, stop=True), reads=[G0t, useg], writes=[yb], accum=True)
                y_tail(c0, w, ybanks, pair=True)
            c.flush()

        if chk(17):
            return
        for nm, xp_, xpT_ in (("s5_re_prompt", xpr, xprT), ("s5_im_prompt", xpi, xpiT)):
            ps = k.psum.next()
            c.op("pe", lambda e, xp_=xp_: e.transpose(out=ps[0:16, 0:128], in_=xp_[:, :], identity=k.ident[:, :]), reads=[xpT_, k.ident], writes=[ps])
            so = c.sb([16, 128], F32, "s5o", st)
            evac(c, k, so, so[:, :], ps, ps[0:16, 0:128])
            c.dma("sp", O[nm].rearrange("(a g) p -> a (g p)", g=2), so[:, :], reads=[so])

        if chk(18):
            return
        x0 = []
        sgl = c.sb([32, 2048], F32, "s5ld", st)
        for nm in ("state_s5_re", "state_s5_im"):
            sg = sgl
            c.dma("sp", sg[0:16, :], D[nm].rearrange("s g p -> s (g p)"), writes=[sg])
            xt = c.sb([128, 16, 16], F32, "x0", st)
            ps = k.psum.next()
            for a in range(16):
                c.op("pe", lambda e, a=a: e.transpose(out=ps[:, a * 16:(a + 1) * 16], in_=sg[0:16, a * 128:(a + 1) * 128], identity=k.ident[0:16, 0:16]), reads=[sg, k.ident], writes=[ps], accum=(a > 0))
            evac(c, k, xt, xt[:, :, :], ps, ps[:, 0:256].rearrange("p (a s) -> p a s", s=16))
            x0.append(xt)
        x0r, x0i = x0
        A3 = (slice(None), slice(None), slice(None))
        arb = abr[A2].unsqueeze(2).to_broadcast([128, 16, 16])
        aib = abi[A2].unsqueeze(2).to_broadcast([128, 16, 16])
        s1 = c.sb([128, 16, 16], F32, "s1", st)
        s2_ = c.sb([128, 16, 16], F32, "s2", st)
        xnr = c.sb([128, 16, 16], F32, "xnr", st)
        xni = c.sb([128, 16, 16], F32, "xni", st)
        u_proj(NP_TOK, 16)
        bps4 = [k.psum.next() for _ in range(4)]
        for a in range(16):
            q, a4 = a // 4, a % 4
            lo = 32 * a4
            bp = bps4[a4]
            for ri in range(2):
                col = (q * 2 + ri) * 16
                if a4 < 3:
                    c.op("pe", lambda e, ri=ri, q=q, lo=lo, bp=bp, col=col: e.matmul(bp[:, col:col + 16], WB[lo:lo + 32, q, ri, :], useg[lo:lo + 32, q, 0:16], start=True, stop=True), reads=[WB, useg], writes=[bp], accum=(q + ri > 0))
                else:
                    c.op("pe", lambda e, ri=ri, q=q, bp=bp, col=col: e.matmul(bp[:, col:col + 16], WB3[64:128, q, ri, :], useg[64:128, q, 0:16], start=True, stop=True), reads=[WB3, useg], writes=[bp], accum=(q + ri > 0))
        bu = c.sb([128, 16, 2, 16], F32, "bu", st)
        for a4 in range(4):
            c.op("act", lambda e, a4=a4: e.copy(out=bu[:, a4:16:4, :, :], in_=bps4[a4][:, 0:128].rearrange("p (q r s) -> p q r s", r=2, s=16)), reads=[bps4[a4]], writes=[bu])
        bpv = bu
        bps = bu
        c.op("dve", lambda e: e.tensor_tensor(out=s1[A3], in0=x0r[A3], in1=arb, op=ALU.mult), reads=[x0r, abr], writes=[s1])
        c.op("dve", lambda e: e.tensor_tensor(out=s2_[A3], in0=x0i[A3], in1=aib, op=ALU.mult), reads=[x0i, abi], writes=[s2_])
        c.op("dve", lambda e: e.tensor_tensor(out=s1[A3], in0=s1[A3], in1=s2_[A3], op=ALU.subtract), reads=[s1, s2_], writes=[s1])
        c.op("dve", lambda e: e.tensor_tensor(out=xnr[A3], in0=bpv[:, :, 0, :], in1=s1[A3], op=ALU.add), reads=[bps, s1], writes=[xnr])
        c.op("dve", lambda e: e.tensor_tensor(out=s1[A3], in0=x0i[A3], in1=arb, op=ALU.mult), reads=[x0i, abr], writes=[s1])
        c.op("dve", lambda e: e.tensor_tensor(out=s2_[A3], in0=x0r[A3], in1=aib, op=ALU.mult), reads=[x0r, abi], writes=[s2_])
        c.op("dve", lambda e: e.tensor_tensor(out=s1[A3], in0=s1[A3], in1=s2_[A3], op=ALU.add), reads=[s1, s2_], writes=[s1])
        c.op("dve", lambda e: e.tensor_tensor(out=xni[A3], in0=bpv[:, :, 1, :], in1=s1[A3], op=ALU.add), reads=[bps, s1], writes=[xni])
        yps = []
        xsh = [c.sb([128, 16, 16], BF16, "xsh", st) for _ in range(2)]
        c.op("act", lambda e: e.copy(out=xsh[0][A3], in_=xnr[A3]), reads=[xnr], writes=[xsh[0]])
        c.op("act", lambda e: e.copy(out=xsh[1][A3], in_=xni[A3]), reads=[xni], writes=[xsh[1]])
        for q in range(4):
            yp = ybanks[q]
            for a4 in range(4):
                a = 4 * q + a4
                for ri in range(2):
                    xs_ = xsh[ri]
                    c.op("pe", lambda e, a=a, ri=ri, xs_=xs_: e.matmul(yp[:, 0:16], WC16[:, a, ri, :], xs_[:, a, :], start=(a4 == 0 and ri == 0), stop=(a4 == 3 and ri == 1)), reads=[WC16, xs_], writes=[yp], accum=not (a4 == 0 and ri == 0))
            yps.append(yp)
        y_tail(NP_TOK, 16, yps, pair=False)
        for nm, xt in (("s5_re_sample", xnr), ("s5_im_sample", xni)):
            stg = store_T(c, k, xt, lambda t, xt=xt: xt[:, t, :], 16, 128, 16, None, st, "s5st", stg=sgl)
            c.dma("sp", O[nm].rearrange("s g p -> s (g p)"), stg[0:16, :], reads=[stg])
        k.psum.ts = allb
        k.psum.i = 0
        c.flush()
GELU = AF.Gelu_apprx_tanh


def add_to_x(c, k, m, c0, w, ps):
    c.op("dve", lambda e: e.tensor_tensor(out=k.X[m][:, c0:c0 + w], in0=ps[:, 0:w], in1=k.X[m][:, c0:c0 + w], op=ALU.add), reads=[ps, k.X[m].ts(c0, c0 + w)], writes=[k.X[m].ts(c0, c0 + w)])


def phase3(c, k, D, O):
    with ExitStack() as st:
        wo = c.sb([128, 8, 1024], BF16, "wo", st)
        W = D["w_out_0"].rearrange("(k p) n -> p k n", p=128)
        woT = [T(wo.ap[:, :, 128 * m:128 * (m + 1)], "woT%d" % m) for m in range(8)]
        for m in range(8):
            load_w_bf16(c, k, woT[m], wo[:, :, 128 * m:128 * (m + 1)], W[:, :, 128 * m:128 * (m + 1)])
        for (c0, c1) in BLOCKS:
            w = c1 - c0
            for m in range(8):
                ps = k.psum.next()
                for kt in range(8):
                    src = k.mixo if kt < 4 else k.mixy
                    c.op("pe", lambda e, kt=kt, src=src: e.matmul(ps[:, 0:w], wo[:, kt, m * 128:(m + 1) * 128], src[:, kt % 4, c0:c0 + w], start=(kt == 0), stop=(kt == 7)), reads=[woT[m], src], writes=[ps], accum=(kt > 0))
                add_to_x(c, k, m, c0, w, ps)
        c.flush()


def store_T(c, k, src_t, src_ap_fn, ntile, pw, n, dst_rows_fn, stack, name, stg=None):
    if stg is None:
        stg = c.sb([32 if n <= 32 else 128, ntile * pw], F32, name, stack)
    per = 512 // pw
    for t0 in range(0, ntile, per):
        ps = k.psum.next()
        nt = min(per, ntile - t0)
        for i in range(nt):
            c.op("pe", lambda e, i=i: e.transpose(out=ps[0:n, i * pw:(i + 1) * pw], in_=src_ap_fn(t0 + i), identity=k.ident[0:pw, 0:pw]), reads=[src_t, k.ident], writes=[ps], accum=(i > 0))
        evac(c, k, stg, stg[0:n, t0 * pw:(t0 + nt) * pw], ps, ps[0:n, 0:nt * pw])
    return stg


def phase_ffn(c, k, D, O, l):
    GROUPS = [(0, 4), (4, 4), (8, 4), (12, 4), (16, 4), (20, 2)]
    with ExitStack() as st:
        xn = c.sb([128, 8, NT], BF16, "fxn", st)
        k.rspool = Pool_(c, [128, 512], F32, 2, "rs", st)
        cbs = c.sb([32, D_FF], F32, "cbs", st)
        cbuf = c.sb([128, 22, 32], F32, "cbuf", st)
        gtail = c.sb([128, 22, 18], F32, "gtail", st)
        wup_p = Pool_(c, [128, 8, 2, 512], BF16, 2, "wup", st)
        wdn_p = Pool_(c, [128, 4, 1024], BF16, 2, "wdn", st)
        gbuf = [c.sb([128, 516], F32, "gbuf", st) for m in range(4)]
        acc_p = Pool_(c, [128, 512], F32, 2, "acc", st)
        ga_p = Pool_(c, [128, 512], F32, 2, "ga", st)
        hb_p = [Pool_(c, [128, 512], BF16, 2, "hb", st) for m in range(4)]
        Wup = D["ffn_w_up"][l].rearrange("(k p) (t n) -> p k t n", p=128, t=2)
        Wdn = D["ffn_w_down"][l].rearrange("(k p) n -> p k n", p=128)

        def load_group(g):
            t0, nt = GROUPS[g]
            wu, wd = wup_p.next(), wdn_p.next()
            for t in range(2):
                load_w_bf16(c, k, wu, wu[:, :, t, 0:128 * nt], Wup[:, :, t, 128 * t0:128 * (t0 + nt)])
            load_w_bf16(c, k, wd, wd[:, 0:nt, :], Wdn[:, t0:t0 + nt, :])
            return wu, wd
        nxt = load_group(0)
        c.dma("sp", cbs[0:32, :], D["cache_ffn_conv"][l].rearrange("s t f -> (s t) f"), writes=[cbs])
        for t0 in (0, 16):
            ps = k.psum.next()
            nt = min(16, 22 - t0)
            for i in range(nt):
                c.op("pe", lambda e, i=i: e.transpose(out=ps[:, i * 32:(i + 1) * 32], in_=cbs[0:32, (t0 + i) * 128:(t0 + i + 1) * 128], identity=k.ident[0:32, 0:32]), reads=[cbs, k.ident], writes=[ps], accum=(i > 0))
            evac(c, k, cbuf, cbuf[:, t0:t0 + nt, :], ps, ps[:, 0:nt * 32].rearrange("p (a b) -> p a b", b=32))
        xnb = {c0: T(xn.ap[:, :, c0:c1], "fxn%d" % c0) for (c0, c1) in BLOCKS}
        for (c0, c1) in BLOCKS:
            rmsnorm_block(c, k, c0, c1 - c0, "norm_ffn%d" % l, xnb[c0])
        w0t, w0c, _ = k.vt["fcw%d_0" % l]
        w1t, w1c, _ = k.vt["fcw%d_1" % l]
        w2t, w2c, _ = k.vt["fcw%d_2" % l]
        bt, bc, _ = k.vt["fcb%d" % l]
        for g in range(len(GROUPS)):
            t0g, ntg = GROUPS[g]
            wu, wd = nxt
            if g + 1 < len(GROUPS):
                nxt = load_group(g + 1)
            for m in range(ntg):
                c.op("pool", lambda e, m=m: e.memset(gbuf[m][:, 0:2], 0.0), writes=[gbuf[m]])
            for (c0, c1) in BLOCKS:
                w = c1 - c0
                wp = w if w == 512 else 16
                hbs = []
                for m in range(ntg):
                    hm = t0g + m
                    gps = k.psum.next()
                    proj(c, k, gps[:, 0:w], gps, wu, lambda kt, m=m: wu[:, kt, 0, m * 128:(m + 1) * 128], xnb[c0], w)
                    vps = k.psum.next()
                    proj(c, k, vps[:, 0:w], vps, wu, lambda kt, m=m: wu[:, kt, 1, m * 128:(m + 1) * 128], xnb[c0], w)
                    gb = gbuf[m]
                    c.op("act", lambda e: e.copy(out=gb[:, 2:2 + w], in_=gps[:, 0:w]), reads=[gps], writes=[gb])
                    acc = acc_p.next()
                    c.op("dve", lambda e: e.tensor_scalar(out=acc[:, 0:wp], in0=gb[:, 2:2 + wp], scalar1=w2t[:, w2c + hm:w2c + hm + 1], scalar2=bt[:, bc + hm:bc + hm + 1], op0=ALU.mult, op1=ALU.add), reads=[gb, w2t, bt], writes=[acc])
                    c.op("dve", lambda e: e.scalar_tensor_tensor(out=acc[:, 0:wp], in0=gb[:, 1:1 + wp], scalar=w1t[:, w1c + hm:w1c + hm + 1], in1=acc[:, 0:wp], op0=ALU.mult, op1=ALU.add), reads=[gb, w1t, acc], writes=[acc])
                    c.op("dve", lambda e: e.scalar_tensor_tensor(out=acc[:, 0:wp], in0=gb[:, 0:wp], scalar=w0t[:, w0c + hm:w0c + hm + 1], in1=acc[:, 0:wp], op0=ALU.mult, op1=ALU.add), reads=[gb, w0t, acc], writes=[acc])
                    if w == 512:
                        c.op("pool", lambda e: e.tensor_copy(out=gb[:, 0:2], in_=gb[:, 512:514]), reads=[gb], writes=[gb])
                    else:
                        cbv = cbuf[:, hm, :].rearrange("p (s t) -> p s t", t=2)
                        c.op("dve", lambda e: e.tensor_scalar(out=acc[:, 16:32], in0=gb[:, 18:34], scalar1=w2t[:, w2c + hm:w2c + hm + 1], scalar2=bt[:, bc + hm:bc + hm + 1], op0=ALU.mult, op1=ALU.add), reads=[gb, w2t, bt], writes=[acc])
                        c.op("dve", lambda e: e.scalar_tensor_tensor(out=acc[:, 16:32], in0=cbv[:, :, 1], scalar=w1t[:, w1c + hm:w1c + hm + 1], in1=acc[:, 16:32], op0=ALU.mult, op1=ALU.add), reads=[cbuf, w1t, acc], writes=[acc])
                        c.op("dve", lambda e: e.scalar_tensor_tensor(out=acc[:, 16:32], in0=cbv[:, :, 0], scalar=w0t[:, w0c + hm:w0c + hm + 1], in1=acc[:, 16:32], op0=ALU.mult, op1=ALU.add), reads=[cbuf, w0t, acc], writes=[acc])
                        c.op("pool", lambda e: e.tensor_copy(out=gtail[:, hm, :], in_=gb[:, 16:34]), reads=[gb], writes=[gtail])
                    ga = ga_p.next()
                    c.op("act", lambda e: e.activation(out=ga[:, 0:w], in_=acc[:, 0:w], func=GELU), reads=[acc], writes=[ga])
                    hb = hb_p[m].next()
                    c.op("dve", lambda e: e.tensor_tensor(out=hb[:, 0:w], in0=vps[:, 0:w], in1=ga[:, 0:w], op=ALU.mult), reads=[vps, ga], writes=[hb])
                    hbs.append(hb)
                for mo in range(8):
                    ps = k.psum.next()
                    for m in range(ntg):
                        c.op("pe", lambda e, m=m: e.matmul(ps[:, 0:w], wd[:, m, mo * 128:(mo + 1) * 128], hbs[m][:, 0:w], start=(m == 0), stop=(m == ntg - 1)), reads=[wd, hbs[m]], writes=[ps], accum=(m > 0))
                    add_to_x(c, k, mo, c0, w, ps)
        stg = store_T(c, k, gtail, lambda t: gtail[:, t, :], 22, 128, 18, None, st, "gts", stg=cbs)
        c.dma("sp", O["ffn_conv_prompt"][l], stg[0:2, :], reads=[stg])
        c.dma("sp", O["ffn_conv_sample"][l][:, 1, :], stg[2:18, :], reads=[stg])
        c.dma("sp", O["ffn_conv_sample"][l][:, 0, :], D["cache_ffn_conv"][l][:, 1, :])
        c.flush()


class T_view:
    def __init__(self, t, c0):
        self.t, self.c0 = t, c0

    def __getitem__(self, idx):
        p, kt, sl = idx
        return self.t.ap[p, kt, self.c0 + (sl.start or 0):self.c0 + sl.stop]


def phase7(c, k, D, O):
    with ExitStack() as st:
        k.rspool = Pool_(c, [128, 512], F32, 2, "rs", st)
        xnf_p = Pool_(c, [128, 8, 512], F32, 2, "xnf", st)
        sq_p = Pool_(c, [128, 8, 512], BF16, 2, "sq7", st)
        stg_p = Pool_(c, [128, 4, 1024], F32, 2, "ystg", st)
        gt, gc, _ = k.vt["norm_final"]
        for (c0, c1) in BLOCKS:
            w = c1 - c0
            xnf, sq = xnf_p.next(), sq_p.next()
            for kt in range(8):
                c.op("act", lambda e, kt=kt: e.activation(out=sq[:, kt, 0:w], in_=k.X[kt][:, c0:c0 + w], func=AF.Square), reads=[k.X[kt].ts(c0, c0 + w)], writes=[sq])
            ps = k.psum.next()
            for kt in range(8):
                c.op("pe", lambda e, kt=kt: e.matmul(ps[:, 0:w], k.ones_bf[:, :], sq[:, kt, 0:w], start=(kt == 0), stop=(kt == 7)), reads=[sq, k.ones_bf], writes=[ps], accum=(kt > 0))
            rs = k.rspool.next()
            c.op("act", lambda e: e.activation(out=rs[:, 0:w], in_=ps[:, 0:w], func=AF.Ln, bias=k.eps_t[:, 0:1], scale=1.0 / 1024.0), reads=[ps, k.eps_t], writes=[rs])
            c.op("act", lambda e: e.activation(out=rs[:, 0:w], in_=rs[:, 0:w], func=AF.Exp, scale=-0.5), reads=[rs], writes=[rs])
            for kt in range(8):
                c.op("dve", lambda e, kt=kt: e.scalar_tensor_tensor(out=xnf[:, kt, 0:w], in0=k.X[kt][:, c0:c0 + w], scalar=gt[:, gc + kt:gc + kt + 1], in1=rs[:, 0:w], op0=ALU.mult, op1=ALU.mult), reads=[k.X[kt].ts(c0, c0 + w), gt, rs], writes=[xnf])
            stg = stg_p.next()
            nj = (w + 127) // 128
            for j in range(nj):
                tw = min(128, w - 128 * j)
                for half in range(2):
                    ps2 = k.psum.next()
                    for q in range(4):
                        kt = half * 4 + q
                        c.op("pe", lambda e, q=q, kt=kt: e.transpose(out=ps2[0:tw, q * 128:(q + 1) * 128], in_=xnf[:, kt, 128 * j:128 * j + tw], identity=k.ident[:, :]), reads=[xnf, k.ident], writes=[ps2], accum=(q > 0))
                    evac(c, k, stg, stg[0:tw, j, half * 512:(half + 1) * 512], ps2, ps2[0:tw, 0:512])
            if w == 512:
                if c0 == 0:
                    c.dma("sp", O["y_prompt"][0:112, :], stg[16:128, 0, :], reads=[stg])
                    c.dma("sp", O["y_prompt"][112:496, :].rearrange("(j p) f -> p j f", p=128), stg[:, 1:4, :], reads=[stg])
                else:
                    c.dma("sp", O["y_prompt"][c0 - 16:c0 + 496, :].rearrange("(j p) f -> p j f", p=128), stg[:, :, :], reads=[stg])
            else:
                c.dma("sp", O["y_prompt"][2032:2048, :], stg[0:16, 0, :], reads=[stg])
                c.dma("sp", O["y_sample"][:, :], stg[16:32, 0, :], reads=[stg])
        c.flush()
def phase5(c, k, D, O):
    with ExitStack() as st:
        wg1 = c.sb([128, 8, 1536], BF16, "wg1", st)
        wx1 = c.sb([128, 8, 1536], BF16, "wx1", st)
        W = D["w_in_1"].rearrange("(k p) n -> p k n", p=128)
        wx1c = [T(wx1.ap[:, :, 384 * i:384 * (i + 1)], "wx1c%d" % i) for i in range(4)]
        wg1c = [T(wg1.ap[:, :, 384 * i:384 * (i + 1)], "wg1c%d" % i) for i in range(4)]
        for i in range(4):
            load_w_bf16(c, k, wx1c[i], wx1[:, :, 384 * i:384 * (i + 1)], W[:, :, 1536 + 384 * i:1536 + 384 * (i + 1)])
            load_w_bf16(c, k, wg1c[i], wg1[:, :, 384 * i:384 * (i + 1)], W[:, :, 384 * i:384 * (i + 1)])
        wa = c.sb([96, 16, 96], BF16, "wa", st)
        wxx = c.sb([96, 16, 96], BF16, "wxx", st)
        load_w_bf16(c, k, wa, wa[:], D["rnn_w_a"].rearrange("n c d -> c n d"))
        load_w_bf16(c, k, wxx, wxx[:], D["rnn_w_x"].rearrange("n c d -> c n d"))
        wo1 = c.sb([96, 16, 1024], BF16, "wo1", st)
        Wo = D["w_out_1"].rearrange("(n c) m -> c n m", c=96)
        load_w_bf16(c, k, wo1, wo1[:, 0:8, :], Wo[:, 0:8, :])
        load_w_bf16(c, k, wo1, wo1[:, 8:16, :], Wo[:, 8:16, :])
        rv = c.sb([96, 128], F32, "rv", st)
        sl = c.sb([96, 16], F32, "sl", st)
        hist = c.sb([96, 16, 48], F32, "hist", st)
        h0 = c.sb([96, 16, 16], F32, "h0", st)
        carry = c.sb([96, 16, 3], F32, "carry", st)
        hprev = c.sb([96, 16], F32, "hprev", st)
        hs = c.sb([96, 16, 16], F32, "hs", st)
        xtail = c.sb([96, 16, 19], F32, "xtail", st)
        carryT = [T(carry.ap[:, n, :], "carry%d" % n) for n in range(16)]
        hprevT = [T(hprev.ap[:, n:n + 1], "hprev%d" % n) for n in range(16)]
        c.op("dve", lambda e: e.memset(carry[:], 0.0), writes=[carryT])
        c.op("dve", lambda e: e.memset(hprev[:], 0.0), writes=[hprevT])
        stgA = c.sb([128, 1536], F32, "stgA", st)
        r96 = lambda ap: ap.rearrange("(n c) -> n c", c=96)
        c.op("pool", lambda e: e.memset(stgA[:, 0:96], 0.0), writes=[stgA])
        for j in range(4):
            c.dma("sp", stgA[16 * j:16 * j + 16, 0:96], r96(D["rnn_conv_w"][j]), writes=[stgA])
        c.dma("sp", stgA[64:80, 0:96], r96(D["rnn_conv_b"]), writes=[stgA])
        c.dma("sp", stgA[80:96, 0:96], D["rnn_b_a"], writes=[stgA])
        c.dma("sp", stgA[96:112, 0:96], D["rnn_b_x"], writes=[stgA])
        c.dma("sp", stgA[112:128, 0:96], r96(D["rnn_lam"]), writes=[stgA])
        ps = k.psum.next()
        c.op("pe", lambda e: e.transpose(out=ps[0:96, 0:128], in_=stgA[:, 0:96], identity=k.ident[:, :]), reads=[stgA, k.ident], writes=[ps])
        evac(c, k, rv, rv[:, :], ps, ps[0:96, 0:128])
        CW = lambda j, n: rv[:, 16 * j + n:16 * j + n + 1]
        CB = lambda n: rv[:, 64 + n:65 + n]
        BA = lambda n: rv[:, 80 + n:81 + n]
        BX = lambda n: rv[:, 96 + n:97 + n]
        c.op("act", lambda e: e.activation(out=sl[:, :], in_=rv[:, 112:128], func=AF.Exp, scale=-1.0), reads=[rv], writes=[sl])
        c.op("act", lambda e: e.activation(out=sl[:, :], in_=sl[:, :], func=AF.Ln, bias=k.one_t[0:96, 0:1], scale=1.0), reads=[sl, k.one_t], writes=[sl])
        c.op("act", lambda e: e.mul(out=sl[:, :], in_=sl[:, :], mul=-8.0), reads=[sl], writes=[sl])
        c.dma("sp", stgA[0:48, :], D["cache_rglru_conv"].rearrange("s t f -> (s t) f"), writes=[stgA])
        for t0 in (0, 8):
            ps = k.psum.next()
            for i in range(8):
                c.op("pe", lambda e, i=i: e.transpose(out=ps[0:96, i * 48:(i + 1) * 48], in_=stgA[0:48, (t0 + i) * 96:(t0 + i + 1) * 96], identity=k.ident[0:48, 0:48]), reads=[stgA, k.ident], writes=[ps], accum=(i > 0))
            evac(c, k, hist, hist[:, t0:t0 + 8, :], ps, ps[0:96, 0:384].rearrange("p (a b) -> p a b", b=48))
        stgB = stgA
        c.dma("sp", stgB[0:16, :], D["state_rglru"], writes=[stgB])
        ps = k.psum.next()
        for n in range(16):
            c.op("pe", lambda e, n=n: e.transpose(out=ps[0:96, n * 16:(n + 1) * 16], in_=stgB[0:16, n * 96:(n + 1) * 96], identity=k.ident[0:16, 0:16]), reads=[stgB, k.ident], writes=[ps], accum=(n > 0))
        evac(c, k, h0, h0[:, :, :], ps, ps[0:96, 0:256].rearrange("p (a b) -> p a b", b=16))

        k.rspool = Pool_(c, [128, 512], F32, 1, "rs", st)
        xn_p = Pool_(c, [128, 8, 256], BF16, 2, "xn5", st)
        ND = 3
        xrw_p = Pool_(c, [96, 260], F32, 2, "xrw", st)
        xc_p = Pool_(c, [96, 256], F32, 2, "xc", st)
        xcb_p = Pool_(c, [96, 256], BF16, 2, "xcb", st)
        tr_p = Pool_(c, [96, 256], F32, ND, "tr", st)
        ti_p = Pool_(c, [96, 256], F32, 2, "ti", st)
        aa_p = Pool_(c, [96, 256], F32, 2, "aa", st)
        a2_p = Pool_(c, [96, 256], F32, 2, "a2", st)
        hh_p = Pool_(c, [96, 256], F32, 2, "hh", st)
        gg_p = Pool_(c, [96, 256], F32, 2, "gg", st)
        ybuf_p = Pool_(c, [96, 16, 256], BF16, 1, "ybuf", st)
        hb_ = c.sb([96, 32], F32, "hb_", st)
        hsl = c.sb([96, 16], F32, "hsl", st)
        c.op("act", lambda e: e.mul(out=hb_[:, :], in_=rv[:, 80:112], mul=0.5), reads=[rv], writes=[hb_])
        c.op("act", lambda e: e.mul(out=hsl[:, :], in_=sl[:, :], mul=0.5), reads=[sl], writes=[hsl])
        B5 = [(i * 256, (i + 1) * 256) for i in range(8)] + [(2048, 2080)]
        for (c0, c1) in B5:
            w = c1 - c0
            wp = w if w != 32 else 16
            xn = xn_p.next()
            ybuf = ybuf_p.next()
            rmsnorm_block(c, k, c0, w, "norm_mix_1", xn)
            for n in range(16):
                xrw, xc, xcb, tr, ti, aa, a2, hh, gg = (p.next() for p in (xrw_p, xc_p, xcb_p, tr_p, ti_p, aa_p, a2_p, hh_p, gg_p))
                bkA = k.psum.next()
                proj(c, k, bkA[0:96, 0:w], bkA, wx1c[n // 4], lambda kt, n=n: wx1[:, kt, 96 * n:96 * (n + 1)], xn, w)
                c.op("act", lambda e: e.copy(out=xrw[:, 3:3 + w], in_=bkA[0:96, 0:w]), reads=[bkA], writes=[xrw])
                c.op("pool", lambda e, n=n: e.tensor_copy(out=xrw[:, 0:3], in_=carry[:, n, :]), reads=[carryT[n]], writes=[xrw])
                c.op("dve", lambda e, n=n: e.tensor_scalar(out=xc[:, 0:wp], in0=xrw[:, 3:3 + wp], scalar1=CW(3, n), scalar2=CB(n), op0=ALU.mult, op1=ALU.add), reads=[xrw, rv], writes=[xc])
                for j in range(3):
                    c.op("dve", lambda e, n=n, j=j: e.scalar_tensor_tensor(out=xc[:, 0:wp], in0=xrw[:, j:j + wp], scalar=CW(j, n), in1=xc[:, 0:wp], op0=ALU.mult, op1=ALU.add), reads=[xrw, rv, xc], writes=[xc])
                if w != 32:
                    c.op("pool", lambda e, n=n: e.tensor_copy(out=carry[:, n, :], in_=xrw[:, w:w + 3]), reads=[xrw], writes=[carryT[n]])
                else:
                    c.op("pool", lambda e, n=n: e.tensor_copy(out=xtail[:, n, :], in_=xrw[:, 16:35]), reads=[xrw], writes=[xtail])
                    hv = hist[:, n, :].rearrange("p (s t) -> p s t", t=3)
                    c.op("dve", lambda e, n=n: e.tensor_scalar(out=xc[:, 16:32], in0=xrw[:, 19:35], scalar1=CW(3, n), scalar2=CB(n), op0=ALU.mult, op1=ALU.add), reads=[xrw, rv], writes=[xc])
                    for j in range(3):
                        c.op("dve", lambda e, n=n, j=j, hv=hv: e.scalar_tensor_tensor(out=xc[:, 16:32], in0=hv[:, :, j], scalar=CW(j, n), in1=xc[:, 16:32], op0=ALU.mult, op1=ALU.add), reads=[hist, rv, xc], writes=[xc])
                c.op("pool", lambda e: e.tensor_copy(out=xcb[:, 0:w], in_=xc[:, 0:w]), reads=[xc], writes=[xcb])
                bkB = k.psum.next()
                c.op("pe", lambda e, n=n: e.matmul(bkB[0:96, 0:w], wa[:, n, :], xcb[:, 0:w], start=True, stop=True), reads=[wa, xcb], writes=[bkB])
                c.op("pe", lambda e, n=n: e.matmul(bkB[0:96, 256:256 + w], wxx[:, n, :], xcb[:, 0:w], start=True, stop=True), reads=[wxx, xcb], writes=[bkB], accum=True)
                c.op("act", lambda e, n=n: e.activation(out=tr[:, 0:w], in_=bkB[0:96, 0:w], func=AF.Tanh, bias=hb_[:, n:n + 1], scale=0.5), reads=[bkB, hb_], writes=[tr])
                c.op("act", lambda e, n=n: e.activation(out=ti[:, 0:w], in_=bkB[0:96, 256:256 + w], func=AF.Tanh, bias=hb_[:, 16 + n:17 + n], scale=0.5), reads=[bkB, hb_], writes=[ti])
                c.op("act", lambda e, n=n: e.activation(out=aa[:, 0:w], in_=tr[:, 0:w], func=AF.Exp, bias=hsl[:, n:n + 1], scale=hsl[:, n:n + 1]), reads=[tr, hsl], writes=[aa])
                c.op("act", lambda e, n=n: e.activation(out=a2[:, 0:w], in_=tr[:, 0:w], func=AF.Exp, bias=sl[:, n:n + 1], scale=sl[:, n:n + 1]), reads=[tr, sl], writes=[a2])
                c.op("act", lambda e: e.activation(out=a2[:, 0:w], in_=a2[:, 0:w], func=AF.Ln, bias=k.one_t[0:96, 0:1], scale=-1.0), reads=[a2, k.one_t], writes=[a2])
                c.op("act", lambda e: e.activation(out=a2[:, 0:w], in_=a2[:, 0:w], func=AF.Exp, scale=0.5), reads=[a2], writes=[a2])
                c.op("dve", lambda e: e.scalar_tensor_tensor(out=ti[:, 0:w], in0=ti[:, 0:w], scalar=1.0, in1=xc[:, 0:w], op0=ALU.add, op1=ALU.mult), reads=[ti, xc], writes=[ti])
                c.op("dve", lambda e: e.scalar_tensor_tensor(out=ti[:, 0:w], in0=ti[:, 0:w], scalar=0.5, in1=a2[:, 0:w], op0=ALU.mult, op1=ALU.mult), reads=[ti, a2], writes=[ti])
                c.op("dve", lambda e, n=n: e.tensor_tensor_scan(out=hh[:, 0:wp], data0=aa[:, 0:wp], data1=ti[:, 0:wp], initial=hprev[:, n:n + 1], op0=ALU.mult, op1=ALU.add), reads=[aa, ti, hprevT[n]], writes=[hh])
                c.op("pool", lambda e, n=n: e.tensor_copy(out=hprev[:, n:n + 1], in_=hh[:, wp - 1:wp]), reads=[hh], writes=[hprevT[n]])
                if w == 32:
                    c.op("dve", lambda e, n=n: e.tensor_tensor(out=hh[:, 16:32], in0=aa[:, 16:32], in1=h0[:, n, :], op=ALU.mult), reads=[aa, h0], writes=[hh])
                    c.op("dve", lambda e: e.tensor_tensor(out=hh[:, 16:32], in0=hh[:, 16:32], in1=ti[:, 16:32], op=ALU.add), reads=[hh, ti], writes=[hh])
                    c.op("pool", lambda e, n=n: e.tensor_copy(out=hs[:, n, :], in_=hh[:, 16:32]), reads=[hh], writes=[hs])
                bkC = k.psum.next()
                proj(c, k, bkC[0:96, 0:w], bkC, wg1c[n // 4], lambda kt, n=n: wg1[:, kt, 96 * n:96 * (n + 1)], xn, w)
                c.op("act", lambda e: e.activation(out=gg[:, 0:w], in_=bkC[0:96, 0:w], func=GELU), reads=[bkC], writes=[gg])
                c.op("dve", lambda e, n=n: e.tensor_tensor(out=ybuf[:, n, 0:w], in0=hh[:, 0:w], in1=gg[:, 0:w], op=ALU.mult), reads=[hh, gg], writes=[ybuf])
            for m in range(8):
                ps = k.psum.next()
                for n in range(16):
                    c.op("pe", lambda e, n=n, m=m: e.matmul(ps[:, 0:w], wo1[:, n, m * 128:(m + 1) * 128], ybuf[:, n, 0:w], start=(n == 0), stop=(n == 15)), reads=[wo1, ybuf], writes=[ps], accum=(n > 0))
                add_to_x(c, k, m, c0, w, ps)
        ps = k.psum.next()
        c.op("pe", lambda e: e.transpose(out=ps[0:16, 0:96], in_=hprev[:, :], identity=k.ident[0:96, 0:96]), reads=[hprevT, k.ident], writes=[ps])
        ho = c.sb([16, 96], F32, "ho", st)
        evac(c, k, ho, ho[:, :], ps, ps[0:16, 0:96])
        c.dma("sp", O["rglru_prompt"].rearrange("(n c) -> n c", c=96), ho[:, :], reads=[ho])
        stg = store_T(c, k, hs, lambda t: hs[:, t, :], 16, 96, 16, None, st, "hsst", stg=stgA)
        c.dma("sp", O["rglru_sample"], stg[0:16, 0:1536], reads=[stg])
        stg2 = stgA
        store_T(c, k, xtail, lambda t: xtail[:, t, :], 16, 96, 19, None, st, "xtst", stg=stg2)
        c.dma("sp", O["rglru_conv_prompt"], stg2[0:3, :], reads=[stg2])
        c.dma("sp", O["rglru_conv_sample"][:, 2, :], stg2[3:19, :], reads=[stg2])
        c.dma("sp", O["rglru_conv_sample"][:, 0:2, :], D["cache_rglru_conv"][:, 1:3, :])
        c.flush()
IN_SHAPES = {
    "x_prompt": [2048, 1024], "x_sample": [16, 1024], "state_gla": [16, 4, 64, 128],
    "state_s5_re": [16, 32, 64], "state_s5_im": [16, 32, 64], "state_rglru": [16, 1536],
    "cache_rglru_conv": [16, 3, 1536], "cache_ffn_conv": [2, 16, 2, 2816], "meta_tokens": [16, 1024],
    "norm_mix_0": [1024], "w_in_0": [1024, 2064], "w_alpha_0": [16, 256], "b_alpha_0": [256],
    "gla_norm_0": [4, 128], "s5_lam_re": [32, 64], "s5_lam_im": [32, 64], "s5_log_dt": [32],
    "s5_b_re": [32, 64, 16], "s5_b_im": [32, 64, 16], "s5_c_re": [32, 16, 64], "s5_c_im": [32, 16, 64],
    "s5_d": [32, 16], "s5_w_glu": [512, 512], "s5_b_glu": [512], "w_out_0": [1024, 1024],
    "norm_mix_1": [1024], "w_in_1": [1024, 3072], "rnn_conv_w": [4, 1536], "rnn_conv_b": [1536],
    "rnn_w_a": [16, 96, 96], "rnn_b_a": [16, 96], "rnn_w_x": [16, 96, 96], "rnn_b_x": [16, 96],
    "rnn_lam": [1536], "w_out_1": [1536, 1024], "norm_ffn": [2, 1024], "ffn_w_up": [2, 1024, 5632],
    "ffn_conv_w": [2, 3, 2816], "ffn_conv_b": [2, 2816], "ffn_w_down": [2, 2816, 1024], "norm_final": [1024],
}
OUT_SHAPES = {
    "y_prompt": [2048, 1024], "y_sample": [16, 1024], "gla_prompt": [4, 64, 128], "gla_sample": [16, 4, 64, 128],
    "s5_re_prompt": [32, 64], "s5_re_sample": [16, 32, 64], "s5_im_prompt": [32, 64], "s5_im_sample": [16, 32, 64],
    "rglru_prompt": [1536], "rglru_sample": [16, 1536], "rglru_conv_prompt": [3, 1536],
    "rglru_conv_sample": [16, 3, 1536], "ffn_conv_prompt": [2, 2, 2816], "ffn_conv_sample": [2, 16, 2, 2816],
}
SHARDED = {"x_prompt": 0, "x_sample": 0, "state_gla": 0, "state_s5_re": 0, "state_s5_im": 0, "state_rglru": 0,
           "cache_rglru_conv": 0, "cache_ffn_conv": 1}


def build(upto=99, debug=False):
    nc = bass.Bass("TRN2", target_bir_lowering=False)
    D = {n: nc.dram_tensor(n, s, F32, kind="ExternalInput").ap() for n, s in IN_SHAPES.items()}
    O = {n: nc.dram_tensor(n, s, F32, kind="ExternalOutput").ap() for n, s in OUT_SHAPES.items()}
    if debug:
        O["dbg_x"] = nc.dram_tensor("dbg_x", [128, 8, NT], F32, kind="ExternalOutput").ap()
        O["dbg_mixo"] = nc.dram_tensor("dbg_mixo", [128, 4, NT], BF16, kind="ExternalOutput").ap()
        O["dbg_mixy"] = nc.dram_tensor("dbg_mixy", [128, 4, NT], BF16, kind="ExternalOutput").ap()
    with ExitStack() as st:
        c = Ctx(nc, st)
        k = K()
        c.init()
        phase0(c, k, D)
        with ExitStack() as st1:
            k.mixo = c.sb([128, 4, NT], BF16, "mixo", st1)
            if upto >= 1:
                phase1(c, k, D, O)
            k.mixy = c.sb([128, 4, NT], BF16, "mixy", st1)
            if upto >= 2:
                phase2(c, k, D, O)
            if debug:
                c.dma("sp", O["dbg_mixo"], k.mixo[:], reads=[k.mixo])
                c.dma("sp", O["dbg_mixy"], k.mixy[:], reads=[k.mixy])
                c.flush()
            if upto >= 3:
                phase3(c, k, D, O)
        if upto >= 4:
            phase_ffn(c, k, D, O, 0)
        if upto >= 5:
            phase5(c, k, D, O)
        if upto >= 6:
            phase_ffn(c, k, D, O, 1)
        if upto >= 7:
            phase7(c, k, D, O)
        if debug:
            for kt in range(8):
                c.dma("sp", O["dbg_x"][:, kt, :], k.X[kt][:, :], reads=[k.X[kt].ts(0, NT)])
        c.flush(final=True)
    return nc


def make_in_maps(inputs, n=8):
    maps = []
    for i in range(n):
        m = {}
        for name in IN_SHAPES:
            a = np.asarray(inputs[name], dtype=np.float32)
            if name in SHARDED:
                ax = SHARDED[name]
                if name == "x_prompt":
                    a = a[i]
                elif ax == 0:
                    a = a[16 * i:16 * (i + 1)]
                else:
                    a = a[:, 16 * i:16 * (i + 1)]
                if name == "x_sample":
                    a = a.reshape(16, 1024)
            m[name] = np.ascontiguousarray(a)
        maps.append(m)
    return maps


def kernel(**inputs):
    nc = build()
    res = run_bass_kernel_spmd(nc, make_in_maps(inputs), core_ids=list(range(8)))
    R = res.results
    cat = lambda n: np.concatenate([np.asarray(r[n]) for r in R], axis=0)
    stk = lambda n: np.stack([np.asarray(r[n]) for r in R], axis=0)
    y_prompt = stk("y_prompt")
    y_sample = cat("y_sample").reshape(128, 1, 1024)
    out = (y_prompt, y_sample, stk("gla_prompt"), cat("gla_sample"), stk("s5_re_prompt"), cat("s5_re_sample"),
           stk("s5_im_prompt"), cat("s5_im_sample"), stk("rglru_prompt"), cat("rglru_sample"),
           stk("rglru_conv_prompt"), cat("rglru_conv_sample"),
           np.stack([np.asarray(r["ffn_conv_prompt"]) for r in R], axis=1),
           np.concatenate([np.asarray(r["ffn_conv_sample"]) for r in R], axis=1))
    return tuple(np.ascontiguousarray(o.astype(np.float32)) for o in out)
```
